# Optimizing a Trainium2 kernel written in Bass

```python
import jax, jax.numpy as jnp
from jax import lax
import numpy as np

D_MODEL = 1024
BATCH = 8
SEQ = 2048
DEPTH = 1

N_META = 16
D_FF = 2816
RET_HEADS = 4
RET_DK = 256
RET_DV = 512
RET_CHUNK = 128
ATT_HEADS = 8
ATT_DH = 128
IDX_HEADS = 8
IDX_DH = 64
TOPK_MAX = 256
DSA_QBLOCK = 32
ROPE_THETA = 10000.0
EPS = 1e-6
RET_QK = RET_HEADS * RET_DK
RET_V = RET_HEADS * RET_DV
ATT_W = ATT_HEADS * ATT_DH
IDX_Q = IDX_HEADS * IDX_DH
IN_SPLITS = (RET_QK, RET_QK, RET_V, RET_V, ATT_W, ATT_W, ATT_W, IDX_Q, IDX_DH, IDX_HEADS, D_MODEL, D_MODEL)
D_IN = sum(IN_SPLITS)

kernel_name = 'hybrid_retention_dsa_macaron'


def rmsnorm(x, g):
    xf = x.astype(jnp.float32)
    y = xf * lax.rsqrt(jnp.mean(xf * xf, axis=-1, keepdims=True) + EPS)
    return (y * g.astype(jnp.float32)).astype(x.dtype)


def rope(x, pos):
    d = x.shape[-1]
    inv_freq = ROPE_THETA ** (-jnp.arange(0, d, 2, dtype=jnp.float32) / d)
    ang = pos.astype(jnp.float32)[:, None] * inv_freq[None, :]
    cos = jnp.cos(ang)[:, None, :]
    sin = jnp.sin(ang)[:, None, :]
    xf = x.astype(jnp.float32)
    x1, x2 = xf[..., : d // 2], xf[..., d // 2:]
    return jnp.concatenate([x1 * cos - x2 * sin, x2 * cos + x1 * sin], axis=-1).astype(x.dtype)


def swiglu(u, w_gate, w_up, w_down):
    return (jax.nn.silu(u @ w_gate) * (u @ w_up)) @ w_down


def retention(q, k, v):
    out_dtype = v.dtype
    q, k, v = (a.astype(jnp.float32) for a in (q, k, v))
    B, H, T, _ = q.shape
    C = RET_CHUNK
    n = (T - N_META) // C
    log_g = jnp.log1p(-(2.0 ** (-5.0 - jnp.arange(H, dtype=jnp.float32))))

    def decay_matrix(c):
        i = jnp.arange(c, dtype=jnp.float32)
        diff = i[:, None] - i[None, :]
        return jnp.where(diff >= 0, jnp.exp(log_g[:, None, None] * jnp.maximum(diff, 0.0)), 0.0)

    qm, km, vm = q[:, :, :N_META], k[:, :, :N_META], v[:, :, :N_META]
    y_meta = jnp.einsum('bhqk,bhke->bhqe', jnp.einsum('bhqd,bhkd->bhqk', qm, km) * decay_matrix(N_META), vm)
    zeta_m = jnp.exp(log_g[:, None] * (N_META - 1.0 - jnp.arange(N_META, dtype=jnp.float32)))
    state0 = jnp.einsum('bhkd,bhke->bhde', km * zeta_m[..., None], vm)

    d_in = decay_matrix(C)
    pos_c = jnp.arange(C, dtype=jnp.float32)
    xi = jnp.exp(log_g[:, None] * (pos_c + 1.0))[..., None]
    zeta = jnp.exp(log_g[:, None] * (C - 1.0 - pos_c))[..., None]
    g_chunk = jnp.exp(log_g * C)[:, None, None]

    def to_chunks(a):
        return a[:, :, N_META:].reshape(B, H, n, C, a.shape[-1]).transpose(2, 0, 1, 3, 4)

    def step(state, qkv):
        qc, kc, vc = qkv
        inner = jnp.einsum('bhqk,bhke->bhqe', jnp.einsum('bhqd,bhkd->bhqk', qc, kc) * d_in, vc)
        cross = jnp.einsum('bhqd,bhde->bhqe', qc, state) * xi
        state = g_chunk * state + jnp.einsum('bhkd,bhke->bhde', kc * zeta, vc)
        return state, inner + cross

    _, y_chunks = lax.scan(step, state0, (to_chunks(q), to_chunks(k), to_chunks(v)))
    y_real = y_chunks.transpose(1, 2, 0, 3, 4).reshape(B, H, n * C, v.shape[-1])
    y = jnp.concatenate([y_meta, y_real], axis=2)
    y = y * lax.rsqrt(jnp.mean(y * y, axis=-1, keepdims=True) + EPS)
    return y.astype(out_dtype)


def dsa_attention(q, k, v, qi, ki, wi, k_top):
    B, T, H, dh = q.shape
    nb = -(-T // DSA_QBLOCK)
    t_pad = nb * DSA_QBLOCK
    pad = t_pad - T

    def blocks(a):
        a = jnp.pad(a, [(0, 0), (0, pad)] + [(0, 0)] * (a.ndim - 2))
        return a.reshape((B, nb, DSA_QBLOCK) + a.shape[2:]).swapaxes(0, 1)

    pos_blocks = jnp.arange(t_pad, dtype=jnp.int32).reshape(nb, DSA_QBLOCK)
    key_pos = jnp.arange(T, dtype=jnp.int32)
    meta_pos = jnp.arange(N_META, dtype=jnp.int32)
    k_meta, v_meta = k[:, :N_META], v[:, :N_META]
    scale = ATT_DH ** -0.5
    idx_scale = (IDX_DH ** -0.5) * (IDX_HEADS ** -0.5)
    gather = jax.vmap(lambda a, i: a[i])

    def one_block(args):
        qb, qib, wib, tq = args
        rel = jax.nn.relu(jnp.einsum('bqhd,bsd->bqhs', qib, ki))
        score = jnp.einsum('bqh,bqhs->bqs', wib, rel).astype(jnp.float32) * idx_scale
        admissible = (key_pos[None, :] >= N_META) & (key_pos[None, :] <= tq[:, None])
        score = jnp.where(admissible[None], score, -jnp.inf)
        _, sel = lax.top_k(score, k_top)
        sel_ok = (sel >= N_META) & (sel <= tq[None, :, None])
        k_sel = gather(k, sel)
        v_sel = gather(v, sel)
        s_meta = jnp.einsum('bqhd,bmhd->bhqm', qb, k_meta).astype(jnp.float32) * scale
        s_meta = jnp.where((meta_pos[None, :] <= tq[:, None])[None, None], s_meta, -jnp.inf)
        s_sel = jnp.einsum('bqhd,bqkhd->bhqk', qb, k_sel).astype(jnp.float32) * scale
        s_sel = jnp.where(sel_ok[:, None], s_sel, -jnp.inf)
        p = jax.nn.softmax(jnp.concatenate([s_meta, s_sel], axis=-1), axis=-1).astype(v.dtype)
        return (jnp.einsum('bhqm,bmhd->bqhd', p[..., :N_META], v_meta)
                + jnp.einsum('bhqk,bqkhd->bqhd', p[..., N_META:], v_sel))

    out = lax.map(one_block, (blocks(q), blocks(qi), blocks(wi), pos_blocks))
    return out.swapaxes(0, 1).reshape(B, t_pad, H, dh)[:, :T]


def hybrid_mixer(u, pos, k_top, w_in, w_ret_out, w_att_out, w_mix_out):
    B, T, _ = u.shape
    proj = u @ w_in
    offs = [int(o) for o in np.cumsum(IN_SPLITS)[:-1]]
    rq, rk, rv, rg, aq, ak, av, iq, ik, iw, g_ret, g_att = jnp.split(proj, offs, axis=-1)
    rq = rope(rq.reshape(B, T, RET_HEADS, RET_DK), pos)
    rk = rope(rk.reshape(B, T, RET_HEADS, RET_DK), pos) * (RET_DK ** -0.5)
    rv = rv.reshape(B, T, RET_HEADS, RET_DV)
    yr = retention(rq.transpose(0, 2, 1, 3), rk.transpose(0, 2, 1, 3), rv.transpose(0, 2, 1, 3))
    yr = yr.transpose(0, 2, 1, 3).reshape(B, T, RET_V)
    yr = (jax.nn.silu(rg) * yr) @ w_ret_out
    aq = rope(aq.reshape(B, T, ATT_HEADS, ATT_DH), pos)
    ak = rope(ak.reshape(B, T, ATT_HEADS, ATT_DH), pos)
    av = av.reshape(B, T, ATT_HEADS, ATT_DH)
    iq = rope(iq.reshape(B, T, IDX_HEADS, IDX_DH), pos)
    ik = rope(ik.reshape(B, T, 1, IDX_DH), pos)[:, :, 0]
    ya = dsa_attention(aq, ak, av, iq, ik, iw, k_top).reshape(B, T, ATT_W) @ w_att_out
    merged = jax.nn.sigmoid(g_ret) * yr + jax.nn.sigmoid(g_att) * ya
    return merged @ w_mix_out


def setup_inputs(seed: int = 0) -> dict:
    key = jax.random.key(seed)
    ks = jax.random.split(key, 16)
    f32 = jnp.float32

    def dense(k, fan_in, fan_out):
        return jax.random.normal(k, (DEPTH, fan_in, fan_out), f32) * fan_in ** -0.5

    def gain(k, shape):
        return 1.0 + 0.02 * jax.random.normal(k, shape, f32)

    return {
        'x': jax.random.normal(ks[0], (BATCH, SEQ, D_MODEL), f32),
        'meta_tokens': jax.random.normal(ks[1], (N_META, D_MODEL), f32),
        'ffn1_norm': gain(ks[2], (DEPTH, D_MODEL)),
        'ffn1_w_gate': dense(ks[3], D_MODEL, D_FF),
        'ffn1_w_up': dense(ks[4], D_MODEL, D_FF),
        'ffn1_w_down': dense(ks[5], D_FF, D_MODEL),
        'mix_norm': gain(ks[6], (DEPTH, D_MODEL)),
        'w_in': dense(ks[7], D_MODEL, D_IN),
        'w_ret_out': dense(ks[8], RET_V, D_MODEL),
        'w_att_out': dense(ks[9], ATT_W, D_MODEL),
        'w_mix_out': dense(ks[10], D_MODEL, D_MODEL),
        'ffn2_norm': gain(ks[11], (DEPTH, D_MODEL)),
        'ffn2_w_gate': dense(ks[12], D_MODEL, D_FF),
        'ffn2_w_up': dense(ks[13], D_MODEL, D_FF),
        'ffn2_w_down': dense(ks[14], D_FF, D_MODEL),
        'final_norm': gain(ks[15], (D_MODEL,)),
    }


def reference(x, meta_tokens, ffn1_norm, ffn1_w_gate, ffn1_w_up, ffn1_w_down, mix_norm, w_in, w_ret_out, w_att_out, w_mix_out, ffn2_norm, ffn2_w_gate, ffn2_w_up, ffn2_w_down, final_norm):
    B, L, D = x.shape
    k_top = min(TOPK_MAX, L // 4)
    meta = jnp.broadcast_to(meta_tokens.astype(x.dtype)[None], (B, N_META, D))
    h = jnp.concatenate([meta, x], axis=1)
    pos = jnp.arange(h.shape[1], dtype=jnp.int32)
    for l in range(DEPTH):
        h = h + 0.5 * swiglu(rmsnorm(h, ffn1_norm[l]), ffn1_w_gate[l], ffn1_w_up[l], ffn1_w_down[l])
        h = h + hybrid_mixer(rmsnorm(h, mix_norm[l]), pos, k_top, w_in[l], w_ret_out[l], w_att_out[l], w_mix_out[l])
        h = h + 0.5 * swiglu(rmsnorm(h, ffn2_norm[l]), ffn2_w_gate[l], ffn2_w_up[l], ffn2_w_down[l])
    y = rmsnorm(h, final_norm)
    return y[:, N_META:]
```

```python
import contextlib
import numpy as np
import concourse.bass as bass
import concourse.mybir as mybir
from concourse.bass_utils import run_bass_kernel_spmd

F32 = mybir.dt.float32
BF16 = mybir.dt.bfloat16
AF = mybir.ActivationFunctionType
ALU = mybir.AluOpType
AX = mybir.AxisListType

D = 1024
DFF = 2816
NM = 16
SEQ = 2048
T = NM + SEQ
DIN = 11848
TB = 256
NT = 2
NBLK = SEQ // TB
EPS = 1e-6
KTOP = 256
NBIS = 22
GAM = [1.0 - 2.0 ** (-5.0 - h) for h in range(4)]
RQ, RK, RV, RG, AQ, AK, AV, IQ, IK, IW, GR, GA = 0, 1024, 2048, 4096, 6144, 7168, 8192, 9216, 9728, 9792, 9800, 10824
ATT_SCALE = 128.0 ** -0.5
NEGBIG = -1.0e30

DEBUG = {}


class Eng:
    def __init__(self, K, name, raw):
        self.K, self.name, self.raw = K, name, raw
        self.sem = None
        self.cnt = 0
        self.waited = {}

    def rotate(self):
        if self.sem is None or self.cnt >= 3000:
            self.sem = self.K.new_sem(self.name)
            self.cnt = 0

    def wait(self, sem, val):
        if self.waited.get(id(sem), 0) >= val:
            return
        if sem is self.sem and self.name == "pe":
            return
        self.raw.wait_ge(sem, val)
        self.waited[id(sem)] = val


class Ctx:
    def __init__(self, nc, es):
        self.nc, self.es = nc, es
        self.nsem = 0
        self.keys = {}
        self.base = {}
        self.E = {}
        for n, raw in (("pe", nc.tensor), ("act", nc.scalar), ("dve", nc.vector), ("pool", nc.gpsimd), ("sp", nc.sync)):
            self.E[n] = Eng(self, n, raw)
        self.dsem = {}
        self.ps_free_list = []
        self.ntmp = 0

    def new_sem(self, name):
        self.nsem += 1
        return self.es.enter_context(self.nc.semaphore("s%d_%s" % (self.nsem, name)))

    def sb(self, name, shape, dtype):
        return self.es.enter_context(self.nc.sbuf_tensor(name, shape, dtype))

    def _get(self, k):
        st = self.keys.get(k)
        if st is None:
            b = self.base.get(k[0])
            st = [None, dict(b) if b else {}]
            self.keys[k] = st
        return st

    def retire(self, slot):
        agg = dict(self.base.get(slot, {}))

        def add(t):
            if t is None:
                return
            cur = agg.get(id(t[0]))
            if cur is None or cur[1] < t[1]:
                agg[id(t[0])] = t

        for k in [k for k in self.keys if k[0] == slot]:
            st = self.keys.pop(k)
            add(st[0])
            for t in st[1].values():
                add(t)
        self.base[slot] = agg

    def _deps(self, eng, reads, writes):
        need = {}

        def add(t):
            if t is None:
                return
            cur = need.get(id(t[0]))
            if cur is None or cur[1] < t[1]:
                need[id(t[0])] = t

        for k in reads:
            add(self._get(k)[0])
        for k in writes:
            st = self._get(k)
            add(st[0])
            for t in st[1].values():
                add(t)
        for sem, val in need.values():
            eng.wait(sem, val)

    def _commit(self, t, reads, writes):
        for k in reads:
            if k in writes:
                continue
            st = self._get(k)
            cur = st[1].get(id(t[0]))
            if cur is None or cur[1] < t[1]:
                st[1][id(t[0])] = t
        for k in writes:
            st = self._get(k)
            st[0] = t
            st[1] = {}

    def op(self, en, fn, reads=(), writes=()):
        eng = self.E[en]
        eng.rotate()
        self._deps(eng, reads, writes)
        ins = fn(eng.raw)
        ins.then_inc(eng.sem, 1)
        eng.cnt += 1
        t = (eng.sem, eng.cnt)
        self._commit(t, reads, writes)
        return t

    def pe(self, fns, reads=(), writes=()):
        eng = self.E["pe"]
        eng.rotate()
        self._deps(eng, reads, writes)
        ins = None
        for fn in fns:
            ins = fn(eng.raw)
        ins.then_inc(eng.sem, 1)
        eng.cnt += 1
        t = (eng.sem, eng.cnt)
        self._commit(t, reads, writes)
        return t

    def dma(self, q, out, in_, reads=(), writes=(), slot=None, serial=True):
        eng = self.E[q]
        self._deps(eng, reads, writes)
        s = self.dsem.get(slot)
        if s is None:
            s = [self.new_sem("d"), 0]
            self.dsem[slot] = s
        if serial and s[1] > 0:
            eng.wait(s[0], s[1])
        ins = eng.raw.dma_start(out=out, in_=in_)
        ins.then_inc(s[0], 16)
        s[1] += 16
        t = (s[0], s[1])
        self._commit(t, reads, writes)
        return t

    def ps_alloc(self):
        assert self.ps_free_list, "out of PSUM banks"
        return self.ps_free_list.pop(0)

    def ps_free(self, b):
        self.ps_free_list.append(b)


def _consts():
    pos = np.arange(T, dtype=np.float32)

    def tab(d):
        inv = (np.float32(10000.0) ** (-np.arange(0, d, 2, dtype=np.float32) / np.float32(d))).astype(np.float32)
        ang = (pos[:, None] * inv[None, :]).astype(np.float32)
        return np.cos(ang).astype(np.float32), np.sin(ang).astype(np.float32)

    c256, s256 = tab(256)
    c128, s128 = tab(128)
    c64, s64 = tab(64)
    rope = np.concatenate([c256, s256, c128, s128, c64, s64], axis=1).astype(np.float32)
    i = np.arange(128)
    dt = np.zeros((128, 4, 128), np.float64)
    xi = np.zeros((128, 4, 128), np.float64)
    zeta = np.zeros((128, 8), np.float64)
    for h in range(4):
        lg = np.log1p(-(2.0 ** (-5.0 - h)))
        diff = i[None, :] - i[:, None]
        dt[:, h, :] = np.where(diff >= 0, np.exp(lg * np.maximum(diff, 0)), 0.0) / 16.0
        xi[:, h, :] = np.exp(lg * (i[None, :] + 1.0))
        zeta[:, h] = np.exp(lg * (127.0 - i)) / 16.0
        zeta[:NM, 4 + h] = np.exp(lg * (NM - 1.0 - np.arange(NM))) / 16.0
    neg = np.where(i[None, :] <= i[:, None], 0.0, NEGBIG)
    return {
        "c_rope": rope,
        "c_dt": dt.astype(np.float32).reshape(128, 512),
        "c_xi": xi.astype(np.float32).reshape(128, 512),
        "c_zeta": zeta.astype(np.float32),
        "c_neg": neg.astype(np.float32),
        "c_ident": np.eye(128, dtype=np.float32),
        "c_ones": np.ones((128, 128), np.float32),
    }


WSPECS = [
    ("w1g", D, DFF), ("w1u", D, DFF), ("w1d", DFF, D), ("win", D, DIN), ("wro", 2048, D),
    ("wao", D, D), ("wmo", D, D), ("w2g", D, DFF), ("w2u", D, DFF), ("w2d", DFF, D),
]


def build_program(nblk=NBLK, stages=99):
    nc = bass.Bass("TRN2", target_bir_lowering=False)
    dram = {}
    dram["x"] = nc.dram_tensor("x", [SEQ, D], F32, kind="ExternalInput").ap()
    dram["meta"] = nc.dram_tensor("meta", [NM, D], F32, kind="ExternalInput").ap()
    dram["gains"] = nc.dram_tensor("gains", [4 * 128, D], F32, kind="ExternalInput").ap()
    for n, r, c in WSPECS:
        dram[n] = nc.dram_tensor(n, [r, c], F32, kind="ExternalInput").ap()
        dram[n + "b"] = nc.dram_tensor(n + "b", [r, c], BF16, kind="Internal").ap()
    cshapes = {"c_rope": [T, 448], "c_dt": [128, 512], "c_xi": [128, 512], "c_zeta": [128, 8],
               "c_neg": [128, 128], "c_ident": [128, 128], "c_ones": [128, 128]}
    for n, s in cshapes.items():
        dram[n] = nc.dram_tensor(n, s, F32, kind="ExternalInput").ap()
    dram["y"] = nc.dram_tensor("y", [SEQ, D], F32, kind="ExternalOutput").ap()
    for n, s in DEBUG.items():
        dram[n] = nc.dram_tensor(n, s, F32, kind="ExternalOutput").ap()

    with contextlib.ExitStack() as es:
        K = Ctx(nc, es)
        _emit(K, dram, nblk, stages)
    return nc


def _emit(K, dram, nblk, stages):
    nc = K.nc
    sb = K.sb
    akT = sb("akT", [128, 8, T], BF16)
    av = sb("av", [128, 17, D], BF16)
    kiT = sb("kiT", [128, SEQ], BF16)
    S = sb("S", [128, 8, 512], F32)
    Sb = sb("Sb", [128, 8, 512], BF16)
    ident = sb("ident", [128, 128], BF16)
    ones = sb("ones", [128, 128], BF16)
    DT = sb("DT", [128, 4, 128], F32)
    XI = sb("XI", [128, 4, 128], F32)
    zeta = sb("zeta", [128, 8], F32)
    NEG = sb("NEG", [128, 128], F32)
    gbc = sb("gbc", [128, D], F32)
    h = sb("h", [128, NT, D], F32)
    uT = sb("uT", [128, 8, TB], BF16)
    utm = [sb("utm%d" % i, [128, D], BF16) for i in range(2)]
    arena = sb("arena", [128, 13312], BF16)
    merged = sb("merged", [128, NT, D], BF16)
    sg = sb("sg", [128, NT, D], BF16)
    tab = sb("tab", [128, NT, 448], F32)
    iw = sb("iw", [128, NT, 8], F32)
    wbuf = [sb("wbuf%d" % i, [128, 8, 512], BF16) for i in range(2)]
    wd = [sb("wd%d" % i, [128, D], BF16) for i in range(3)]
    xs = [sb("xs%d" % i, [128, 512], F32) for i in range(2)]
    rt = [sb("rt%d" % i, [128, 256], F32) for i in range(4)]
    rp = [sb("rp%d" % i, [128, 512], BF16) for i in range(2)]
    sgt = [sb("sgt%d" % i, [128, TB], BF16) for i in range(2)]
    rl = [sb("rl%d" % i, [128, 512], F32) for i in range(2)]
    pT = [sb("pT%d" % i, [128, TB], BF16) for i in range(3)]
    rs = [sb("rs%d" % i, [128, TB], F32) for i in range(2)]
    junk = sb("junk", [128, D], BF16)
    mk = sb("mk", [128, SEQ], BF16)
    ATb = [sb("ATb%d" % i, [128, 128], BF16) for i in range(2)]
    rqx = [sb("rqx%d" % i, [128, 2, 128], BF16) for i in range(2)]
    stat = sb("stat", [128, 64], F32)
    epsb = sb("epsb", [128, 1], F32)
    psb = [K.es.enter_context(nc.psum_tensor("psb%d" % i, [128, 512], F32)) for i in range(8)]
    K.ps_free_list = list(range(8))

    def PS(b):
        return ("ps", b)

    rr = {}

    def nxt(name, n):
        v = rr.get(name, 0)
        rr[name] = (v + 1) % n
        return v

    A_actT = arena[:, 0:22 * TB].rearrange("p (a b) -> p a b", a=22)
    A_rqT = arena[:, 0:1024].rearrange("p (a b) -> p a b", a=4)
    A_rkT = arena[:, 1024:2048].rearrange("p (a b) -> p a b", a=4)
    A_kz = arena[:, 2048:3072].rearrange("p (a b) -> p a b", a=NT)
    A_rv = arena[:, 3072:5120].rearrange("p (a b) -> p a b", a=NT)
    A_rg = arena[:, 5120:7168].rearrange("p (a b) -> p a b", a=NT)
    A_ygT = arena[:, 7168:11264].rearrange("p (a b) -> p a b", a=16)
    A_aqT = arena[:, 0:2048].rearrange("p (a b) -> p a b", a=8)
    A_iqT = arena[:, 2048:3072].rearrange("p (a b) -> p a b", a=4)
    A_attnT = arena[:, 3072:5120].rearrange("p (a b) -> p a b", a=8)
    A_sc = arena[:, 5120:9216].bitcast(F32)
    A_maskT = arena[:, 9216:13312].rearrange("p (a b) -> p a b", a=16)

    K.op("dve", lambda e: e.memset(epsb[:, :], EPS), reads=[], writes=[("epsb",)])
    def cload(dst, src, key, cast):
        K.dma("pool" if cast else "sp", dst, src, writes=[key], slot=key)

    cload(ident[:, :], dram["c_ident"], ("ident",), True)
    cload(ones[:, :], dram["c_ones"], ("ones",), True)
    cload(DT[:, :, :].rearrange("p a b -> p (a b)"), dram["c_dt"], ("DT",), False)
    cload(XI[:, :, :].rearrange("p a b -> p (a b)"), dram["c_xi"], ("XI",), False)
    cload(zeta[:, :], dram["c_zeta"], ("zeta",), False)
    cload(NEG[:, :], dram["c_neg"], ("NEG",), False)

    for n, r, c in WSPECS:
        for rc in range(r // 128):
            K.dma("pool", dram[n + "b"][rc * 128:(rc + 1) * 128, :], dram[n][rc * 128:(rc + 1) * 128, :],
                  writes=[], slot=("wb", n), serial=False)
        s = K.dsem[("wb", n)]
        K._commit((s[0], s[1]), [], [("wb", n)])

    def wload(name, c0, w):
        i = nxt("wbuf", 2)
        src = dram[name + "b"].rearrange("(k p) c -> p k c", p=128)[:, :, c0:c0 + w]
        K.dma("sp", wbuf[i][:, :, :w], src, reads=[("wb", name)], writes=[("wbuf", i)], slot=("wbuf", i))
        return i

    def wdload(name, r0):
        i = nxt("wd", 3)
        K.dma("sp", wd[i][:, :], dram[name + "b"][r0:r0 + 128, :], reads=[("wb", name)], writes=[("wd", i)],
              slot=("wd", i))
        return i

    def transposes(srcs, src_reads, P, dst, dst_writes, eng="act"):
        n = len(srcs)
        b = K.ps_alloc()
        pv = psb[b][:, :].bitcast(BF16).rearrange("p (a b) -> p a b", a=8)
        fns = []
        for k, s_ap in enumerate(srcs):
            fns.append(lambda pe, k=k, s_ap=s_ap: pe.transpose(pv[:, k, :P], s_ap, ident[:P, :P]))
        K.pe(fns, reads=list(src_reads) + [("ident",)], writes=[PS(b)])
        if eng == "act":
            K.op("act", lambda e: e.activation(out=dst, in_=pv[:, :n, :P], func=AF.Copy), reads=[PS(b)], writes=dst_writes)
        else:
            K.op("dve", lambda e: e.tensor_copy(dst, pv[:, :n, :P]), reads=[PS(b)], writes=dst_writes)
        K.ps_free(b)

    def norm_to_uT(P, ntl, grow):
        K.dma("sp", gbc[:, :], dram["gains"][grow * 128:(grow + 1) * 128, :], writes=[("gbc",)], slot=("gbc",))
        for tt in range(ntl):
            hs = h[:P, tt, :]
            c = tt
            K.op("act", lambda e: e.activation(out=junk[:P, :], in_=hs, func=AF.Square, accum_out=stat[:P, c:c + 1]),
                 reads=[("h", tt)], writes=[("stat", c)])
            K.op("act", lambda e: e.activation(out=stat[:P, c:c + 1], in_=stat[:P, c:c + 1], func=AF.Sqrt, bias=epsb[:P, :], scale=1.0 / D),
                 reads=[("stat", c), ("epsb",)], writes=[("stat", c)])
            K.op("dve", lambda e: e.reciprocal(out=stat[:P, c:c + 1], in_=stat[:P, c:c + 1]),
                 reads=[("stat", c)], writes=[("stat", c)])
            ui = nxt("utm", 2)
            K.op("dve", lambda e: e.scalar_tensor_tensor(out=utm[ui][:P, :], in0=hs, scalar=stat[:P, c:c + 1],
                                                         in1=gbc[:P, :], op0=ALU.mult, op1=ALU.mult),
                 reads=[("h", tt), ("stat", c), ("gbc",)], writes=[("utm", ui)])
            transposes([utm[ui][:P, k * 128:(k + 1) * 128] for k in range(8)], [("utm", ui)], P,
                       uT[:, :, tt * P:(tt + 1) * P], [("uT", tt)], eng="act")

    def down_proj(lhs_fn, lhs_reads_fn, nk, wname, P, ntl, evac):
        banks = [[K.ps_alloc() for dh in range(2)] for tt in range(ntl)]
        for kc in range(nk):
            wi = wdload(wname, kc * 128)
            fns = []
            for tt in range(ntl):
                for dh in range(2):
                    fns.append(lambda pe, tt=tt, dh=dh: pe.matmul(
                        psb[banks[tt][dh]][:P, :], lhsT=lhs_fn(kc, tt), rhs=wd[wi][:, dh * 512:(dh + 1) * 512],
                        start=(kc == 0), stop=(kc == nk - 1)))
            K.pe(fns, reads=[("wd", wi)] + lhs_reads_fn(kc),
                 writes=[PS(banks[tt][dh]) for tt in range(ntl) for dh in range(2)])
        for tt in range(ntl):
            for dh in range(2):
                evac(tt, dh, banks[tt][dh])
                K.ps_free(banks[tt][dh])

    def ffn(P, ntl, grow, wg, wu, wdn):
        N = P * ntl
        norm_to_uT(P, ntl, grow)
        K.retire("arena")
        c0 = 0
        while c0 < DFF:
            w = min(512, DFF - c0)
            ig = wload(wg, c0, w)
            iu = wload(wu, c0, w)
            for fl in range(w // 128):
                fc = c0 // 128 + fl
                pg = K.ps_alloc()
                pu = K.ps_alloc()
                K.pe([lambda pe, k=k: pe.matmul(psb[pg][:, :N], lhsT=wbuf[ig][:, k, fl * 128:(fl + 1) * 128],
                                                rhs=uT[:, k, :N], start=(k == 0), stop=(k == 7)) for k in range(8)],
                     reads=[("wbuf", ig)] + [("uT", tt) for tt in range(ntl)], writes=[PS(pg)])
                K.pe([lambda pe, k=k: pe.matmul(psb[pu][:, :N], lhsT=wbuf[iu][:, k, fl * 128:(fl + 1) * 128],
                                                rhs=uT[:, k, :N], start=(k == 0), stop=(k == 7)) for k in range(8)],
                     reads=[("wbuf", iu)] + [("uT", tt) for tt in range(ntl)], writes=[PS(pu)])
                si = nxt("sgt", 2)
                K.op("act", lambda e: e.activation(out=sgt[si][:, :N], in_=psb[pg][:, :N], func=AF.Silu),
                     reads=[PS(pg)], writes=[("sgt", si)])
                K.op("dve", lambda e: e.tensor_tensor(out=A_actT[:, fc, :N], in0=sgt[si][:, :N], in1=psb[pu][:, :N],
                                                      op=ALU.mult),
                     reads=[("sgt", si), PS(pu)], writes=[("arena", "actT", fc)])
                K.ps_free(pg)
                K.ps_free(pu)
            c0 += w

        def evac(tt, dh, b):
            K.op("dve", lambda e: e.scalar_tensor_tensor(out=h[:P, tt, dh * 512:(dh + 1) * 512], in0=psb[b][:P, :],
                                                         scalar=0.5, in1=h[:P, tt, dh * 512:(dh + 1) * 512],
                                                         op0=ALU.mult, op1=ALU.add),
                 reads=[PS(b)], writes=[("h", tt)])

        down_proj(lambda kc, tt: A_actT[:, kc, tt * P:(tt + 1) * P], lambda kc: [("arena", "actT", kc)], 22, wdn, P, ntl, evac)

    def proj(P, ntl, c0, w, evac):
        wi = wload("win", c0, w)
        for tt in range(ntl):
            b = K.ps_alloc()
            K.pe([lambda pe, k=k: pe.matmul(psb[b][:P, :w], lhsT=uT[:, k, tt * P:(tt + 1) * P], rhs=wbuf[wi][:, k, :w],
                                            start=(k == 0), stop=(k == 7)) for k in range(8)],
                 reads=[("wbuf", wi), ("uT", tt)], writes=[PS(b)])
            evac(tt, b)
            K.ps_free(b)

    def rope_evac(P, tt, b, w, nh, d, tc0, dst_i):
        hd = d // 2
        xi_ = nxt("xs", 2)
        K.op("act", lambda e: e.activation(out=xs[xi_][:P, :w], in_=psb[b][:P, :w], func=AF.Copy),
             reads=[PS(b)], writes=[("xs", xi_)])
        x = xs[xi_][:P, :w].rearrange("p (h two f) -> p h two f", h=nh, two=2)
        o = rp[dst_i][:P, :w].rearrange("p (h two f) -> p h two f", h=nh, two=2)
        x1, x2 = x[:, :, 0, :], x[:, :, 1, :]
        cosb = tab[:P, tt, tc0:tc0 + hd].unsqueeze(1).to_broadcast([P, nh, hd])
        sinb = tab[:P, tt, tc0 + hd:tc0 + 2 * hd].unsqueeze(1).to_broadcast([P, nh, hd])
        tv = [rt[i][:P, :nh * hd].rearrange("p (h f) -> p h f", h=nh) for i in range(4)]
        rd = [("xs", xi_), ("tab",)]
        K.op("dve", lambda e: e.tensor_tensor(out=tv[0], in0=x1, in1=cosb, op=ALU.mult), reads=rd, writes=[("rt", 0)])
        K.op("dve", lambda e: e.tensor_tensor(out=tv[1], in0=x2, in1=sinb, op=ALU.mult), reads=rd, writes=[("rt", 1)])
        K.op("pool", lambda e: e.tensor_tensor(out=tv[2], in0=x2, in1=cosb, op=ALU.mult), reads=rd, writes=[("rt", 2)])
        K.op("pool", lambda e: e.tensor_tensor(out=tv[3], in0=x1, in1=sinb, op=ALU.mult), reads=rd, writes=[("rt", 3)])
        K.op("dve", lambda e: e.tensor_tensor(out=o[:, :, 0, :], in0=tv[0], in1=tv[1], op=ALU.subtract),
             reads=[("rt", 0), ("rt", 1)], writes=[("rp", dst_i, 0)])
        K.op("pool", lambda e: e.tensor_tensor(out=o[:, :, 1, :], in0=tv[2], in1=tv[3], op=ALU.add),
             reads=[("rt", 2), ("rt", 3)], writes=[("rp", dst_i, 1)])

    def RP(i):
        return [("rp", i, 0), ("rp", i, 1)]

    def mixer(P, ntl, blk):
        is_meta = blk < 0
        pos0 = 0 if is_meta else NM + blk * TB
        K.dma("sp", tab[:P, 0:ntl, :], dram["c_rope"][pos0:pos0 + P * ntl, :].rearrange("(a p) c -> p a c", p=P),
              writes=[("tab",)], slot=("tab",))
        norm_to_uT(P, ntl, 1)
        K.retire("arena")
        for hp in range(2):
            if not is_meta:
                def ev_rq(tt, b):
                    ri = nxt("rp", 2)
                    rope_evac(P, tt, b, 512, 2, 256, 0, ri)
                    transposes([rp[ri][:P, k * 128:(k + 1) * 128] for k in range(4)], RP(ri), P,
                               A_rqT[:, :, tt * P:(tt + 1) * P], [("arena", "rqT", tt)], eng="dve")
                proj(P, ntl, RQ + hp * 512, 512, ev_rq)

            def ev_rk(tt, b):
                ri = nxt("rp", 2)
                rope_evac(P, tt, b, 512, 2, 256, 0, ri)
                if not is_meta:
                    transposes([rp[ri][:P, k * 128:(k + 1) * 128] for k in range(4)], RP(ri), P,
                               A_rkT[:, :, tt * P:(tt + 1) * P], [("arena", "rkT", tt)], eng="dve")
                zc = (4 if is_meta else 0) + 2 * hp
                K.op("pool", lambda e: e.tensor_tensor(
                    out=A_kz[:P, tt, :].rearrange("p (h f) -> p h f", h=2),
                    in0=rp[ri][:P, :].rearrange("p (h f) -> p h f", h=2),
                    in1=zeta[:P, zc:zc + 2].unsqueeze(2).to_broadcast([P, 2, 256]), op=ALU.mult),
                    reads=RP(ri) + [("zeta",)], writes=[("arena", "kz", tt)])
            proj(P, ntl, RK + hp * 512, 512, ev_rk)
            for g in range(2):
                def ev_rv(tt, b, g=g):
                    K.op("act", lambda e: e.activation(out=A_rv[:P, tt, g * 512:(g + 1) * 512], in_=psb[b][:P, :],
                                                       func=AF.Copy), reads=[PS(b)], writes=[("arena", "rv", tt, g)])
                proj(P, ntl, RV + hp * 1024 + g * 512, 512, ev_rv)
            if not is_meta:
                for g in range(2):
                    def ev_rg(tt, b, g=g):
                        K.op("act", lambda e: e.activation(out=A_rg[:P, tt, g * 512:(g + 1) * 512], in_=psb[b][:P, :],
                                                           func=AF.Silu), reads=[PS(b)], writes=[("arena", "rg", tt, g)])
                    proj(P, ntl, RG + hp * 1024 + g * 512, 512, ev_rg)
            for tt in range(ntl):
                for hl in range(2):
                    hh = 2 * hp + hl
                    if not is_meta:
                        cs = slice(tt * P, (tt + 1) * P)
                        b1 = K.ps_alloc()
                        K.pe([lambda pe, dc=dc: pe.matmul(psb[b1][:, :128], lhsT=A_rkT[:, 2 * hl + dc, cs],
                                                          rhs=A_rqT[:, 2 * hl + dc, cs], start=(dc == 0), stop=(dc == 1))
                              for dc in range(2)],
                             reads=[("arena", "rkT", tt), ("arena", "rqT", tt)], writes=[PS(b1)])
                        ai = nxt("ATb", 2)
                        K.op("dve", lambda e: e.tensor_tensor(out=ATb[ai][:, :], in0=psb[b1][:, :128], in1=DT[:, hh, :],
                                                              op=ALU.mult),
                             reads=[PS(b1), ("DT",)], writes=[("ATb", ai)])
                        K.ps_free(b1)
                        qi = nxt("rqx", 2)
                        for dc in range(2):
                            K.op("pool", lambda e, dc=dc: e.tensor_tensor(out=rqx[qi][:, dc, :], in0=A_rqT[:, 2 * hl + dc, cs],
                                                                          in1=XI[:, hh, :], op=ALU.mult),
                                 reads=[("arena", "rqT", tt), ("XI",)], writes=[("rqx", qi, dc)])
                        b2 = K.ps_alloc()
                        fns = [lambda pe: pe.matmul(psb[b2][:, :], lhsT=ATb[ai][:, :], rhs=A_rv[:, tt, hl * 512:(hl + 1) * 512],
                                                    start=True, stop=False)]
                        for dc in range(2):
                            fns.append(lambda pe, dc=dc: pe.matmul(psb[b2][:, :], lhsT=rqx[qi][:, dc, :], rhs=Sb[:, 2 * hh + dc, :],
                                                                   start=False, stop=(dc == 1)))
                        K.pe(fns, reads=[("ATb", ai), ("arena", "rv", tt, hl), ("rqx", qi, 0), ("rqx", qi, 1), ("Sb", hh)],
                             writes=[PS(b2)])
                        c = 8 + nxt("gss", 8)
                        K.op("act", lambda e: e.activation(out=junk[:, :512], in_=psb[b2][:, :], func=AF.Square,
                                                           accum_out=stat[:, c:c + 1]),
                             reads=[PS(b2)], writes=[("stat", c)])
                        K.op("act", lambda e: e.activation(out=stat[:, c:c + 1], in_=stat[:, c:c + 1], func=AF.Sqrt, bias=epsb[:, :], scale=1.0 / 512),
                             reads=[("stat", c), ("epsb",)], writes=[("stat", c)])
                        K.op("dve", lambda e: e.reciprocal(out=stat[:, c:c + 1], in_=stat[:, c:c + 1]),
                             reads=[("stat", c)], writes=[("stat", c)])
                        K.op("dve", lambda e: e.scalar_tensor_tensor(
                            out=A_rg[:, tt, hl * 512:(hl + 1) * 512], in0=psb[b2][:, :], scalar=stat[:, c:c + 1],
                            in1=A_rg[:, tt, hl * 512:(hl + 1) * 512], op0=ALU.mult, op1=ALU.mult),
                            reads=[PS(b2), ("stat", c)], writes=[("arena", "rg", tt, hl)])
                        K.ps_free(b2)
                    for dc in range(2):
                        b3 = K.ps_alloc()
                        K.pe([lambda pe: pe.matmul(psb[b3][:, :], lhsT=A_kz[:P, tt, hl * 256 + dc * 128:hl * 256 + (dc + 1) * 128],
                                                   rhs=A_rv[:P, tt, hl * 512:(hl + 1) * 512], start=True, stop=True)],
                             reads=[("arena", "kz", tt), ("arena", "rv", tt, hl)], writes=[PS(b3)])
                        if is_meta:
                            K.op("dve", lambda e: e.tensor_copy(S[:, 2 * hh + dc, :], psb[b3][:, :]),
                                 reads=[PS(b3)], writes=[("S", hh, dc)])
                        else:
                            K.op("dve", lambda e: e.scalar_tensor_tensor(
                                out=S[:, 2 * hh + dc, :], in0=S[:, 2 * hh + dc, :], scalar=float(GAM[hh] ** 128),
                                in1=psb[b3][:, :], op0=ALU.mult, op1=ALU.add),
                                reads=[PS(b3)], writes=[("S", hh, dc)])
                        K.ps_free(b3)
                        K.op("pool", lambda e: e.tensor_copy(Sb[:, 2 * hh + dc, :], S[:, 2 * hh + dc, :]),
                             reads=[("S", hh, dc)], writes=[("Sb", hh)])
                if not is_meta:
                    transposes([A_rg[:P, tt, k * 128:(k + 1) * 128] for k in range(8)],
                               [("arena", "rg", tt, 0), ("arena", "rg", tt, 1)], P,
                               A_ygT[:, hp * 8:(hp + 1) * 8, tt * P:(tt + 1) * P], [("arena", "ygT", hp, tt)], eng="act")
        if not is_meta:
            for g in range(2):
                def ev_gr(tt, b, g=g):
                    K.op("act", lambda e: e.activation(out=sg[:P, tt, g * 512:(g + 1) * 512], in_=psb[b][:P, :],
                                                       func=AF.Sigmoid), reads=[PS(b)], writes=[("sg", tt, g)])
                proj(P, ntl, GR + g * 512, 512, ev_gr)

            def ev_ro(tt, dh, b):
                K.op("dve", lambda e: e.tensor_tensor(out=merged[:P, tt, dh * 512:(dh + 1) * 512], in0=psb[b][:P, :],
                                                      in1=sg[:P, tt, dh * 512:(dh + 1) * 512], op=ALU.mult),
                     reads=[PS(b), ("sg", tt, dh)], writes=[("mg", tt, dh)])
            down_proj(lambda kc, tt: A_ygT[:, kc, tt * P:(tt + 1) * P],
                      lambda kc: [("arena", "ygT", kc // 8, tt) for tt in range(ntl)], 16, "wro", P, ntl, ev_ro)
        K.retire("arena")
        if not is_meta:
            for g in range(2):
                def ev_aq(tt, b, g=g):
                    ri = nxt("rp", 2)
                    rope_evac(P, tt, b, 512, 4, 128, 256, ri)
                    transposes([rp[ri][:P, k * 128:(k + 1) * 128] for k in range(4)], RP(ri), P,
                               A_aqT[:, g * 4:(g + 1) * 4, tt * P:(tt + 1) * P], [("arena", "aqT", tt, g)], eng="dve")
                proj(P, ntl, AQ + g * 512, 512, ev_aq)
        for g in range(2):
            def ev_ak(tt, b, g=g):
                ri = nxt("rp", 2)
                rope_evac(P, tt, b, 512, 4, 128, 256, ri)
                gt = 0 if is_meta else 1 + blk * NT + tt
                transposes([rp[ri][:P, k * 128:(k + 1) * 128] for k in range(4)], RP(ri), P,
                           akT[:, g * 4:(g + 1) * 4, pos0 + tt * P:pos0 + (tt + 1) * P], [("akT", gt, g)], eng="dve")
            proj(P, ntl, AK + g * 512, 512, ev_ak)
        for g in range(2):
            def ev_av(tt, b, g=g):
                gt = 0 if is_meta else 1 + blk * NT + tt
                K.op("act", lambda e: e.activation(out=av[:P, gt, g * 512:(g + 1) * 512], in_=psb[b][:P, :], func=AF.Copy),
                     reads=[PS(b)], writes=[("av", gt, g)])
            proj(P, ntl, AV + g * 512, 512, ev_av)
        if is_meta:
            return

        def ev_iq(tt, b):
            ri = nxt("rp", 2)
            rope_evac(P, tt, b, 512, 8, 64, 384, ri)
            transposes([rp[ri][:P, k * 128:(k + 1) * 128] for k in range(4)], RP(ri), P,
                       A_iqT[:, :, tt * P:(tt + 1) * P], [("arena", "iqT", tt)], eng="dve")
        proj(P, ntl, IQ, 512, ev_iq)

        def ev_ik(tt, b):
            ri = nxt("rp", 2)
            xi_ = nxt("xs", 2)
            K.op("act", lambda e: e.activation(out=xs[xi_][:P, :72], in_=psb[b][:P, :72], func=AF.Copy),
                 reads=[PS(b)], writes=[("xs", xi_)])
            x1, x2 = xs[xi_][:P, 0:32], xs[xi_][:P, 32:64]
            cosb, sinb = tab[:P, tt, 384:416], tab[:P, tt, 416:448]
            rd = [("xs", xi_), ("tab",)]
            K.op("dve", lambda e: e.tensor_tensor(out=rt[0][:P, :32], in0=x1, in1=cosb, op=ALU.mult), reads=rd, writes=[("rt", 0)])
            K.op("dve", lambda e: e.tensor_tensor(out=rt[1][:P, :32], in0=x2, in1=sinb, op=ALU.mult), reads=rd, writes=[("rt", 1)])
            K.op("dve", lambda e: e.tensor_tensor(out=rt[2][:P, :32], in0=x2, in1=cosb, op=ALU.mult), reads=rd, writes=[("rt", 2)])
            K.op("dve", lambda e: e.tensor_tensor(out=rt[3][:P, :32], in0=x1, in1=sinb, op=ALU.mult), reads=rd, writes=[("rt", 3)])
            for rep in range(2):
                K.op("dve", lambda e, rep=rep: e.tensor_tensor(out=rp[ri][:P, rep * 64:rep * 64 + 32], in0=rt[0][:P, :32],
                                                               in1=rt[1][:P, :32], op=ALU.subtract),
                     reads=[("rt", 0), ("rt", 1)], writes=[("rp", ri, 0)])
                K.op("dve", lambda e, rep=rep: e.tensor_tensor(out=rp[ri][:P, rep * 64 + 32:rep * 64 + 64], in0=rt[2][:P, :32],
                                                               in1=rt[3][:P, :32], op=ALU.add),
                     reads=[("rt", 2), ("rt", 3)], writes=[("rp", ri, 1)])
            K.op("dve", lambda e: e.tensor_copy(iw[:P, tt, :], xs[xi_][:P, 64:72]), reads=[("xs", xi_)], writes=[("iw", tt)])
            gi = blk * NT + tt
            transposes([rp[ri][:P, 0:128]], RP(ri), P, kiT[:, gi * 128:(gi + 1) * 128].unsqueeze(1), [("kiT", gi)], eng="dve")
        proj(P, ntl, IK, 72, ev_ik)

        Lblk = 128 * (blk * NT + NT)
        for tt in range(ntl):
            gi = blk * NT + tt
            L = 128 * (gi + 1)
            cs = slice(tt * 128, (tt + 1) * 128)
            for sbk in range((L + 511) // 512):
                ncols = min(512, L - sbk * 512)
                kt = [("kiT", t_) for t_ in range(sbk * 4, sbk * 4 + ncols // 128)]
                for hh in range(8):
                    c, r = hh // 2, hh % 2
                    b = K.ps_alloc()
                    K.pe([lambda pe: pe.matmul(psb[b][:, :ncols], lhsT=A_iqT[r * 64:(r + 1) * 64, c, cs],
                                               rhs=kiT[r * 64:(r + 1) * 64, sbk * 512:sbk * 512 + ncols], start=True, stop=True)],
                         reads=[("arena", "iqT", tt)] + kt, writes=[PS(b)])
                    li = nxt("rl", 2)
                    K.op("act", lambda e: e.activation(out=rl[li][:, :ncols], in_=psb[b][:, :ncols], func=AF.Relu),
                         reads=[PS(b)], writes=[("rl", li)])
                    K.ps_free(b)
                    scv = A_sc[:, sbk * 512:sbk * 512 + ncols]
                    if hh == 0:
                        K.op("dve", lambda e: e.tensor_scalar(out=scv, in0=rl[li][:, :ncols], scalar1=iw[:, tt, 0:1], scalar2=None,
                                                              op0=ALU.mult),
                             reads=[("rl", li), ("iw", tt)], writes=[("arena", "sc", sbk)])
                    else:
                        K.op("dve", lambda e: e.scalar_tensor_tensor(out=scv, in0=rl[li][:, :ncols], scalar=iw[:, tt, hh:hh + 1],
                                                                     in1=scv, op0=ALU.mult, op1=ALU.add),
                             reads=[("rl", li), ("iw", tt)], writes=[("arena", "sc", sbk)])
            SCK = [("arena", "sc", s_) for s_ in range(4)]
            if L < Lblk:
                K.op("pool", lambda e: e.memset(A_sc[:, L:Lblk], NEGBIG), reads=[], writes=SCK)
            LO, W0, MID, CNT, TMP = 16 + tt * 8, 17 + tt * 8, 18 + tt * 8, 19 + tt * 8, 20 + tt * 8

            def col(c_):
                return stat[:, c_:c_ + 1]
            if gi >= 2:
                K.op("dve", lambda e: e.tensor_reduce(out=col(LO), in_=A_sc[:, :L], axis=AX.X, op=ALU.min),
                     reads=SCK, writes=[("stat", LO)])
                K.op("dve", lambda e: e.tensor_reduce(out=col(W0), in_=A_sc[:, :L], axis=AX.X, op=ALU.max),
                     reads=SCK, writes=[("stat", W0)])
                K.op("dve", lambda e: e.tensor_tensor(out=col(W0), in0=col(W0), in1=col(LO), op=ALU.subtract),
                     reads=[("stat", W0), ("stat", LO)], writes=[("stat", W0)])
            K.op("dve", lambda e: e.tensor_tensor(out=A_sc[:, L - 128:L], in0=A_sc[:, L - 128:L], in1=NEG[:, :], op=ALU.add),
                 reads=[("NEG",)] + SCK, writes=SCK)
            if gi >= 2:
                for it in range(NBIS):
                    f = 2.0 ** -(it + 1)
                    K.op("dve", lambda e: e.scalar_tensor_tensor(out=col(MID), in0=col(W0), scalar=f, in1=col(LO),
                                                                 op0=ALU.mult, op1=ALU.add),
                         reads=[("stat", W0), ("stat", LO)], writes=[("stat", MID)])
                    K.op("dve", lambda e: e.tensor_scalar(out=mk[:, :L], in0=A_sc[:, :L], scalar1=col(MID), scalar2=0.0,
                                                          op0=ALU.is_ge, op1=ALU.add, accum_out=col(CNT)),
                         reads=SCK + [("stat", MID)], writes=[("mk",), ("stat", CNT)])
                    K.op("dve", lambda e: e.scalar_tensor_tensor(out=col(TMP), in0=col(CNT), scalar=KTOP - 0.5, in1=col(W0),
                                                                 op0=ALU.is_ge, op1=ALU.mult),
                         reads=[("stat", CNT), ("stat", W0)], writes=[("stat", TMP)])
                    K.op("dve", lambda e: e.scalar_tensor_tensor(out=col(LO), in0=col(TMP), scalar=f, in1=col(LO),
                                                                 op0=ALU.mult, op1=ALU.add),
                         reads=[("stat", TMP), ("stat", LO)], writes=[("stat", LO)])
            else:
                K.op("dve", lambda e: e.memset(col(LO), -1.0e29), reads=[], writes=[("stat", LO)])
            K.op("dve", lambda e: e.tensor_scalar(out=mk[:, :Lblk], in0=A_sc[:, :Lblk], scalar1=col(LO), scalar2=None,
                                                  op0=ALU.is_ge),
                 reads=SCK + [("stat", LO)], writes=[("mk",)])
            nsb = Lblk // 128
            for s0 in range(0, nsb, 8):
                n = min(8, nsb - s0)
                transposes([mk[:, (s0 + k) * 128:(s0 + k + 1) * 128] for k in range(n)], [("mk",)], 128,
                           A_maskT[:, s0:s0 + n, cs], [("arena", "maskT", tt)], eng="act")

        nkt = blk * NT + NT
        for hh in range(8):
            g = hh // 4
            bo = K.ps_alloc()
            bs = K.ps_alloc()
            tiles = [(-1, NM, 0)] + [(si, 128, 128 if si == nkt - 1 else 0) for si in range(nkt)]
            for idx, (si, Ps, q0) in enumerate(tiles):
                first, last = idx == 0, idx == len(tiles) - 1
                kc0 = 0 if si < 0 else NM + si * 128
                gt = si + 1
                b = K.ps_alloc()
                K.pe([lambda pe: pe.matmul(psb[b][:Ps, q0:TB], lhsT=akT[:, hh, kc0:kc0 + Ps], rhs=A_aqT[:, hh, q0:TB],
                                           start=True, stop=True)],
                     reads=[("akT", gt, g), ("arena", "aqT", 0, g), ("arena", "aqT", 1, g)], writes=[PS(b)])
                pi = nxt("pT", 3)
                K.op("act", lambda e: e.activation(out=pT[pi][:Ps, q0:TB], in_=psb[b][:Ps, q0:TB], func=AF.Exp, scale=ATT_SCALE),
                     reads=[PS(b)], writes=[("pT", pi)])
                K.ps_free(b)
                if si >= 0:
                    K.op("pool", lambda e: e.tensor_tensor(out=pT[pi][:Ps, q0:TB], in0=pT[pi][:Ps, q0:TB],
                                                           in1=A_maskT[:Ps, si, q0:TB], op=ALU.mult),
                         reads=[("pT", pi), ("arena", "maskT", 0), ("arena", "maskT", 1)], writes=[("pT", pi)])
                K.pe([lambda pe: pe.matmul(psb[bo][:, q0:TB], lhsT=av[:Ps, gt, hh * 128:(hh + 1) * 128], rhs=pT[pi][:Ps, q0:TB],
                                           start=first, stop=last),
                      lambda pe: pe.matmul(psb[bs][:, q0:TB], lhsT=ones[:Ps, :], rhs=pT[pi][:Ps, q0:TB],
                                           start=first, stop=last)],
                     reads=[("av", gt, g), ("pT", pi), ("ones",)], writes=[PS(bo), PS(bs)])
            ri_ = nxt("rs", 2)
            K.op("dve", lambda e: e.reciprocal(out=rs[ri_][:, :], in_=psb[bs][:, :TB]), reads=[PS(bs)], writes=[("rs", ri_)])
            K.op("dve", lambda e: e.tensor_tensor(out=A_attnT[:, hh, :], in0=psb[bo][:, :TB], in1=rs[ri_][:, :], op=ALU.mult),
                 reads=[PS(bo), ("rs", ri_)], writes=[("arena", "attnT", hh)])
            K.ps_free(bo)
            K.ps_free(bs)
        for g in range(2):
            def ev_ga(tt, b, g=g):
                K.op("act", lambda e: e.activation(out=sg[:P, tt, g * 512:(g + 1) * 512], in_=psb[b][:P, :], func=AF.Sigmoid),
                     reads=[PS(b)], writes=[("sg", tt, g)])
            proj(P, ntl, GA + g * 512, 512, ev_ga)

        def ev_ao(tt, dh, b):
            sl = slice(dh * 512, (dh + 1) * 512)
            xi_ = nxt("xs", 2)
            K.op("dve", lambda e: e.tensor_tensor(out=xs[xi_][:P, :], in0=psb[b][:P, :], in1=sg[:P, tt, sl], op=ALU.mult),
                 reads=[PS(b), ("sg", tt, dh)], writes=[("xs", xi_)])
            K.op("pool", lambda e: e.tensor_tensor(out=merged[:P, tt, sl], in0=merged[:P, tt, sl], in1=xs[xi_][:P, :], op=ALU.add),
                 reads=[("xs", xi_)], writes=[("mg", tt, dh)])
        down_proj(lambda kc, tt: A_attnT[:, kc, tt * P:(tt + 1) * P], lambda kc: [("arena", "attnT", kc)], 8, "wao", P, ntl, ev_ao)
        for tt in range(ntl):
            transposes([merged[:P, tt, k * 128:(k + 1) * 128] for k in range(8)], [("mg", tt, 0), ("mg", tt, 1)], P,
                       uT[:, :, tt * P:(tt + 1) * P], [("uT", tt)], eng="act")

        def ev_mo(tt, dh, b):
            sl = slice(dh * 512, (dh + 1) * 512)
            K.op("dve", lambda e: e.tensor_tensor(out=h[:P, tt, sl], in0=psb[b][:P, :], in1=h[:P, tt, sl], op=ALU.add),
                 reads=[PS(b)], writes=[("h", tt)])
        down_proj(lambda kc, tt: uT[:, kc, tt * P:(tt + 1) * P], lambda kc: [("uT", tt) for tt in range(ntl)], 8, "wmo", P, ntl, ev_mo)

    K.dma("sp", h[:NM, 0, :], dram["meta"], writes=[("h", 0)], slot=("x", 0))
    ffn(NM, 1, 0, "w1g", "w1u", "w1d")
    mixer(NM, 1, -1)
    for blk in range(nblk):
        for tt in range(NT):
            r0 = blk * TB + tt * 128
            K.dma("sp", h[:, tt, :], dram["x"][r0:r0 + 128, :], writes=[("h", tt)], slot=("x", tt))
        ffn(128, NT, 0, "w1g", "w1u", "w1d")
        if stages >= 2:
            mixer(128, NT, blk)
        if stages >= 3:
            ffn(128, NT, 2, "w2g", "w2u", "w2d")
        if stages >= 4:
            K.dma("sp", gbc[:, :], dram["gains"][3 * 128:4 * 128, :], writes=[("gbc",)], slot=("gbc",))
        for tt in range(NT):
            r0 = blk * TB + tt * 128
            if stages >= 4:
                c = tt
                hs = h[:, tt, :]
                K.op("act", lambda e: e.activation(out=junk[:, :], in_=hs, func=AF.Square, accum_out=stat[:, c:c + 1]),
                     reads=[("h", tt)], writes=[("stat", c)])
                K.op("act", lambda e: e.activation(out=stat[:, c:c + 1], in_=stat[:, c:c + 1], func=AF.Sqrt, bias=epsb[:, :], scale=1.0 / D),
                     reads=[("stat", c), ("epsb",)], writes=[("stat", c)])
                K.op("dve", lambda e: e.reciprocal(out=stat[:, c:c + 1], in_=stat[:, c:c + 1]),
                     reads=[("stat", c)], writes=[("stat", c)])
                K.op("dve", lambda e: e.scalar_tensor_tensor(out=hs, in0=hs, scalar=stat[:, c:c + 1], in1=gbc[:, :],
                                                             op0=ALU.mult, op1=ALU.mult),
                     reads=[("stat", c), ("gbc",)], writes=[("h", tt)])
            K.dma("sp", dram["y"][r0:r0 + 128, :], h[:, tt, :], reads=[("h", tt)], writes=[], slot=("y", tt))
    sp = K.E["sp"]
    for tt in range(NT):
        s = K.dsem[("y", tt)]
        sp.wait(s[0], s[1])


_CACHE = {}


def _prep_inputs(inputs):
    f = lambda a: np.ascontiguousarray(np.asarray(a, dtype=np.float32))
    shared = {
        "meta": f(inputs["meta_tokens"]),
        "gains": np.ascontiguousarray(np.concatenate([
            np.broadcast_to(f(inputs["ffn1_norm"]).reshape(1, D), (128, D)),
            np.broadcast_to(f(inputs["mix_norm"]).reshape(1, D), (128, D)),
            np.broadcast_to(f(inputs["ffn2_norm"]).reshape(1, D), (128, D)),
            np.broadcast_to(f(inputs["final_norm"]).reshape(1, D), (128, D))], axis=0)),
        "w1g": f(inputs["ffn1_w_gate"][0]), "w1u": f(inputs["ffn1_w_up"][0]), "w1d": f(inputs["ffn1_w_down"][0]),
        "win": f(inputs["w_in"][0]), "wro": f(inputs["w_ret_out"][0]), "wao": f(inputs["w_att_out"][0]),
        "wmo": f(inputs["w_mix_out"][0]),
        "w2g": f(inputs["ffn2_w_gate"][0]), "w2u": f(inputs["ffn2_w_up"][0]), "w2d": f(inputs["ffn2_w_down"][0]),
    }
    shared.update(_consts())
    return shared


def kernel(**inputs):
    x = np.asarray(inputs["x"], dtype=np.float32)
    B = x.shape[0]
    shared = _prep_inputs(inputs)
    if "nc" not in _CACHE:
        _CACHE["nc"] = build_program()
    nc = _CACHE["nc"]
    in_maps = []
    for b in range(B):
        m = dict(shared)
        m["x"] = np.ascontiguousarray(x[b])
        in_maps.append(m)
    res = run_bass_kernel_spmd(nc, in_maps, core_ids=list(range(B)))
    return np.stack([np.asarray(r["y"], dtype=np.float32) for r in res.results], axis=0)
```

```python
import contextlib
import numpy as np
import concourse.bass as bass
import concourse.mybir as mybir
from concourse.bass_utils import run_bass_kernel_spmd

F32 = mybir.dt.float32
BF16 = mybir.dt.bfloat16
AF = mybir.ActivationFunctionType
ALU = mybir.AluOpType
AX = mybir.AxisListType

D = 1024
DFF = 2816
NM = 16
SEQ = 2048
T = NM + SEQ
DIN = 11848
TB = 256
NT = 2
NBLK = SEQ // TB
EPS = 1e-6
KTOP = 256
NBIS = 18
GAM = [1.0 - 2.0 ** (-5.0 - h) for h in range(4)]
RQ, RK, RV, RG, AQ, AK, AV, IQ, IK, IW, GR, GA = 0, 1024, 2048, 4096, 6144, 7168, 8192, 9216, 9728, 9792, 9800, 10824
ATT_SCALE = 128.0 ** -0.5
NEGBIG = -1.0e30

DEBUG = {}


class Eng:
    def __init__(self, K, name, raw):
        self.K, self.name, self.raw = K, name, raw
        self.sem = None
        self.cnt = 0
        self.waited = {}

    def rotate(self):
        if self.sem is None or self.cnt >= 3000:
            self.sem = self.K.new_sem(self.name)
            self.cnt = 0

    def wait(self, sem, val):
        if self.waited.get(id(sem), 0) >= val:
            return
        if sem is self.sem and self.name == "pe":
            return
        self.raw.wait_ge(sem, val)
        self.waited[id(sem)] = val


class Ctx:
    def __init__(self, nc, es):
        self.nc, self.es = nc, es
        self.nsem = 0
        self.keys = {}
        self.base = {}
        self.E = {}
        for n, raw in (("pe", nc.tensor), ("act", nc.scalar), ("dve", nc.vector), ("pool", nc.gpsimd), ("sp", nc.sync)):
            self.E[n] = Eng(self, n, raw)
        self.dsem = {}
        self.ps_free_list = []
        self.ntmp = 0

    def new_sem(self, name):
        self.nsem += 1
        return self.es.enter_context(self.nc.semaphore("s%d_%s" % (self.nsem, name)))

    def sb(self, name, shape, dtype):
        return self.es.enter_context(self.nc.sbuf_tensor(name, shape, dtype))

    def _get(self, k):
        st = self.keys.get(k)
        if st is None:
            b = self.base.get(k[0])
            st = [None, dict(b) if b else {}]
            self.keys[k] = st
        return st

    def retire(self, slot):
        agg = dict(self.base.get(slot, {}))

        def add(t):
            if t is None:
                return
            cur = agg.get(id(t[0]))
            if cur is None or cur[1] < t[1]:
                agg[id(t[0])] = t

        for k in [k for k in self.keys if k[0] == slot]:
            st = self.keys.pop(k)
            add(st[0])
            for t in st[1].values():
                add(t)
        self.base[slot] = agg

    def _deps(self, eng, reads, writes):
        need = {}

        def add(t):
            if t is None:
                return
            cur = need.get(id(t[0]))
            if cur is None or cur[1] < t[1]:
                need[id(t[0])] = t

        for k in reads:
            add(self._get(k)[0])
        for k in writes:
            st = self._get(k)
            add(st[0])
            for t in st[1].values():
                add(t)
        for sem, val in need.values():
            eng.wait(sem, val)

    def _commit(self, t, reads, writes):
        for k in reads:
            if k in writes:
                continue
            st = self._get(k)
            cur = st[1].get(id(t[0]))
            if cur is None or cur[1] < t[1]:
                st[1][id(t[0])] = t
        for k in writes:
            st = self._get(k)
            st[0] = t
            st[1] = {}

    def op(self, en, fn, reads=(), writes=()):
        eng = self.E[en]
        eng.rotate()
        self._deps(eng, reads, writes)
        ins = fn(eng.raw)
        ins.then_inc(eng.sem, 1)
        eng.cnt += 1
        t = (eng.sem, eng.cnt)
        self._commit(t, reads, writes)
        return t

    def pe(self, fns, reads=(), writes=()):
        eng = self.E["pe"]
        eng.rotate()
        self._deps(eng, reads, writes)
        ins = None
        for fn in fns:
            ins = fn(eng.raw)
        ins.then_inc(eng.sem, 1)
        eng.cnt += 1
        t = (eng.sem, eng.cnt)
        self._commit(t, reads, writes)
        return t

    def dma(self, q, out, in_, reads=(), writes=(), slot=None, serial=True):
        eng = self.E[q]
        self._deps(eng, reads, writes)
        s = self.dsem.get(slot)
        if s is None:
            s = [self.new_sem("d"), 0]
            self.dsem[slot] = s
        if serial and s[1] > 0:
            eng.wait(s[0], s[1])
        ins = eng.raw.dma_start(out=out, in_=in_)
        ins.then_inc(s[0], 16)
        s[1] += 16
        t = (s[0], s[1])
        self._commit(t, reads, writes)
        return t

    def ps_alloc(self):
        assert self.ps_free_list, "out of PSUM banks"
        return self.ps_free_list.pop(0)

    def ps_free(self, b):
        self.ps_free_list.append(b)


def _consts():
    pos = np.arange(T, dtype=np.float32)

    def tab(d):
        inv = (np.float32(10000.0) ** (-np.arange(0, d, 2, dtype=np.float32) / np.float32(d))).astype(np.float32)
        ang = (pos[:, None] * inv[None, :]).astype(np.float32)
        return np.cos(ang).astype(np.float32), np.sin(ang).astype(np.float32)

    c256, s256 = tab(256)
    c128, s128 = tab(128)
    c64, s64 = tab(64)
    rope = np.concatenate([c256, s256, c128, s128, c64, s64], axis=1).astype(np.float32)
    i = np.arange(128)
    dt = np.zeros((128, 4, 128), np.float64)
    xi = np.zeros((128, 4, 128), np.float64)
    zeta = np.zeros((128, 8), np.float64)
    for h in range(4):
        lg = np.log1p(-(2.0 ** (-5.0 - h)))
        diff = i[None, :] - i[:, None]
        dt[:, h, :] = np.where(diff >= 0, np.exp(lg * np.maximum(diff, 0)), 0.0) / 16.0
        xi[:, h, :] = np.exp(lg * (i[None, :] + 1.0))
        zeta[:, h] = np.exp(lg * (127.0 - i)) / 16.0
        zeta[:NM, 4 + h] = np.exp(lg * (NM - 1.0 - np.arange(NM))) / 16.0
    neg = np.where(i[None, :] <= i[:, None], 0.0, NEGBIG)
    return {
        "c_rope": rope,
        "c_dt": dt.astype(np.float32).reshape(128, 512),
        "c_xi": xi.astype(np.float32).reshape(128, 512),
        "c_zeta": zeta.astype(np.float32),
        "c_neg": neg.astype(np.float32),
        "c_ident": np.eye(128, dtype=np.float32),
        "c_ones": np.ones((128, 128), np.float32),
    }


WSPECS = [
    ("w1g", D, DFF), ("w1u", D, DFF), ("w1d", DFF, D), ("win", D, DIN), ("wro", 2048, D),
    ("wao", D, D), ("wmo", D, D), ("w2g", D, DFF), ("w2u", D, DFF), ("w2d", DFF, D),
]


WDIMS = {n: (r, c) for n, r, c in WSPECS}
WIN_GROUPS = [(c0, 512) for c0 in range(0, 9728, 512)] + [(IK, 72), (GR, 512), (GR + 512, 512), (GA, 512), (GA + 512, 512)]
WIN_SLAB = {c0: i for i, (c0, w) in enumerate(WIN_GROUPS)}


def _slabs(n):
    r, c = WDIMS[n]
    if n == "win":
        return [("c", c0, w) for c0, w in WIN_GROUPS]
    if c == DFF:
        return [("c", c0, min(512, DFF - c0)) for c0 in range(0, DFF, 512)]
    return [("r", r0, min(512, r - r0)) for r0 in range(0, r, 512)]


def _slab_plan(nblk, stages):
    def ffn_(g, u, d):
        out = []
        for i in range(len(_slabs(g))):
            out += [(g, i), (u, i)]
        out += [(d, i) for i in range(len(_slabs(d)))]
        return out

    def mix_(meta):
        out = []
        for hp in range(2):
            if not meta:
                out.append(("win", WIN_SLAB[RQ + hp * 512]))
            out.append(("win", WIN_SLAB[RK + hp * 512]))
            out += [("win", WIN_SLAB[RV + hp * 1024 + g * 512]) for g in range(2)]
            if not meta:
                out += [("win", WIN_SLAB[RG + hp * 1024 + g * 512]) for g in range(2)]
        if not meta:
            out += [("win", WIN_SLAB[GR + g * 512]) for g in range(2)]
            out += [("wro", i) for i in range(4)]
            out += [("win", WIN_SLAB[AQ + g * 512]) for g in range(2)]
        out += [("win", WIN_SLAB[AK + g * 512]) for g in range(2)]
        out += [("win", WIN_SLAB[AV + g * 512]) for g in range(2)]
        if not meta:
            out += [("win", WIN_SLAB[IQ]), ("win", WIN_SLAB[IK])]
            out += [("win", WIN_SLAB[GA + g * 512]) for g in range(2)]
            out += [("wao", i) for i in range(2)] + [("wmo", i) for i in range(2)]
        return out

    plan = ffn_("w1g", "w1u", "w1d") + mix_(True)
    for b in range(nblk):
        plan += ffn_("w1g", "w1u", "w1d")
        if stages >= 2:
            plan += mix_(False)
        if stages >= 3:
            plan += ffn_("w2g", "w2u", "w2d")
    return plan


def build_program(nblk=NBLK, stages=99):
    nc = bass.Bass("TRN2", target_bir_lowering=False)
    dram = {}
    dram["x"] = nc.dram_tensor("x", [SEQ, D], F32, kind="ExternalInput").ap()
    dram["meta"] = nc.dram_tensor("meta", [NM, D], F32, kind="ExternalInput").ap()
    dram["gains"] = nc.dram_tensor("gains", [4 * 128, D], F32, kind="ExternalInput").ap()
    for n, r, c in WSPECS:
        dram[n] = nc.dram_tensor(n, [r, c], F32, kind="ExternalInput").ap()
        dram[n + "b"] = nc.dram_tensor(n + "b", [len(_slabs(n)), 128, 4096], BF16, kind="Internal").ap()
    cshapes = {"c_rope": [T, 448], "c_dt": [128, 512], "c_xi": [128, 512], "c_zeta": [128, 8],
               "c_neg": [128, 128], "c_ident": [128, 128], "c_ones": [128, 128]}
    for n, s in cshapes.items():
        dram[n] = nc.dram_tensor(n, s, F32, kind="ExternalInput").ap()
    dram["y"] = nc.dram_tensor("y", [SEQ, D], F32, kind="ExternalOutput").ap()
    for n, s in DEBUG.items():
        dram[n] = nc.dram_tensor(n, s, F32, kind="ExternalOutput").ap()

    with contextlib.ExitStack() as es:
        K = Ctx(nc, es)
        _emit(K, dram, nblk, stages)
    return nc


def _emit(K, dram, nblk, stages):
    nc = K.nc
    sb = K.sb
    akT = sb("akT", [128, 8, T], BF16)
    av = sb("av", [128, 17, D], BF16)
    kiT = sb("kiT", [128, SEQ], BF16)
    S = sb("S", [128, 8, 512], F32)
    Sb = sb("Sb", [128, 8, 512], BF16)
    ident = sb("ident", [128, 128], BF16)
    ones = sb("ones", [128, 128], BF16)
    DT = sb("DT", [128, 4, 128], F32)
    XI = sb("XI", [128, 4, 128], F32)
    zeta = sb("zeta", [128, 8], F32)
    NEG = sb("NEG", [128, 128], F32)
    gbc = sb("gbc", [128, D], F32)
    h = sb("h", [128, NT, D], F32)
    uT = sb("uT", [128, 8, TB], BF16)
    utm = [sb("utm%d" % i, [128, D], BF16) for i in range(2)]
    arena = sb("arena", [128, 13312], BF16)
    merged = sb("merged", [128, NT, D], BF16)
    sg = sb("sg", [128, NT, D], BF16)
    tab = sb("tab", [128, NT, 448], F32)
    iw = sb("iw", [128, NT, 8], F32)
    NWS = 3
    wslot = [sb("wslot%d" % i, [128, 4096], BF16) for i in range(NWS)]
    xs = [sb("xs%d" % i, [128, 512], F32) for i in range(2)]
    rt = [sb("rt%d" % i, [128, 256], F32) for i in range(4)]
    rp = [sb("rp%d" % i, [128, 512], BF16) for i in range(2)]
    sgt = [sb("sgt%d" % i, [128, TB], BF16) for i in range(2)]
    rl = [sb("rl%d" % i, [128, 512], F32) for i in range(2)]
    pT = [sb("pT%d" % i, [128, TB], BF16) for i in range(3)]
    rs = [sb("rs%d" % i, [128, TB], F32) for i in range(2)]
    junk = sb("junk", [128, D], BF16)
    mk = sb("mk", [128, SEQ], BF16)
    ATb = [sb("ATb%d" % i, [128, 128], BF16) for i in range(2)]
    rqx = [sb("rqx%d" % i, [128, 2, 128], BF16) for i in range(2)]
    stat = sb("stat", [128, 64], F32)
    epsb = sb("epsb", [128, 1], F32)
    psb = [K.es.enter_context(nc.psum_tensor("psb%d" % i, [128, 512], F32)) for i in range(8)]
    K.ps_free_list = list(range(8))

    def PS(b):
        return ("ps", b)

    rr = {}

    def nxt(name, n):
        v = rr.get(name, 0)
        rr[name] = (v + 1) % n
        return v

    A_actT = arena[:, 0:22 * TB].rearrange("p (a b) -> p a b", a=22)
    A_rqT = arena[:, 0:1024].rearrange("p (a b) -> p a b", a=4)
    A_rkT = arena[:, 1024:2048].rearrange("p (a b) -> p a b", a=4)
    A_kz = arena[:, 2048:3072].rearrange("p (a b) -> p a b", a=NT)
    A_rv = arena[:, 3072:5120].rearrange("p (a b) -> p a b", a=NT)
    A_rg = arena[:, 5120:7168].rearrange("p (a b) -> p a b", a=NT)
    A_ygT = arena[:, 7168:11264].rearrange("p (a b) -> p a b", a=16)
    A_aqT = arena[:, 0:2048].rearrange("p (a b) -> p a b", a=8)
    A_iqT = arena[:, 2048:3072].rearrange("p (a b) -> p a b", a=4)
    A_attnT = arena[:, 3072:5120].rearrange("p (a b) -> p a b", a=8)
    A_sc = arena[:, 5120:9216].bitcast(F32)
    A_maskT = arena[:, 9216:13312].rearrange("p (a b) -> p a b", a=16)

    K.op("dve", lambda e: e.memset(epsb[:, :], EPS), reads=[], writes=[("epsb",)])
    def cload(dst, src, key, cast):
        K.dma("pool" if cast else "sp", dst, src, writes=[key], slot=key)

    cload(ident[:, :], dram["c_ident"], ("ident",), True)
    cload(ones[:, :], dram["c_ones"], ("ones",), True)
    cload(DT[:, :, :].rearrange("p a b -> p (a b)"), dram["c_dt"], ("DT",), False)
    cload(XI[:, :, :].rearrange("p a b -> p (a b)"), dram["c_xi"], ("XI",), False)
    cload(zeta[:, :], dram["c_zeta"], ("zeta",), False)
    cload(NEG[:, :], dram["c_neg"], ("NEG",), False)

    for n, r, c in WSPECS:
        for si, (kind, off, sz) in enumerate(_slabs(n)):
            if kind == "c":
                dst = dram[n + "b"][si, :, 0:8 * sz].rearrange("p (k c) -> p k c", k=8)
                src = dram[n].rearrange("(k p) c -> p k c", p=128)[:, :, off:off + sz]
            else:
                nrc = sz // 128
                dst = dram[n + "b"][si, :, 0:nrc * 1024].rearrange("p (k c) -> p k c", k=nrc)
                src = dram[n][off:off + sz, :].rearrange("(k p) c -> p k c", p=128)
            K.dma("pool", dst, src, writes=[], slot=("wb", n), serial=False)
        sm = K.dsem[("wb", n)]
        K._commit((sm[0], sm[1]), [], [("wb", n)])

    plan = _slab_plan(nblk, stages)
    wst = {"issued": 0, "used": 0}

    def _issue_one():
        i = wst["issued"]
        if i >= len(plan):
            return
        n, si = plan[i]
        kind, off, sz = _slabs(n)[si]
        ncol = 8 * sz if kind == "c" else (sz // 128) * 1024
        sl = i % NWS
        K.dma("sp", wslot[sl][:, :ncol], dram[n + "b"][si, :, 0:ncol], reads=[("wb", n)], writes=[("ws", sl)], slot=("ws", sl))
        wst["issued"] += 1

    def wnext(n, si):
        i = wst["used"]
        assert plan[i] == (n, si), (i, plan[i], n, si)
        while wst["issued"] <= i:
            assert wst["issued"] < wst.get("done", 0) + NWS
            _issue_one()
        wst["used"] += 1
        return i % NWS

    def wdone(k=1):
        wst["done"] = wst.get("done", 0) + k
        while wst["issued"] < min(len(plan), wst["done"] + NWS):
            _issue_one()

    def wload(name, c0, w):
        if name == "win":
            si = WIN_SLAB[c0]
        else:
            si = c0 // 512
        return wnext(name, si)

    def wv(i, w):
        return wslot[i][:, 0:8 * w].rearrange("p (k c) -> p k c", k=8)

    def transposes(srcs, src_reads, P, dst, dst_writes, eng="act"):
        n = len(srcs)
        b = K.ps_alloc()
        pv = psb[b][:, :].bitcast(BF16).rearrange("p (a b) -> p a b", a=8)
        fns = []
        for k, s_ap in enumerate(srcs):
            fns.append(lambda pe, k=k, s_ap=s_ap: pe.transpose(pv[:, k, :P], s_ap, ident[:P, :P]))
        K.pe(fns, reads=list(src_reads) + [("ident",)], writes=[PS(b)])
        if eng == "act":
            K.op("act", lambda e: e.activation(out=dst, in_=pv[:, :n, :P], func=AF.Copy), reads=[PS(b)], writes=dst_writes)
        else:
            K.op("dve", lambda e: e.tensor_copy(dst, pv[:, :n, :P]), reads=[PS(b)], writes=dst_writes)
        K.ps_free(b)

    def norm_to_uT(P, ntl, grow):
        K.dma("sp", gbc[:, :], dram["gains"][grow * 128:(grow + 1) * 128, :], writes=[("gbc",)], slot=("gbc",))
        for tt in range(ntl):
            hs = h[:P, tt, :]
            c = tt
            K.op("act", lambda e: e.activation(out=junk[:P, :], in_=hs, func=AF.Square, accum_out=stat[:P, c:c + 1]),
                 reads=[("h", tt)], writes=[("stat", c)])
            K.op("act", lambda e: e.activation(out=stat[:P, c:c + 1], in_=stat[:P, c:c + 1], func=AF.Sqrt, bias=epsb[:P, :], scale=1.0 / D),
                 reads=[("stat", c), ("epsb",)], writes=[("stat", c)])
            K.op("dve", lambda e: e.reciprocal(out=stat[:P, c:c + 1], in_=stat[:P, c:c + 1]),
                 reads=[("stat", c)], writes=[("stat", c)])
            ui = nxt("utm", 2)
            K.op("dve", lambda e: e.scalar_tensor_tensor(out=utm[ui][:P, :], in0=hs, scalar=stat[:P, c:c + 1],
                                                         in1=gbc[:P, :], op0=ALU.mult, op1=ALU.mult),
                 reads=[("h", tt), ("stat", c), ("gbc",)], writes=[("utm", ui)])
            transposes([utm[ui][:P, k * 128:(k + 1) * 128] for k in range(8)], [("utm", ui)], P,
                       uT[:, :, tt * P:(tt + 1) * P], [("uT", tt)], eng="act")

    def down_proj(lhs_fn, lhs_reads_fn, nk, wname, P, ntl, evac):
        banks = [[K.ps_alloc() for dh in range(2)] for tt in range(ntl)]
        wi = None
        for kc in range(nk):
            if kc % 4 == 0:
                wi = wnext(wname, kc // 4)
            wvv = wslot[wi][:, :].rearrange("p (k c) -> p k c", k=4)
            fns = []
            for tt in range(ntl):
                for dh in range(2):
                    fns.append(lambda pe, tt=tt, dh=dh: pe.matmul(
                        psb[banks[tt][dh]][:P, :], lhsT=lhs_fn(kc, tt), rhs=wvv[:, kc % 4, dh * 512:(dh + 1) * 512],
                        start=(kc == 0), stop=(kc == nk - 1)))
            K.pe(fns, reads=[("ws", wi)] + lhs_reads_fn(kc),
                 writes=[PS(banks[tt][dh]) for tt in range(ntl) for dh in range(2)])
            if kc % 4 == 3 or kc == nk - 1:
                wdone(1)
        for tt in range(ntl):
            for dh in range(2):
                evac(tt, dh, banks[tt][dh])
                K.ps_free(banks[tt][dh])

    def ffn(P, ntl, grow, wg, wu, wdn):
        N = P * ntl
        norm_to_uT(P, ntl, grow)
        K.retire("arena")
        c0 = 0
        while c0 < DFF:
            w = min(512, DFF - c0)
            ig = wload(wg, c0, w)
            iu = wload(wu, c0, w)
            for fl in range(w // 128):
                fc = c0 // 128 + fl
                pg = K.ps_alloc()
                pu = K.ps_alloc()
                K.pe([lambda pe, k=k: pe.matmul(psb[pg][:, :N], lhsT=wv(ig, w)[:, k, fl * 128:(fl + 1) * 128],
                                                rhs=uT[:, k, :N], start=(k == 0), stop=(k == 7)) for k in range(8)],
                     reads=[("ws", ig)] + [("uT", tt) for tt in range(ntl)], writes=[PS(pg)])
                K.pe([lambda pe, k=k: pe.matmul(psb[pu][:, :N], lhsT=wv(iu, w)[:, k, fl * 128:(fl + 1) * 128],
                                                rhs=uT[:, k, :N], start=(k == 0), stop=(k == 7)) for k in range(8)],
                     reads=[("ws", iu)] + [("uT", tt) for tt in range(ntl)], writes=[PS(pu)])
                si = nxt("sgt", 2)
                K.op("act", lambda e: e.activation(out=sgt[si][:, :N], in_=psb[pg][:, :N], func=AF.Silu),
                     reads=[PS(pg)], writes=[("sgt", si)])
                K.op("dve", lambda e: e.tensor_tensor(out=A_actT[:, fc, :N], in0=sgt[si][:, :N], in1=psb[pu][:, :N],
                                                      op=ALU.mult),
                     reads=[("sgt", si), PS(pu)], writes=[("arena", "actT", fc)])
                K.ps_free(pg)
                K.ps_free(pu)
            wdone(2)
            c0 += w

        def evac(tt, dh, b):
            K.op("dve", lambda e: e.scalar_tensor_tensor(out=h[:P, tt, dh * 512:(dh + 1) * 512], in0=psb[b][:P, :],
                                                         scalar=0.5, in1=h[:P, tt, dh * 512:(dh + 1) * 512],
                                                         op0=ALU.mult, op1=ALU.add),
                 reads=[PS(b)], writes=[("h", tt)])

        down_proj(lambda kc, tt: A_actT[:, kc, tt * P:(tt + 1) * P], lambda kc: [("arena", "actT", kc)], 22, wdn, P, ntl, evac)

    def proj(P, ntl, c0, w, evac):
        wi = wload("win", c0, w)
        for tt in range(ntl):
            b = K.ps_alloc()
            K.pe([lambda pe, k=k: pe.matmul(psb[b][:P, :w], lhsT=uT[:, k, tt * P:(tt + 1) * P], rhs=wv(wi, w)[:, k, :w],
                                            start=(k == 0), stop=(k == 7)) for k in range(8)],
                 reads=[("ws", wi), ("uT", tt)], writes=[PS(b)])
            evac(tt, b)
            K.ps_free(b)
        wdone(1)

    def rope_evac(P, tt, b, w, nh, d, tc0, dst_i):
        hd = d // 2
        xi_ = nxt("xs", 2)
        K.op("act", lambda e: e.activation(out=xs[xi_][:P, :w], in_=psb[b][:P, :w], func=AF.Copy),
             reads=[PS(b)], writes=[("xs", xi_)])
        x = xs[xi_][:P, :w].rearrange("p (h two f) -> p h two f", h=nh, two=2)
        o = rp[dst_i][:P, :w].rearrange("p (h two f) -> p h two f", h=nh, two=2)
        x1, x2 = x[:, :, 0, :], x[:, :, 1, :]
        cosb = tab[:P, tt, tc0:tc0 + hd].unsqueeze(1).to_broadcast([P, nh, hd])
        sinb = tab[:P, tt, tc0 + hd:tc0 + 2 * hd].unsqueeze(1).to_broadcast([P, nh, hd])
        tv = [rt[i][:P, :nh * hd].rearrange("p (h f) -> p h f", h=nh) for i in range(4)]
        rd = [("xs", xi_), ("tab",)]
        K.op("dve", lambda e: e.tensor_tensor(out=tv[0], in0=x1, in1=cosb, op=ALU.mult), reads=rd, writes=[("rt", 0)])
        K.op("dve", lambda e: e.tensor_tensor(out=tv[1], in0=x2, in1=sinb, op=ALU.mult), reads=rd, writes=[("rt", 1)])
        K.op("pool", lambda e: e.tensor_tensor(out=tv[2], in0=x2, in1=cosb, op=ALU.mult), reads=rd, writes=[("rt", 2)])
        K.op("pool", lambda e: e.tensor_tensor(out=tv[3], in0=x1, in1=sinb, op=ALU.mult), reads=rd, writes=[("rt", 3)])
        K.op("dve", lambda e: e.tensor_tensor(out=o[:, :, 0, :], in0=tv[0], in1=tv[1], op=ALU.subtract),
             reads=[("rt", 0), ("rt", 1)], writes=[("rp", dst_i, 0)])
        K.op("pool", lambda e: e.tensor_tensor(out=o[:, :, 1, :], in0=tv[2], in1=tv[3], op=ALU.add),
             reads=[("rt", 2), ("rt", 3)], writes=[("rp", dst_i, 1)])

    def RP(i):
        return [("rp", i, 0), ("rp", i, 1)]

    def mixer(P, ntl, blk):
        is_meta = blk < 0
        pos0 = 0 if is_meta else NM + blk * TB
        K.dma("sp", tab[:P, 0:ntl, :], dram["c_rope"][pos0:pos0 + P * ntl, :].rearrange("(a p) c -> p a c", p=P),
              writes=[("tab",)], slot=("tab",))
        norm_to_uT(P, ntl, 1)
        K.retire("arena")
        for hp in range(2):
            if not is_meta:
                def ev_rq(tt, b):
                    ri = nxt("rp", 2)
                    rope_evac(P, tt, b, 512, 2, 256, 0, ri)
                    transposes([rp[ri][:P, k * 128:(k + 1) * 128] for k in range(4)], RP(ri), P,
                               A_rqT[:, :, tt * P:(tt + 1) * P], [("arena", "rqT", tt)], eng="dve")
                proj(P, ntl, RQ + hp * 512, 512, ev_rq)

            def ev_rk(tt, b):
                ri = nxt("rp", 2)
                rope_evac(P, tt, b, 512, 2, 256, 0, ri)
                if not is_meta:
                    transposes([rp[ri][:P, k * 128:(k + 1) * 128] for k in range(4)], RP(ri), P,
                               A_rkT[:, :, tt * P:(tt + 1) * P], [("arena", "rkT", tt)], eng="dve")
                zc = (4 if is_meta else 0) + 2 * hp
                K.op("pool", lambda e: e.tensor_tensor(
                    out=A_kz[:P, tt, :].rearrange("p (h f) -> p h f", h=2),
                    in0=rp[ri][:P, :].rearrange("p (h f) -> p h f", h=2),
                    in1=zeta[:P, zc:zc + 2].unsqueeze(2).to_broadcast([P, 2, 256]), op=ALU.mult),
                    reads=RP(ri) + [("zeta",)], writes=[("arena", "kz", tt)])
            proj(P, ntl, RK + hp * 512, 512, ev_rk)
            for g in range(2):
                def ev_rv(tt, b, g=g):
                    K.op("act", lambda e: e.activation(out=A_rv[:P, tt, g * 512:(g + 1) * 512], in_=psb[b][:P, :],
                                                       func=AF.Copy), reads=[PS(b)], writes=[("arena", "rv", tt, g)])
                proj(P, ntl, RV + hp * 1024 + g * 512, 512, ev_rv)
            if not is_meta:
                for g in range(2):
                    def ev_rg(tt, b, g=g):
                        K.op("act", lambda e: e.activation(out=A_rg[:P, tt, g * 512:(g + 1) * 512], in_=psb[b][:P, :],
                                                           func=AF.Silu), reads=[PS(b)], writes=[("arena", "rg", tt, g)])
                    proj(P, ntl, RG + hp * 1024 + g * 512, 512, ev_rg)
            for tt in range(ntl):
                for hl in range(2):
                    hh = 2 * hp + hl
                    if not is_meta:
                        cs = slice(tt * P, (tt + 1) * P)
                        b1 = K.ps_alloc()
                        K.pe([lambda pe, dc=dc: pe.matmul(psb[b1][:, :128], lhsT=A_rkT[:, 2 * hl + dc, cs],
                                                          rhs=A_rqT[:, 2 * hl + dc, cs], start=(dc == 0), stop=(dc == 1))
                              for dc in range(2)],
                             reads=[("arena", "rkT", tt), ("arena", "rqT", tt)], writes=[PS(b1)])
                        ai = nxt("ATb", 2)
                        K.op("dve", lambda e: e.tensor_tensor(out=ATb[ai][:, :], in0=psb[b1][:, :128], in1=DT[:, hh, :],
                                                              op=ALU.mult),
                             reads=[PS(b1), ("DT",)], writes=[("ATb", ai)])
                        K.ps_free(b1)
                        qi = nxt("rqx", 2)
                        for dc in range(2):
                            K.op("pool", lambda e, dc=dc: e.tensor_tensor(out=rqx[qi][:, dc, :], in0=A_rqT[:, 2 * hl + dc, cs],
                                                                          in1=XI[:, hh, :], op=ALU.mult),
                                 reads=[("arena", "rqT", tt), ("XI",)], writes=[("rqx", qi, dc)])
                        b2 = K.ps_alloc()
                        fns = [lambda pe: pe.matmul(psb[b2][:, :], lhsT=ATb[ai][:, :], rhs=A_rv[:, tt, hl * 512:(hl + 1) * 512],
                                                    start=True, stop=False)]
                        for dc in range(2):
                            fns.append(lambda pe, dc=dc: pe.matmul(psb[b2][:, :], lhsT=rqx[qi][:, dc, :], rhs=Sb[:, 2 * hh + dc, :],
                                                                   start=False, stop=(dc == 1)))
                        K.pe(fns, reads=[("ATb", ai), ("arena", "rv", tt, hl), ("rqx", qi, 0), ("rqx", qi, 1), ("Sb", hh)],
                             writes=[PS(b2)])
                        c = 8 + nxt("gss", 8)
                        K.op("act", lambda e: e.activation(out=junk[:, :512], in_=psb[b2][:, :], func=AF.Square,
                                                           accum_out=stat[:, c:c + 1]),
                             reads=[PS(b2)], writes=[("stat", c)])
                        K.op("act", lambda e: e.activation(out=stat[:, c:c + 1], in_=stat[:, c:c + 1], func=AF.Sqrt, bias=epsb[:, :], scale=1.0 / 512),
                             reads=[("stat", c), ("epsb",)], writes=[("stat", c)])
                        K.op("dve", lambda e: e.reciprocal(out=stat[:, c:c + 1], in_=stat[:, c:c + 1]),
                             reads=[("stat", c)], writes=[("stat", c)])
                        K.op("dve", lambda e: e.scalar_tensor_tensor(
                            out=A_rg[:, tt, hl * 512:(hl + 1) * 512], in0=psb[b2][:, :], scalar=stat[:, c:c + 1],
                            in1=A_rg[:, tt, hl * 512:(hl + 1) * 512], op0=ALU.mult, op1=ALU.mult),
                            reads=[PS(b2), ("stat", c)], writes=[("arena", "rg", tt, hl)])
                        K.ps_free(b2)
                    for dc in range(2):
                        b3 = K.ps_alloc()
                        K.pe([lambda pe: pe.matmul(psb[b3][:, :], lhsT=A_kz[:P, tt, hl * 256 + dc * 128:hl * 256 + (dc + 1) * 128],
                                                   rhs=A_rv[:P, tt, hl * 512:(hl + 1) * 512], start=True, stop=True)],
                             reads=[("arena", "kz", tt), ("arena", "rv", tt, hl)], writes=[PS(b3)])
                        if is_meta:
                            K.op("dve", lambda e: e.tensor_copy(S[:, 2 * hh + dc, :], psb[b3][:, :]),
                                 reads=[PS(b3)], writes=[("S", hh, dc)])
                        else:
                            K.op("dve", lambda e: e.scalar_tensor_tensor(
                                out=S[:, 2 * hh + dc, :], in0=S[:, 2 * hh + dc, :], scalar=float(GAM[hh] ** 128),
                                in1=psb[b3][:, :], op0=ALU.mult, op1=ALU.add),
                                reads=[PS(b3)], writes=[("S", hh, dc)])
                        K.ps_free(b3)
                        K.op("pool", lambda e: e.tensor_copy(Sb[:, 2 * hh + dc, :], S[:, 2 * hh + dc, :]),
                             reads=[("S", hh, dc)], writes=[("Sb", hh)])
                if not is_meta:
                    transposes([A_rg[:P, tt, k * 128:(k + 1) * 128] for k in range(8)],
                               [("arena", "rg", tt, 0), ("arena", "rg", tt, 1)], P,
                               A_ygT[:, hp * 8:(hp + 1) * 8, tt * P:(tt + 1) * P], [("arena", "ygT", hp, tt)], eng="act")
        if not is_meta:
            for g in range(2):
                def ev_gr(tt, b, g=g):
                    K.op("act", lambda e: e.activation(out=sg[:P, tt, g * 512:(g + 1) * 512], in_=psb[b][:P, :],
                                                       func=AF.Sigmoid), reads=[PS(b)], writes=[("sg", tt, g)])
                proj(P, ntl, GR + g * 512, 512, ev_gr)

            def ev_ro(tt, dh, b):
                K.op("dve", lambda e: e.tensor_tensor(out=merged[:P, tt, dh * 512:(dh + 1) * 512], in0=psb[b][:P, :],
                                                      in1=sg[:P, tt, dh * 512:(dh + 1) * 512], op=ALU.mult),
                     reads=[PS(b), ("sg", tt, dh)], writes=[("mg", tt, dh)])
            down_proj(lambda kc, tt: A_ygT[:, kc, tt * P:(tt + 1) * P],
                      lambda kc: [("arena", "ygT", kc // 8, tt) for tt in range(ntl)], 16, "wro", P, ntl, ev_ro)
        K.retire("arena")
        if not is_meta:
            for g in range(2):
                def ev_aq(tt, b, g=g):
                    ri = nxt("rp", 2)
                    rope_evac(P, tt, b, 512, 4, 128, 256, ri)
                    transposes([rp[ri][:P, k * 128:(k + 1) * 128] for k in range(4)], RP(ri), P,
                               A_aqT[:, g * 4:(g + 1) * 4, tt * P:(tt + 1) * P], [("arena", "aqT", tt, g)], eng="dve")
                proj(P, ntl, AQ + g * 512, 512, ev_aq)
        for g in range(2):
            def ev_ak(tt, b, g=g):
                ri = nxt("rp", 2)
                rope_evac(P, tt, b, 512, 4, 128, 256, ri)
                gt = 0 if is_meta else 1 + blk * NT + tt
                transposes([rp[ri][:P, k * 128:(k + 1) * 128] for k in range(4)], RP(ri), P,
                           akT[:, g * 4:(g + 1) * 4, pos0 + tt * P:pos0 + (tt + 1) * P], [("akT", gt, g)], eng="dve")
            proj(P, ntl, AK + g * 512, 512, ev_ak)
        for g in range(2):
            def ev_av(tt, b, g=g):
                gt = 0 if is_meta else 1 + blk * NT + tt
                K.op("act", lambda e: e.activation(out=av[:P, gt, g * 512:(g + 1) * 512], in_=psb[b][:P, :], func=AF.Copy),
                     reads=[PS(b)], writes=[("av", gt, g)])
            proj(P, ntl, AV + g * 512, 512, ev_av)
        if is_meta:
            return

        def ev_iq(tt, b):
            ri = nxt("rp", 2)
            rope_evac(P, tt, b, 512, 8, 64, 384, ri)
            transposes([rp[ri][:P, k * 128:(k + 1) * 128] for k in range(4)], RP(ri), P,
                       A_iqT[:, :, tt * P:(tt + 1) * P], [("arena", "iqT", tt)], eng="dve")
        proj(P, ntl, IQ, 512, ev_iq)

        def ev_ik(tt, b):
            ri = nxt("rp", 2)
            xi_ = nxt("xs", 2)
            K.op("act", lambda e: e.activation(out=xs[xi_][:P, :72], in_=psb[b][:P, :72], func=AF.Copy),
                 reads=[PS(b)], writes=[("xs", xi_)])
            x1, x2 = xs[xi_][:P, 0:32], xs[xi_][:P, 32:64]
            cosb, sinb = tab[:P, tt, 384:416], tab[:P, tt, 416:448]
            rd = [("xs", xi_), ("tab",)]
            K.op("dve", lambda e: e.tensor_tensor(out=rt[0][:P, :32], in0=x1, in1=cosb, op=ALU.mult), reads=rd, writes=[("rt", 0)])
            K.op("dve", lambda e: e.tensor_tensor(out=rt[1][:P, :32], in0=x2, in1=sinb, op=ALU.mult), reads=rd, writes=[("rt", 1)])
            K.op("dve", lambda e: e.tensor_tensor(out=rt[2][:P, :32], in0=x2, in1=cosb, op=ALU.mult), reads=rd, writes=[("rt", 2)])
            K.op("dve", lambda e: e.tensor_tensor(out=rt[3][:P, :32], in0=x1, in1=sinb, op=ALU.mult), reads=rd, writes=[("rt", 3)])
            for rep in range(2):
                K.op("dve", lambda e, rep=rep: e.tensor_tensor(out=rp[ri][:P, rep * 64:rep * 64 + 32], in0=rt[0][:P, :32],
                                                               in1=rt[1][:P, :32], op=ALU.subtract),
                     reads=[("rt", 0), ("rt", 1)], writes=[("rp", ri, 0)])
                K.op("dve", lambda e, rep=rep: e.tensor_tensor(out=rp[ri][:P, rep * 64 + 32:rep * 64 + 64], in0=rt[2][:P, :32],
                                                               in1=rt[3][:P, :32], op=ALU.add),
                     reads=[("rt", 2), ("rt", 3)], writes=[("rp", ri, 1)])
            K.op("dve", lambda e: e.tensor_copy(iw[:P, tt, :], xs[xi_][:P, 64:72]), reads=[("xs", xi_)], writes=[("iw", tt)])
            gi = blk * NT + tt
            transposes([rp[ri][:P, 0:128]], RP(ri), P, kiT[:, gi * 128:(gi + 1) * 128].unsqueeze(1), [("kiT", gi)], eng="dve")
        proj(P, ntl, IK, 72, ev_ik)

        Lblk = 128 * (blk * NT + NT)
        for tt in range(ntl):
            gi = blk * NT + tt
            L = 128 * (gi + 1)
            cs = slice(tt * 128, (tt + 1) * 128)
            for sbk in range((L + 511) // 512):
                ncols = min(512, L - sbk * 512)
                kt = [("kiT", t_) for t_ in range(sbk * 4, sbk * 4 + ncols // 128)]
                for hh in range(8):
                    c, r = hh // 2, hh % 2
                    b = K.ps_alloc()
                    K.pe([lambda pe: pe.matmul(psb[b][:, :ncols], lhsT=A_iqT[r * 64:(r + 1) * 64, c, cs],
                                               rhs=kiT[r * 64:(r + 1) * 64, sbk * 512:sbk * 512 + ncols], start=True, stop=True)],
                         reads=[("arena", "iqT", tt)] + kt, writes=[PS(b)])
                    li = nxt("rl", 2)
                    K.op("act", lambda e: e.activation(out=rl[li][:, :ncols], in_=psb[b][:, :ncols], func=AF.Relu),
                         reads=[PS(b)], writes=[("rl", li)])
                    K.ps_free(b)
                    scv = A_sc[:, sbk * 512:sbk * 512 + ncols]
                    if hh == 0:
                        K.op("dve", lambda e: e.tensor_scalar(out=scv, in0=rl[li][:, :ncols], scalar1=iw[:, tt, 0:1], scalar2=None,
                                                              op0=ALU.mult),
                             reads=[("rl", li), ("iw", tt)], writes=[("arena", "sc", sbk)])
                    else:
                        K.op("dve", lambda e: e.scalar_tensor_tensor(out=scv, in0=rl[li][:, :ncols], scalar=iw[:, tt, hh:hh + 1],
                                                                     in1=scv, op0=ALU.mult, op1=ALU.add),
                             reads=[("rl", li), ("iw", tt)], writes=[("arena", "sc", sbk)])
            SCK = [("arena", "sc", s_) for s_ in range(4)]
            if L < Lblk:
                K.op("pool", lambda e: e.memset(A_sc[:, L:Lblk], NEGBIG), reads=[], writes=SCK)
            LO, W0, MID, CNT, TMP = 16 + tt * 8, 17 + tt * 8, 18 + tt * 8, 19 + tt * 8, 20 + tt * 8

            def col(c_):
                return stat[:, c_:c_ + 1]
            if gi >= 2:
                K.op("dve", lambda e: e.tensor_reduce(out=col(LO), in_=A_sc[:, :L], axis=AX.X, op=ALU.min),
                     reads=SCK, writes=[("stat", LO)])
                K.op("dve", lambda e: e.tensor_reduce(out=col(W0), in_=A_sc[:, :L], axis=AX.X, op=ALU.max),
                     reads=SCK, writes=[("stat", W0)])
                K.op("dve", lambda e: e.tensor_tensor(out=col(W0), in0=col(W0), in1=col(LO), op=ALU.subtract),
                     reads=[("stat", W0), ("stat", LO)], writes=[("stat", W0)])
            K.op("dve", lambda e: e.tensor_tensor(out=A_sc[:, L - 128:L], in0=A_sc[:, L - 128:L], in1=NEG[:, :], op=ALU.add),
                 reads=[("NEG",)] + SCK, writes=SCK)
            if gi >= 2:
                for it in range(NBIS):
                    f = 2.0 ** -(it + 1)
                    K.op("dve", lambda e: e.scalar_tensor_tensor(out=col(MID), in0=col(W0), scalar=f, in1=col(LO),
                                                                 op0=ALU.mult, op1=ALU.add),
                         reads=[("stat", W0), ("stat", LO)], writes=[("stat", MID)])
                    K.op("dve", lambda e: e.tensor_scalar(out=mk[:, :L], in0=A_sc[:, :L], scalar1=col(MID), scalar2=0.0,
                                                          op0=ALU.is_ge, op1=ALU.add, accum_out=col(CNT)),
                         reads=SCK + [("stat", MID)], writes=[("mk",), ("stat", CNT)])
                    K.op("dve", lambda e: e.scalar_tensor_tensor(out=col(TMP), in0=col(CNT), scalar=KTOP - 0.5, in1=col(W0),
                                                                 op0=ALU.is_ge, op1=ALU.mult),
                         reads=[("stat", CNT), ("stat", W0)], writes=[("stat", TMP)])
                    K.op("dve", lambda e: e.scalar_tensor_tensor(out=col(LO), in0=col(TMP), scalar=f, in1=col(LO),
                                                                 op0=ALU.mult, op1=ALU.add),
                         reads=[("stat", TMP), ("stat", LO)], writes=[("stat", LO)])
            else:
                K.op("dve", lambda e: e.memset(col(LO), -1.0e29), reads=[], writes=[("stat", LO)])
            K.op("dve", lambda e: e.tensor_scalar(out=mk[:, :Lblk], in0=A_sc[:, :Lblk], scalar1=col(LO), scalar2=None,
                                                  op0=ALU.is_ge),
                 reads=SCK + [("stat", LO)], writes=[("mk",)])
            nsb = Lblk // 128
            for s0 in range(0, nsb, 8):
                n = min(8, nsb - s0)
                transposes([mk[:, (s0 + k) * 128:(s0 + k + 1) * 128] for k in range(n)], [("mk",)], 128,
                           A_maskT[:, s0:s0 + n, cs], [("arena", "maskT", tt)], eng="act")

        nkt = blk * NT + NT
        for hh in range(8):
            g = hh // 4
            bo = K.ps_alloc()
            bs = K.ps_alloc()
            tiles = [(-1, NM, 0)] + [(si, 128, 128 if si == nkt - 1 else 0) for si in range(nkt)]
            for idx, (si, Ps, q0) in enumerate(tiles):
                first, last = idx == 0, idx == len(tiles) - 1
                kc0 = 0 if si < 0 else NM + si * 128
                gt = si + 1
                b = K.ps_alloc()
                K.pe([lambda pe: pe.matmul(psb[b][:Ps, q0:TB], lhsT=akT[:, hh, kc0:kc0 + Ps], rhs=A_aqT[:, hh, q0:TB],
                                           start=True, stop=True)],
                     reads=[("akT", gt, g), ("arena", "aqT", 0, g), ("arena", "aqT", 1, g)], writes=[PS(b)])
                pi = nxt("pT", 3)
                K.op("act", lambda e: e.activation(out=pT[pi][:Ps, q0:TB], in_=psb[b][:Ps, q0:TB], func=AF.Exp, scale=ATT_SCALE),
                     reads=[PS(b)], writes=[("pT", pi)])
                K.ps_free(b)
                if si >= 0:
                    K.op("pool", lambda e: e.tensor_tensor(out=pT[pi][:Ps, q0:TB], in0=pT[pi][:Ps, q0:TB],
                                                           in1=A_maskT[:Ps, si, q0:TB], op=ALU.mult),
                         reads=[("pT", pi), ("arena", "maskT", 0), ("arena", "maskT", 1)], writes=[("pT", pi)])
                K.pe([lambda pe: pe.matmul(psb[bo][:, q0:TB], lhsT=av[:Ps, gt, hh * 128:(hh + 1) * 128], rhs=pT[pi][:Ps, q0:TB],
                                           start=first, stop=last),
                      lambda pe: pe.matmul(psb[bs][:, q0:TB], lhsT=ones[:Ps, :], rhs=pT[pi][:Ps, q0:TB],
                                           start=first, stop=last)],
                     reads=[("av", gt, g), ("pT", pi), ("ones",)], writes=[PS(bo), PS(bs)])
            ri_ = nxt("rs", 2)
            K.op("dve", lambda e: e.reciprocal(out=rs[ri_][:, :], in_=psb[bs][:, :TB]), reads=[PS(bs)], writes=[("rs", ri_)])
            K.op("dve", lambda e: e.tensor_tensor(out=A_attnT[:, hh, :], in0=psb[bo][:, :TB], in1=rs[ri_][:, :], op=ALU.mult),
                 reads=[PS(bo), ("rs", ri_)], writes=[("arena", "attnT", hh)])
            K.ps_free(bo)
            K.ps_free(bs)
        for g in range(2):
            def ev_ga(tt, b, g=g):
                K.op("act", lambda e: e.activation(out=sg[:P, tt, g * 512:(g + 1) * 512], in_=psb[b][:P, :], func=AF.Sigmoid),
                     reads=[PS(b)], writes=[("sg", tt, g)])
            proj(P, ntl, GA + g * 512, 512, ev_ga)

        def ev_ao(tt, dh, b):
            sl = slice(dh * 512, (dh + 1) * 512)
            xi_ = nxt("xs", 2)
            K.op("dve", lambda e: e.tensor_tensor(out=xs[xi_][:P, :], in0=psb[b][:P, :], in1=sg[:P, tt, sl], op=ALU.mult),
                 reads=[PS(b), ("sg", tt, dh)], writes=[("xs", xi_)])
            K.op("pool", lambda e: e.tensor_tensor(out=merged[:P, tt, sl], in0=merged[:P, tt, sl], in1=xs[xi_][:P, :], op=ALU.add),
                 reads=[("xs", xi_)], writes=[("mg", tt, dh)])
        down_proj(lambda kc, tt: A_attnT[:, kc, tt * P:(tt + 1) * P], lambda kc: [("arena", "attnT", kc)], 8, "wao", P, ntl, ev_ao)
        for tt in range(ntl):
            transposes([merged[:P, tt, k * 128:(k + 1) * 128] for k in range(8)], [("mg", tt, 0), ("mg", tt, 1)], P,
                       uT[:, :, tt * P:(tt + 1) * P], [("uT", tt)], eng="act")

        def ev_mo(tt, dh, b):
            sl = slice(dh * 512, (dh + 1) * 512)
            K.op("dve", lambda e: e.tensor_tensor(out=h[:P, tt, sl], in0=psb[b][:P, :], in1=h[:P, tt, sl], op=ALU.add),
                 reads=[PS(b)], writes=[("h", tt)])
        down_proj(lambda kc, tt: uT[:, kc, tt * P:(tt + 1) * P], lambda kc: [("uT", tt) for tt in range(ntl)], 8, "wmo", P, ntl, ev_mo)

    K.dma("sp", h[:NM, 0, :], dram["meta"], writes=[("h", 0)], slot=("x", 0))
    ffn(NM, 1, 0, "w1g", "w1u", "w1d")
    mixer(NM, 1, -1)
    for blk in range(nblk):
        for tt in range(NT):
            r0 = blk * TB + tt * 128
            K.dma("sp", h[:, tt, :], dram["x"][r0:r0 + 128, :], writes=[("h", tt)], slot=("x", tt))
        ffn(128, NT, 0, "w1g", "w1u", "w1d")
        if stages >= 2:
            mixer(128, NT, blk)
        if stages >= 3:
            ffn(128, NT, 2, "w2g", "w2u", "w2d")
        if stages >= 4:
            K.dma("sp", gbc[:, :], dram["gains"][3 * 128:4 * 128, :], writes=[("gbc",)], slot=("gbc",))
        for tt in range(NT):
            r0 = blk * TB + tt * 128
            if stages >= 4:
                c = tt
                hs = h[:, tt, :]
                K.op("act", lambda e: e.activation(out=junk[:, :], in_=hs, func=AF.Square, accum_out=stat[:, c:c + 1]),
                     reads=[("h", tt)], writes=[("stat", c)])
                K.op("act", lambda e: e.activation(out=stat[:, c:c + 1], in_=stat[:, c:c + 1], func=AF.Sqrt, bias=epsb[:, :], scale=1.0 / D),
                     reads=[("stat", c), ("epsb",)], writes=[("stat", c)])
                K.op("dve", lambda e: e.reciprocal(out=stat[:, c:c + 1], in_=stat[:, c:c + 1]),
                     reads=[("stat", c)], writes=[("stat", c)])
                K.op("dve", lambda e: e.scalar_tensor_tensor(out=hs, in0=hs, scalar=stat[:, c:c + 1], in1=gbc[:, :],
                                                             op0=ALU.mult, op1=ALU.mult),
                     reads=[("stat", c), ("gbc",)], writes=[("h", tt)])
            K.dma("pool", dram["y"][r0:r0 + 128, :], h[:, tt, :], reads=[("h", tt)], writes=[], slot=("y", tt))
    sp = K.E["sp"]
    for tt in range(NT):
        s = K.dsem[("y", tt)]
        sp.wait(s[0], s[1])


_CACHE = {}


def _prep_inputs(inputs):
    f = lambda a: np.ascontiguousarray(np.asarray(a, dtype=np.float32))
    shared = {
        "meta": f(inputs["meta_tokens"]),
        "gains": np.ascontiguousarray(np.concatenate([
            np.broadcast_to(f(inputs["ffn1_norm"]).reshape(1, D), (128, D)),
            np.broadcast_to(f(inputs["mix_norm"]).reshape(1, D), (128, D)),
            np.broadcast_to(f(inputs["ffn2_norm"]).reshape(1, D), (128, D)),
            np.broadcast_to(f(inputs["final_norm"]).reshape(1, D), (128, D))], axis=0)),
        "w1g": f(inputs["ffn1_w_gate"][0]), "w1u": f(inputs["ffn1_w_up"][0]), "w1d": f(inputs["ffn1_w_down"][0]),
        "win": f(inputs["w_in"][0]), "wro": f(inputs["w_ret_out"][0]), "wao": f(inputs["w_att_out"][0]),
        "wmo": f(inputs["w_mix_out"][0]),
        "w2g": f(inputs["ffn2_w_gate"][0]), "w2u": f(inputs["ffn2_w_up"][0]), "w2d": f(inputs["ffn2_w_down"][0]),
    }
    shared.update(_consts())
    return shared


def kernel(**inputs):
    x = np.asarray(inputs["x"], dtype=np.float32)
    B = x.shape[0]
    shared = _prep_inputs(inputs)
    if "nc" not in _CACHE:
        _CACHE["nc"] = build_program()
    nc = _CACHE["nc"]
    in_maps = []
    for b in range(B):
        m = dict(shared)
        m["x"] = np.ascontiguousarray(x[b])
        in_maps.append(m)
    res = run_bass_kernel_spmd(nc, in_maps, core_ids=list(range(B)))
    return np.stack([np.asarray(r["y"], dtype=np.float32) for r in res.results], axis=0)
```

```python
import contextlib
import numpy as np
import concourse.bass as bass
import concourse.mybir as mybir
from concourse.bass_utils import run_bass_kernel_spmd

F32 = mybir.dt.float32
BF16 = mybir.dt.bfloat16
AF = mybir.ActivationFunctionType
ALU = mybir.AluOpType
AX = mybir.AxisListType

D = 1024
DFF = 2816
NM = 16
SEQ = 2048
T = NM + SEQ
DIN = 11848
TB = 256
NT = 2
NBLK = SEQ // TB
EPS = 1e-6
KTOP = 256
NBIS = 18
GAM = [1.0 - 2.0 ** (-5.0 - h) for h in range(4)]
RQ, RK, RV, RG, AQ, AK, AV, IQ, IK, IW, GR, GA = 0, 1024, 2048, 4096, 6144, 7168, 8192, 9216, 9728, 9792, 9800, 10824
ATT_SCALE = 128.0 ** -0.5
NEGBIG = -1.0e30

DEBUG = {}


class Eng:
    def __init__(self, K, name, raw):
        self.K, self.name, self.raw = K, name, raw
        self.sem = None
        self.cnt = 0
        self.waited = {}

    def rotate(self):
        if self.sem is None or self.cnt >= 3000:
            self.sem = self.K.new_sem(self.name)
            self.cnt = 0

    def wait(self, sem, val):
        if self.waited.get(id(sem), 0) >= val:
            return
        if sem is self.sem and self.name == "pe":
            return
        self.raw.wait_ge(sem, val)
        self.waited[id(sem)] = val


class Ctx:
    def __init__(self, nc, es):
        self.nc, self.es = nc, es
        self.nsem = 0
        self.keys = {}
        self.base = {}
        self.E = {}
        for n, raw in (("pe", nc.tensor), ("act", nc.scalar), ("dve", nc.vector), ("pool", nc.gpsimd), ("sp", nc.sync)):
            self.E[n] = Eng(self, n, raw)
        self.dsem = {}
        self.ps_free_list = []
        self.ntmp = 0

    def new_sem(self, name):
        self.nsem += 1
        return self.es.enter_context(self.nc.semaphore("s%d_%s" % (self.nsem, name)))

    def sb(self, name, shape, dtype):
        return self.es.enter_context(self.nc.sbuf_tensor(name, shape, dtype))

    def _get(self, k):
        st = self.keys.get(k)
        if st is None:
            b = self.base.get(k[0])
            st = [None, dict(b) if b else {}]
            self.keys[k] = st
        return st

    def retire(self, slot):
        agg = dict(self.base.get(slot, {}))

        def add(t):
            if t is None:
                return
            cur = agg.get(id(t[0]))
            if cur is None or cur[1] < t[1]:
                agg[id(t[0])] = t

        for k in [k for k in self.keys if k[0] == slot]:
            st = self.keys.pop(k)
            add(st[0])
            for t in st[1].values():
                add(t)
        self.base[slot] = agg

    def _deps(self, eng, reads, writes):
        need = {}

        def add(t):
            if t is None:
                return
            cur = need.get(id(t[0]))
            if cur is None or cur[1] < t[1]:
                need[id(t[0])] = t

        for k in reads:
            add(self._get(k)[0])
        for k in writes:
            st = self._get(k)
            add(st[0])
            for t in st[1].values():
                add(t)
        for sem, val in need.values():
            eng.wait(sem, val)

    def _commit(self, t, reads, writes):
        for k in reads:
            if k in writes:
                continue
            st = self._get(k)
            cur = st[1].get(id(t[0]))
            if cur is None or cur[1] < t[1]:
                st[1][id(t[0])] = t
        for k in writes:
            st = self._get(k)
            st[0] = t
            st[1] = {}

    def op(self, en, fn, reads=(), writes=()):
        eng = self.E[en]
        eng.rotate()
        self._deps(eng, reads, writes)
        ins = fn(eng.raw)
        ins.then_inc(eng.sem, 1)
        eng.cnt += 1
        t = (eng.sem, eng.cnt)
        self._commit(t, reads, writes)
        return t

    def pe(self, fns, reads=(), writes=()):
        eng = self.E["pe"]
        eng.rotate()
        self._deps(eng, reads, writes)
        ins = None
        for fn in fns:
            ins = fn(eng.raw)
        ins.then_inc(eng.sem, 1)
        eng.cnt += 1
        t = (eng.sem, eng.cnt)
        self._commit(t, reads, writes)
        return t

    def dma(self, q, out, in_, reads=(), writes=(), slot=None, serial=True):
        eng = self.E[q]
        self._deps(eng, reads, writes)
        s = self.dsem.get(slot)
        if s is None:
            s = [self.new_sem("d"), 0]
            self.dsem[slot] = s
        if serial and s[1] > 0:
            eng.wait(s[0], s[1])
        ins = eng.raw.dma_start(out=out, in_=in_)
        ins.then_inc(s[0], 16)
        s[1] += 16
        t = (s[0], s[1])
        self._commit(t, reads, writes)
        return t

    def ps_alloc(self):
        assert self.ps_free_list, "out of PSUM banks"
        return self.ps_free_list.pop(0)

    def ps_free(self, b):
        self.ps_free_list.append(b)


def _consts():
    pos = np.arange(T, dtype=np.float32)

    def tab(d):
        inv = (np.float32(10000.0) ** (-np.arange(0, d, 2, dtype=np.float32) / np.float32(d))).astype(np.float32)
        ang = (pos[:, None] * inv[None, :]).astype(np.float32)
        return np.cos(ang).astype(np.float32), np.sin(ang).astype(np.float32)

    c256, s256 = tab(256)
    c128, s128 = tab(128)
    c64, s64 = tab(64)
    rope = np.concatenate([c256, s256, c128, s128, c64, s64], axis=1).astype(np.float32)
    i = np.arange(128)
    dt = np.zeros((128, 4, 128), np.float64)
    xi = np.zeros((128, 4, 128), np.float64)
    zeta = np.zeros((128, 8), np.float64)
    for h in range(4):
        lg = np.log1p(-(2.0 ** (-5.0 - h)))
        diff = i[None, :] - i[:, None]
        dt[:, h, :] = np.where(diff >= 0, np.exp(lg * np.maximum(diff, 0)), 0.0) / 16.0
        xi[:, h, :] = np.exp(lg * (i[None, :] + 1.0))
        zeta[:, h] = np.exp(lg * (127.0 - i)) / 16.0
        zeta[:NM, 4 + h] = np.exp(lg * (NM - 1.0 - np.arange(NM))) / 16.0
    neg = np.where(i[None, :] <= i[:, None], 0.0, NEGBIG)
    return {
        "c_rope": rope,
        "c_dt": dt.astype(np.float32).reshape(128, 512),
        "c_xi": xi.astype(np.float32).reshape(128, 512),
        "c_zeta": zeta.astype(np.float32),
        "c_neg": neg.astype(np.float32),
        "c_ident": np.eye(128, dtype=np.float32),
        "c_ones": np.ones((128, 128), np.float32),
    }


WSPECS = [
    ("w1g", D, DFF), ("w1u", D, DFF), ("w1d", DFF, D), ("win", D, DIN), ("wro", 2048, D),
    ("wao", D, D), ("wmo", D, D), ("w2g", D, DFF), ("w2u", D, DFF), ("w2d", DFF, D),
]


WDIMS = {n: (r, c) for n, r, c in WSPECS}
WIN_GROUPS = [(c0, 512) for c0 in range(0, 9728, 512)] + [(IK, 72), (GR, 512), (GR + 512, 512), (GA, 512), (GA + 512, 512)]
WIN_SLAB = {c0: i for i, (c0, w) in enumerate(WIN_GROUPS)}


SCRATCH = ["w1gu", "w1d", "win", "wro", "wao", "wmo", "w2gu", "w2d"]


def _slabs(n):
    if n in ("w1gu", "w2gu"):
        return [("gu", c0, 256) for c0 in range(0, DFF, 256)]
    r, c = WDIMS[n]
    if n == "win":
        return [("c", c0, w) for c0, w in WIN_GROUPS]
    if c == DFF:
        return [("c", c0, min(512, DFF - c0)) for c0 in range(0, DFF, 512)]
    return [("r", r0, min(512, r - r0)) for r0 in range(0, r, 512)]


def _slab_plan(nblk, stages):
    def ffn_(gu, d):
        return [(gu, i) for i in range(len(_slabs(gu)))] + [(d, i) for i in range(len(_slabs(d)))]

    def mix_(meta):
        out = []
        for hp in range(2):
            if not meta:
                out.append(("win", WIN_SLAB[RQ + hp * 512]))
            out.append(("win", WIN_SLAB[RK + hp * 512]))
            out += [("win", WIN_SLAB[RV + hp * 1024 + g * 512]) for g in range(2)]
            if not meta:
                out += [("win", WIN_SLAB[RG + hp * 1024 + g * 512]) for g in range(2)]
        if not meta:
            out += [("win", WIN_SLAB[GR + g * 512]) for g in range(2)]
            out += [("wro", i) for i in range(4)]
            out += [("win", WIN_SLAB[AQ + g * 512]) for g in range(2)]
        out += [("win", WIN_SLAB[AK + g * 512]) for g in range(2)]
        out += [("win", WIN_SLAB[AV + g * 512]) for g in range(2)]
        if not meta:
            out += [("win", WIN_SLAB[IQ]), ("win", WIN_SLAB[IK])]
            out += [("win", WIN_SLAB[GA + g * 512]) for g in range(2)]
            out += [("wao", i) for i in range(2)] + [("wmo", i) for i in range(2)]
        return out

    plan = ffn_("w1gu", "w1d") + mix_(True)
    for b in range(nblk):
        plan += ffn_("w1gu", "w1d")
        if stages >= 2:
            plan += mix_(False)
        if stages >= 3:
            plan += ffn_("w2gu", "w2d")
    return plan


def build_program(nblk=NBLK, stages=99):
    nc = bass.Bass("TRN2", target_bir_lowering=False)
    dram = {}
    dram["x"] = nc.dram_tensor("x", [SEQ, D], F32, kind="ExternalInput").ap()
    dram["meta"] = nc.dram_tensor("meta", [NM, D], F32, kind="ExternalInput").ap()
    dram["gains"] = nc.dram_tensor("gains", [4 * 128, D], F32, kind="ExternalInput").ap()
    for n, r, c in WSPECS:
        dram[n] = nc.dram_tensor(n, [r, c], F32, kind="ExternalInput").ap()
    for n in SCRATCH:
        dram[n + "b"] = nc.dram_tensor(n + "b", [len(_slabs(n)), 128, 4096], BF16, kind="Internal").ap()
    cshapes = {"c_rope": [T, 448], "c_dt": [128, 512], "c_xi": [128, 512], "c_zeta": [128, 8],
               "c_neg": [128, 128], "c_ident": [128, 128], "c_ones": [128, 128]}
    for n, s in cshapes.items():
        dram[n] = nc.dram_tensor(n, s, F32, kind="ExternalInput").ap()
    dram["y"] = nc.dram_tensor("y", [SEQ, D], F32, kind="ExternalOutput").ap()
    for n, s in DEBUG.items():
        dram[n] = nc.dram_tensor(n, s, F32, kind="ExternalOutput").ap()

    with contextlib.ExitStack() as es:
        K = Ctx(nc, es)
        _emit(K, dram, nblk, stages)
    return nc


def _emit(K, dram, nblk, stages):
    nc = K.nc
    sb = K.sb
    akT = sb("akT", [128, 8, T], BF16)
    av = sb("av", [128, 17, D], BF16)
    kiT = sb("kiT", [128, SEQ], BF16)
    S = sb("S", [128, 8, 512], F32)
    Sb = sb("Sb", [128, 8, 512], BF16)
    ident = sb("ident", [128, 128], BF16)
    ones = sb("ones", [128, 128], BF16)
    DT = sb("DT", [128, 4, 128], F32)
    XI = sb("XI", [128, 4, 128], F32)
    zeta = sb("zeta", [128, 8], F32)
    NEG = sb("NEG", [128, 128], F32)
    gbc = sb("gbc", [128, D], F32)
    h = sb("h", [128, NT, D], F32)
    uT = sb("uT", [128, 8, TB], BF16)
    utm = [sb("utm%d" % i, [128, D], BF16) for i in range(2)]
    arena = sb("arena", [128, 13312], BF16)
    merged = sb("merged", [128, NT, D], BF16)
    sg = sb("sg", [128, NT, D], BF16)
    tab = sb("tab", [128, NT, 448], F32)
    iw = sb("iw", [128, NT, 8], F32)
    NWS = 3
    wslot = [sb("wslot%d" % i, [128, 4096], BF16) for i in range(NWS)]
    xs = [sb("xs%d" % i, [128, 512], F32) for i in range(2)]
    rt = [sb("rt%d" % i, [128, 256], F32) for i in range(8)]
    rp = [sb("rp%d" % i, [128, 512], BF16) for i in range(2)]
    sgt = [sb("sgt%d" % i, [128, TB], BF16) for i in range(2)]
    pT = [sb("pT%d" % i, [128, TB], BF16) for i in range(3)]
    rs = [sb("rs%d" % i, [128, TB], F32) for i in range(2)]
    junk = sb("junk", [128, D], BF16)
    mk = sb("mk", [128, SEQ], BF16)
    ATb = [sb("ATb%d" % i, [128, 128], BF16) for i in range(2)]
    rqx = [sb("rqx%d" % i, [128, 2, 128], BF16) for i in range(2)]
    stat = sb("stat", [128, 64], F32)
    epsb = sb("epsb", [128, 1], F32)
    psb = [K.es.enter_context(nc.psum_tensor("psb%d" % i, [128, 512], F32)) for i in range(8)]
    K.ps_free_list = list(range(8))

    def PS(b):
        return ("ps", b)

    rr = {}

    def nxt(name, n):
        v = rr.get(name, 0)
        rr[name] = (v + 1) % n
        return v

    A_actT = arena[:, 0:22 * TB].rearrange("p (a b) -> p a b", a=22)
    A_rqT = arena[:, 0:1024].rearrange("p (a b) -> p a b", a=4)
    A_rkT = arena[:, 1024:2048].rearrange("p (a b) -> p a b", a=4)
    A_kz = arena[:, 2048:3072].rearrange("p (a b) -> p a b", a=NT)
    A_rv = arena[:, 3072:5120].rearrange("p (a b) -> p a b", a=NT)
    A_rg = arena[:, 5120:7168].rearrange("p (a b) -> p a b", a=NT)
    A_ygT = arena[:, 7168:11264].rearrange("p (a b) -> p a b", a=16)
    A_aqT = arena[:, 0:2048].rearrange("p (a b) -> p a b", a=8)
    A_iqT = arena[:, 2048:3072].rearrange("p (a b) -> p a b", a=4)
    A_attnT = arena[:, 3072:5120].rearrange("p (a b) -> p a b", a=8)
    A_sc = arena[:, 5120:9216].bitcast(F32)
    A_maskT = arena[:, 9216:13312].rearrange("p (a b) -> p a b", a=16)

    K.op("dve", lambda e: e.memset(epsb[:, :], EPS), reads=[], writes=[("epsb",)])
    def cload(dst, src, key, cast):
        K.dma("pool" if cast else "sp", dst, src, writes=[key], slot=key)

    cload(ident[:, :], dram["c_ident"], ("ident",), True)
    cload(ones[:, :], dram["c_ones"], ("ones",), True)
    cload(DT[:, :, :].rearrange("p a b -> p (a b)"), dram["c_dt"], ("DT",), False)
    cload(XI[:, :, :].rearrange("p a b -> p (a b)"), dram["c_xi"], ("XI",), False)
    cload(zeta[:, :], dram["c_zeta"], ("zeta",), False)
    cload(NEG[:, :], dram["c_neg"], ("NEG",), False)

    for n in SCRATCH:
        for si, (kind, off, sz) in enumerate(_slabs(n)):
            if kind == "gu":
                dstv = dram[n + "b"][si, :, :].rearrange("p (k c) -> p k c", k=8)
                for j, src_n in enumerate((n[:2] + "g", n[:2] + "u")):
                    src = dram[src_n].rearrange("(k p) c -> p k c", p=128)[:, :, off:off + 256]
                    K.dma("pool", dstv[:, :, j * 256:(j + 1) * 256], src, writes=[], slot=("wb", n), serial=False)
                continue
            if kind == "c":
                dst = dram[n + "b"][si, :, 0:8 * sz].rearrange("p (k c) -> p k c", k=8)
                src = dram[n].rearrange("(k p) c -> p k c", p=128)[:, :, off:off + sz]
            else:
                nrc = sz // 128
                dst = dram[n + "b"][si, :, 0:nrc * 1024].rearrange("p (k c) -> p k c", k=nrc)
                src = dram[n][off:off + sz, :].rearrange("(k p) c -> p k c", p=128)
            K.dma("pool", dst, src, writes=[], slot=("wb", n), serial=False)
        sm = K.dsem[("wb", n)]
        K._commit((sm[0], sm[1]), [], [("wb", n)])

    plan = _slab_plan(nblk, stages)
    wst = {"issued": 0, "used": 0}

    def _issue_one():
        i = wst["issued"]
        if i >= len(plan):
            return
        n, si = plan[i]
        kind, off, sz = _slabs(n)[si]
        ncol = 4096 if kind == "gu" else (8 * sz if kind == "c" else (sz // 128) * 1024)
        sl = i % NWS
        K.dma("sp", wslot[sl][:, :ncol], dram[n + "b"][si, :, 0:ncol], reads=[("wb", n)], writes=[("ws", sl)], slot=("ws", sl))
        wst["issued"] += 1

    def wnext(n, si):
        i = wst["used"]
        assert plan[i] == (n, si), (i, plan[i], n, si)
        while wst["issued"] <= i:
            assert wst["issued"] < wst.get("done", 0) + NWS
            _issue_one()
        wst["used"] += 1
        return i % NWS

    def wdone(k=1):
        wst["done"] = wst.get("done", 0) + k
        while wst["issued"] < min(len(plan), wst["done"] + NWS):
            _issue_one()

    def wload(name, c0, w):
        if name == "win":
            si = WIN_SLAB[c0]
        else:
            si = c0 // 512
        return wnext(name, si)

    def wv(i, w):
        return wslot[i][:, 0:8 * w].rearrange("p (k c) -> p k c", k=8)

    def transposes(srcs, src_reads, P, dst, dst_writes, eng="act"):
        n = len(srcs)
        b = K.ps_alloc()
        pv = psb[b][:, :].bitcast(BF16).rearrange("p (a b) -> p a b", a=8)
        fns = []
        for k, s_ap in enumerate(srcs):
            fns.append(lambda pe, k=k, s_ap=s_ap: pe.transpose(pv[:, k, :P], s_ap, ident[:P, :P]))
        K.pe(fns, reads=list(src_reads) + [("ident",)], writes=[PS(b)])
        if eng == "act":
            K.op("act", lambda e: e.activation(out=dst, in_=pv[:, :n, :P], func=AF.Copy), reads=[PS(b)], writes=dst_writes)
        else:
            K.op("dve", lambda e: e.tensor_copy(dst, pv[:, :n, :P]), reads=[PS(b)], writes=dst_writes)
        K.ps_free(b)

    def norm_to_uT(P, ntl, grow):
        K.dma("sp", gbc[:, :], dram["gains"][grow * 128:(grow + 1) * 128, :], writes=[("gbc",)], slot=("gbc",))
        for tt in range(ntl):
            hs = h[:P, tt, :]
            c = tt
            K.op("act", lambda e: e.activation(out=junk[:P, :], in_=hs, func=AF.Square, accum_out=stat[:P, c:c + 1]),
                 reads=[("h", tt)], writes=[("stat", c)])
            K.op("act", lambda e: e.activation(out=stat[:P, c:c + 1], in_=stat[:P, c:c + 1], func=AF.Sqrt, bias=epsb[:P, :], scale=1.0 / D),
                 reads=[("stat", c), ("epsb",)], writes=[("stat", c)])
            K.op("dve", lambda e: e.reciprocal(out=stat[:P, c:c + 1], in_=stat[:P, c:c + 1]),
                 reads=[("stat", c)], writes=[("stat", c)])
            ui = nxt("utm", 2)
            K.op("dve", lambda e: e.scalar_tensor_tensor(out=utm[ui][:P, :], in0=hs, scalar=stat[:P, c:c + 1],
                                                         in1=gbc[:P, :], op0=ALU.mult, op1=ALU.mult),
                 reads=[("h", tt), ("stat", c), ("gbc",)], writes=[("utm", ui)])
            transposes([utm[ui][:P, k * 128:(k + 1) * 128] for k in range(8)], [("utm", ui)], P,
                       uT[:, :, tt * P:(tt + 1) * P], [("uT", tt)], eng="act")

    def down_proj(lhs_fn, lhs_reads_fn, nk, wname, P, ntl, evac):
        banks = [[K.ps_alloc() for dh in range(2)] for tt in range(ntl)]
        wi = None
        for kc in range(nk):
            if kc % 4 == 0:
                wi = wnext(wname, kc // 4)
            wvv = wslot[wi][:, :].rearrange("p (k c) -> p k c", k=4)
            fns = []
            for tt in range(ntl):
                for dh in range(2):
                    fns.append(lambda pe, tt=tt, dh=dh: pe.matmul(
                        psb[banks[tt][dh]][:P, :], lhsT=lhs_fn(kc, tt), rhs=wvv[:, kc % 4, dh * 512:(dh + 1) * 512],
                        start=(kc == 0), stop=(kc == nk - 1)))
            K.pe(fns, reads=[("ws", wi)] + lhs_reads_fn(kc),
                 writes=[PS(banks[tt][dh]) for tt in range(ntl) for dh in range(2)])
            if kc % 4 == 3 or kc == nk - 1:
                wdone(1)
        for tt in range(ntl):
            for dh in range(2):
                evac(tt, dh, banks[tt][dh])
                K.ps_free(banks[tt][dh])

    def ffn(P, ntl, grow, wgu, wdn):
        N = P * ntl
        norm_to_uT(P, ntl, grow)
        K.retire("arena")
        for j in range(DFF // 256):
            ig = wnext(wgu, j)
            gv = wslot[ig][:, :].rearrange("p (k c) -> p k c", k=8)
            for fl in range(2):
                fc = 2 * j + fl
                pg = K.ps_alloc()
                pu = K.ps_alloc()
                K.pe([lambda pe, k=k: pe.matmul(psb[pg][:, :N], lhsT=gv[:, k, fl * 128:(fl + 1) * 128],
                                                rhs=uT[:, k, :N], start=(k == 0), stop=(k == 7)) for k in range(8)],
                     reads=[("ws", ig)] + [("uT", tt) for tt in range(ntl)], writes=[PS(pg)])
                K.pe([lambda pe, k=k: pe.matmul(psb[pu][:, :N], lhsT=gv[:, k, 256 + fl * 128:256 + (fl + 1) * 128],
                                                rhs=uT[:, k, :N], start=(k == 0), stop=(k == 7)) for k in range(8)],
                     reads=[("ws", ig)] + [("uT", tt) for tt in range(ntl)], writes=[PS(pu)])
                si = nxt("sgt", 2)
                K.op("act", lambda e: e.activation(out=sgt[si][:, :N], in_=psb[pg][:, :N], func=AF.Silu),
                     reads=[PS(pg)], writes=[("sgt", si)])
                K.op("dve", lambda e: e.tensor_tensor(out=A_actT[:, fc, :N], in0=sgt[si][:, :N], in1=psb[pu][:, :N],
                                                      op=ALU.mult),
                     reads=[("sgt", si), PS(pu)], writes=[("arena", "actT", fc)])
                K.ps_free(pg)
                K.ps_free(pu)
            wdone(1)

        def evac(tt, dh, b):
            K.op("dve", lambda e: e.scalar_tensor_tensor(out=h[:P, tt, dh * 512:(dh + 1) * 512], in0=psb[b][:P, :],
                                                         scalar=0.5, in1=h[:P, tt, dh * 512:(dh + 1) * 512],
                                                         op0=ALU.mult, op1=ALU.add),
                 reads=[PS(b)], writes=[("h", tt)])

        down_proj(lambda kc, tt: A_actT[:, kc, tt * P:(tt + 1) * P], lambda kc: [("arena", "actT", kc)], 22, wdn, P, ntl, evac)

    deferred = []

    def flush():
        while deferred:
            deferred.pop(0)()

    def proj(P, ntl, c0, w, evac):
        wi = wload("win", c0, w)
        for tt in range(ntl):
            b = K.ps_alloc()
            K.pe([lambda pe, k=k: pe.matmul(psb[b][:P, :w], lhsT=uT[:, k, tt * P:(tt + 1) * P], rhs=wv(wi, w)[:, k, :w],
                                            start=(k == 0), stop=(k == 7)) for k in range(8)],
                 reads=[("ws", wi), ("uT", tt)], writes=[PS(b)])
            flush()
            post = evac(tt, b)
            if post is not None:
                deferred.append(post)
            K.ps_free(b)
        wdone(1)

    def rope_evac(P, tt, b, w, nh, d, tc0, dst_i):
        hd = d // 2
        xi_ = nxt("xs", 2)
        K.op("act", lambda e: e.activation(out=xs[xi_][:P, :w], in_=psb[b][:P, :w], func=AF.Copy),
             reads=[PS(b)], writes=[("xs", xi_)])
        x = xs[xi_][:P, :w].rearrange("p (h two f) -> p h two f", h=nh, two=2)
        o = rp[dst_i][:P, :w].rearrange("p (h two f) -> p h two f", h=nh, two=2)
        x1, x2 = x[:, :, 0, :], x[:, :, 1, :]
        cosb = tab[:P, tt, tc0:tc0 + hd].unsqueeze(1).to_broadcast([P, nh, hd])
        sinb = tab[:P, tt, tc0 + hd:tc0 + 2 * hd].unsqueeze(1).to_broadcast([P, nh, hd])
        r0 = 4 * nxt("rtset", 2)
        tv = [rt[r0 + i][:P, :nh * hd].rearrange("p (h f) -> p h f", h=nh) for i in range(4)]
        rd = [("xs", xi_), ("tab",)]
        K.op("dve", lambda e: e.tensor_tensor(out=tv[0], in0=x1, in1=cosb, op=ALU.mult), reads=rd, writes=[("rt", r0)])
        K.op("dve", lambda e: e.tensor_tensor(out=tv[1], in0=x2, in1=sinb, op=ALU.mult), reads=rd, writes=[("rt", r0 + 1)])
        K.op("pool", lambda e: e.tensor_tensor(out=tv[2], in0=x2, in1=cosb, op=ALU.mult), reads=rd, writes=[("rt", r0 + 2)])
        K.op("pool", lambda e: e.tensor_tensor(out=tv[3], in0=x1, in1=sinb, op=ALU.mult), reads=rd, writes=[("rt", r0 + 3)])
        K.op("dve", lambda e: e.tensor_tensor(out=o[:, :, 0, :], in0=tv[0], in1=tv[1], op=ALU.subtract),
             reads=[("rt", r0), ("rt", r0 + 1)], writes=[("rp", dst_i, 0)])
        K.op("pool", lambda e: e.tensor_tensor(out=o[:, :, 1, :], in0=tv[2], in1=tv[3], op=ALU.add),
             reads=[("rt", r0 + 2), ("rt", r0 + 3)], writes=[("rp", dst_i, 1)])

    def RP(i):
        return [("rp", i, 0), ("rp", i, 1)]

    def mixer(P, ntl, blk):
        is_meta = blk < 0
        pos0 = 0 if is_meta else NM + blk * TB
        K.dma("sp", tab[:P, 0:ntl, :], dram["c_rope"][pos0:pos0 + P * ntl, :].rearrange("(a p) c -> p a c", p=P),
              writes=[("tab",)], slot=("tab",))
        norm_to_uT(P, ntl, 1)
        K.retire("arena")
        for hp in range(2):
            if not is_meta:
                def ev_rq(tt, b):
                    ri = nxt("rp", 2)
                    rope_evac(P, tt, b, 512, 2, 256, 0, ri)
                    return lambda: transposes([rp[ri][:P, k * 128:(k + 1) * 128] for k in range(4)], RP(ri), P,
                                              A_rqT[:, :, tt * P:(tt + 1) * P], [("arena", "rqT", tt)], eng="dve")
                proj(P, ntl, RQ + hp * 512, 512, ev_rq)

            def ev_rk(tt, b):
                ri = nxt("rp", 2)
                rope_evac(P, tt, b, 512, 2, 256, 0, ri)
                zc = (4 if is_meta else 0) + 2 * hp
                K.op("pool", lambda e: e.tensor_tensor(
                    out=A_kz[:P, tt, :].rearrange("p (h f) -> p h f", h=2),
                    in0=rp[ri][:P, :].rearrange("p (h f) -> p h f", h=2),
                    in1=zeta[:P, zc:zc + 2].unsqueeze(2).to_broadcast([P, 2, 256]), op=ALU.mult),
                    reads=RP(ri) + [("zeta",)], writes=[("arena", "kz", tt)])
                if is_meta:
                    return None
                return lambda: transposes([rp[ri][:P, k * 128:(k + 1) * 128] for k in range(4)], RP(ri), P,
                                          A_rkT[:, :, tt * P:(tt + 1) * P], [("arena", "rkT", tt)], eng="dve")
            proj(P, ntl, RK + hp * 512, 512, ev_rk)
            for g in range(2):
                def ev_rv(tt, b, g=g):
                    K.op("act", lambda e: e.activation(out=A_rv[:P, tt, g * 512:(g + 1) * 512], in_=psb[b][:P, :],
                                                       func=AF.Copy), reads=[PS(b)], writes=[("arena", "rv", tt, g)])
                proj(P, ntl, RV + hp * 1024 + g * 512, 512, ev_rv)
            if not is_meta:
                for g in range(2):
                    def ev_rg(tt, b, g=g):
                        K.op("act", lambda e: e.activation(out=A_rg[:P, tt, g * 512:(g + 1) * 512], in_=psb[b][:P, :],
                                                           func=AF.Silu), reads=[PS(b)], writes=[("arena", "rg", tt, g)])
                    proj(P, ntl, RG + hp * 1024 + g * 512, 512, ev_rg)
            flush()
            for tt in range(ntl):
                cs = slice(tt * P, (tt + 1) * P)
                HL = range(2)
                b1s, b2s, b3s, ais, qis, cst = {}, {}, {}, {}, {}, {}
                if not is_meta:
                    for hl in HL:
                        b1s[hl] = K.ps_alloc()
                        K.pe([lambda pe, dc=dc: pe.matmul(psb[b1s[hl]][:, :128], lhsT=A_rkT[:, 2 * hl + dc, cs],
                                                          rhs=A_rqT[:, 2 * hl + dc, cs], start=(dc == 0), stop=(dc == 1))
                              for dc in range(2)],
                             reads=[("arena", "rkT", tt), ("arena", "rqT", tt)], writes=[PS(b1s[hl])])
                for hl in HL:
                    for dc in range(2):
                        b3 = K.ps_alloc()
                        b3s[(hl, dc)] = b3
                        K.pe([lambda pe: pe.matmul(psb[b3][:, :], lhsT=A_kz[:P, tt, hl * 256 + dc * 128:hl * 256 + (dc + 1) * 128],
                                                   rhs=A_rv[:P, tt, hl * 512:(hl + 1) * 512], start=True, stop=True)],
                             reads=[("arena", "kz", tt), ("arena", "rv", tt, hl)], writes=[PS(b3)])
                if not is_meta:
                    for hl in HL:
                        hh = 2 * hp + hl
                        ai = nxt("ATb", 2)
                        ais[hl] = ai
                        K.op("dve", lambda e: e.tensor_tensor(out=ATb[ai][:, :], in0=psb[b1s[hl]][:, :128], in1=DT[:, hh, :],
                                                              op=ALU.mult),
                             reads=[PS(b1s[hl]), ("DT",)], writes=[("ATb", ai)])
                        K.ps_free(b1s[hl])
                        qi = nxt("rqx", 2)
                        qis[hl] = qi
                        for dc in range(2):
                            K.op("pool", lambda e, dc=dc: e.tensor_tensor(out=rqx[qi][:, dc, :], in0=A_rqT[:, 2 * hl + dc, cs],
                                                                          in1=XI[:, hh, :], op=ALU.mult),
                                 reads=[("arena", "rqT", tt), ("XI",)], writes=[("rqx", qi, dc)])
                    for hl in HL:
                        hh = 2 * hp + hl
                        ai, qi = ais[hl], qis[hl]
                        b2 = K.ps_alloc()
                        b2s[hl] = b2
                        fns = [lambda pe: pe.matmul(psb[b2][:, :], lhsT=ATb[ai][:, :], rhs=A_rv[:, tt, hl * 512:(hl + 1) * 512],
                                                    start=True, stop=False)]
                        for dc in range(2):
                            fns.append(lambda pe, dc=dc: pe.matmul(psb[b2][:, :], lhsT=rqx[qi][:, dc, :], rhs=Sb[:, 2 * hh + dc, :],
                                                                   start=False, stop=(dc == 1)))
                        K.pe(fns, reads=[("ATb", ai), ("arena", "rv", tt, hl), ("rqx", qi, 0), ("rqx", qi, 1), ("Sb", hh, 0), ("Sb", hh, 1)],
                             writes=[PS(b2)])
                for hl in HL:
                    hh = 2 * hp + hl
                    for dc in range(2):
                        b3 = b3s[(hl, dc)]
                        if is_meta:
                            K.op("dve", lambda e: e.tensor_copy(S[:, 2 * hh + dc, :], psb[b3][:, :]),
                                 reads=[PS(b3)], writes=[("S", hh, dc)])
                        else:
                            K.op("dve", lambda e: e.scalar_tensor_tensor(
                                out=S[:, 2 * hh + dc, :], in0=S[:, 2 * hh + dc, :], scalar=float(GAM[hh] ** 128),
                                in1=psb[b3][:, :], op0=ALU.mult, op1=ALU.add),
                                reads=[PS(b3)], writes=[("S", hh, dc)])
                        K.ps_free(b3)
                        K.op("pool", lambda e: e.tensor_copy(Sb[:, 2 * hh + dc, :], S[:, 2 * hh + dc, :]),
                             reads=[("S", hh, dc)], writes=[("Sb", hh, dc)])
                if not is_meta:
                    for hl in HL:
                        b2 = b2s[hl]
                        c = 8 + nxt("gss", 8)
                        cst[hl] = c
                        K.op("act", lambda e: e.activation(out=junk[:, :512], in_=psb[b2][:, :], func=AF.Square,
                                                           accum_out=stat[:, c:c + 1]),
                             reads=[PS(b2)], writes=[("stat", c)])
                    for hl in HL:
                        c = cst[hl]
                        K.op("act", lambda e: e.activation(out=stat[:, c:c + 1], in_=stat[:, c:c + 1], func=AF.Sqrt, bias=epsb[:, :], scale=1.0 / 512),
                             reads=[("stat", c), ("epsb",)], writes=[("stat", c)])
                    for hl in HL:
                        c = cst[hl]
                        K.op("dve", lambda e: e.reciprocal(out=stat[:, c:c + 1], in_=stat[:, c:c + 1]),
                             reads=[("stat", c)], writes=[("stat", c)])
                    for hl in HL:
                        c, b2 = cst[hl], b2s[hl]
                        K.op("dve", lambda e: e.scalar_tensor_tensor(
                            out=A_rg[:, tt, hl * 512:(hl + 1) * 512], in0=psb[b2][:, :], scalar=stat[:, c:c + 1],
                            in1=A_rg[:, tt, hl * 512:(hl + 1) * 512], op0=ALU.mult, op1=ALU.mult),
                            reads=[PS(b2), ("stat", c)], writes=[("arena", "rg", tt, hl)])
                        K.ps_free(b2)
                    transposes([A_rg[:P, tt, k * 128:(k + 1) * 128] for k in range(8)],
                               [("arena", "rg", tt, 0), ("arena", "rg", tt, 1)], P,
                               A_ygT[:, hp * 8:(hp + 1) * 8, tt * P:(tt + 1) * P], [("arena", "ygT", hp, tt)], eng="act")
        if not is_meta:
            for g in range(2):
                def ev_gr(tt, b, g=g):
                    K.op("act", lambda e: e.activation(out=sg[:P, tt, g * 512:(g + 1) * 512], in_=psb[b][:P, :],
                                                       func=AF.Sigmoid), reads=[PS(b)], writes=[("sg", tt, g)])
                proj(P, ntl, GR + g * 512, 512, ev_gr)

            def ev_ro(tt, dh, b):
                K.op("dve", lambda e: e.tensor_tensor(out=merged[:P, tt, dh * 512:(dh + 1) * 512], in0=psb[b][:P, :],
                                                      in1=sg[:P, tt, dh * 512:(dh + 1) * 512], op=ALU.mult),
                     reads=[PS(b), ("sg", tt, dh)], writes=[("mg", tt, dh)])
            down_proj(lambda kc, tt: A_ygT[:, kc, tt * P:(tt + 1) * P],
                      lambda kc: [("arena", "ygT", kc // 8, tt) for tt in range(ntl)], 16, "wro", P, ntl, ev_ro)
        K.retire("arena")
        if not is_meta:
            for g in range(2):
                def ev_aq(tt, b, g=g):
                    ri = nxt("rp", 2)
                    rope_evac(P, tt, b, 512, 4, 128, 256, ri)
                    return lambda: transposes([rp[ri][:P, k * 128:(k + 1) * 128] for k in range(4)], RP(ri), P,
                                              A_aqT[:, g * 4:(g + 1) * 4, tt * P:(tt + 1) * P], [("arena", "aqT", tt, g)], eng="dve")
                proj(P, ntl, AQ + g * 512, 512, ev_aq)
        for g in range(2):
            def ev_ak(tt, b, g=g):
                ri = nxt("rp", 2)
                rope_evac(P, tt, b, 512, 4, 128, 256, ri)
                gt = 0 if is_meta else 1 + blk * NT + tt
                return lambda: transposes([rp[ri][:P, k * 128:(k + 1) * 128] for k in range(4)], RP(ri), P,
                                          akT[:, g * 4:(g + 1) * 4, pos0 + tt * P:pos0 + (tt + 1) * P], [("akT", gt, g)], eng="dve")
            proj(P, ntl, AK + g * 512, 512, ev_ak)
        for g in range(2):
            def ev_av(tt, b, g=g):
                gt = 0 if is_meta else 1 + blk * NT + tt
                K.op("act", lambda e: e.activation(out=av[:P, gt, g * 512:(g + 1) * 512], in_=psb[b][:P, :], func=AF.Copy),
                     reads=[PS(b)], writes=[("av", gt, g)])
            proj(P, ntl, AV + g * 512, 512, ev_av)
        if is_meta:
            flush()
            return

        def ev_iq(tt, b):
            ri = nxt("rp", 2)
            rope_evac(P, tt, b, 512, 8, 64, 384, ri)
            return lambda: transposes([rp[ri][:P, k * 128:(k + 1) * 128] for k in range(4)], RP(ri), P,
                                      A_iqT[:, :, tt * P:(tt + 1) * P], [("arena", "iqT", tt)], eng="dve")
        proj(P, ntl, IQ, 512, ev_iq)

        def ev_ik(tt, b):
            ri = nxt("rp", 2)
            xi_ = nxt("xs", 2)
            K.op("act", lambda e: e.activation(out=xs[xi_][:P, :72], in_=psb[b][:P, :72], func=AF.Copy),
                 reads=[PS(b)], writes=[("xs", xi_)])
            x1, x2 = xs[xi_][:P, 0:32], xs[xi_][:P, 32:64]
            cosb, sinb = tab[:P, tt, 384:416], tab[:P, tt, 416:448]
            rd = [("xs", xi_), ("tab",)]
            K.op("dve", lambda e: e.tensor_tensor(out=rt[0][:P, :32], in0=x1, in1=cosb, op=ALU.mult), reads=rd, writes=[("rt", 0)])
            K.op("dve", lambda e: e.tensor_tensor(out=rt[1][:P, :32], in0=x2, in1=sinb, op=ALU.mult), reads=rd, writes=[("rt", 1)])
            K.op("dve", lambda e: e.tensor_tensor(out=rt[2][:P, :32], in0=x2, in1=cosb, op=ALU.mult), reads=rd, writes=[("rt", 2)])
            K.op("dve", lambda e: e.tensor_tensor(out=rt[3][:P, :32], in0=x1, in1=sinb, op=ALU.mult), reads=rd, writes=[("rt", 3)])
            for rep in range(2):
                K.op("dve", lambda e, rep=rep: e.tensor_tensor(out=rp[ri][:P, rep * 64:rep * 64 + 32], in0=rt[0][:P, :32],
                                                               in1=rt[1][:P, :32], op=ALU.subtract),
                     reads=[("rt", 0), ("rt", 1)], writes=[("rp", ri, 0)])
                K.op("dve", lambda e, rep=rep: e.tensor_tensor(out=rp[ri][:P, rep * 64 + 32:rep * 64 + 64], in0=rt[2][:P, :32],
                                                               in1=rt[3][:P, :32], op=ALU.add),
                     reads=[("rt", 2), ("rt", 3)], writes=[("rp", ri, 1)])
            K.op("dve", lambda e: e.tensor_copy(iw[:P, tt, :], xs[xi_][:P, 64:72]), reads=[("xs", xi_)], writes=[("iw", tt)])
            gi = blk * NT + tt
            return lambda: transposes([rp[ri][:P, 0:128]], RP(ri), P, kiT[:, gi * 128:(gi + 1) * 128].unsqueeze(1),
                                      [("kiT", gi)], eng="dve")
        proj(P, ntl, IK, 72, ev_ik)
        flush()

        Lblk = 128 * (blk * NT + NT)
        for tt in range(ntl):
            gi = blk * NT + tt
            L = 128 * (gi + 1)
            cs = slice(tt * 128, (tt + 1) * 128)
            for sbk in range((L + 511) // 512):
                ncols = min(512, L - sbk * 512)
                kt = [("kiT", t_) for t_ in range(sbk * 4, sbk * 4 + ncols // 128)]
                for hh in range(8):
                    c, r = hh // 2, hh % 2
                    b = K.ps_alloc()
                    K.pe([lambda pe: pe.matmul(psb[b][:, :ncols], lhsT=A_iqT[r * 64:(r + 1) * 64, c, cs],
                                               rhs=kiT[r * 64:(r + 1) * 64, sbk * 512:sbk * 512 + ncols], start=True, stop=True)],
                         reads=[("arena", "iqT", tt)] + kt, writes=[PS(b)])
                    li = nxt("xs", 2)
                    K.op("act", lambda e: e.activation(out=xs[li][:, :ncols], in_=psb[b][:, :ncols], func=AF.Relu),
                         reads=[PS(b)], writes=[("xs", li)])
                    K.ps_free(b)
                    scv = A_sc[:, sbk * 512:sbk * 512 + ncols]
                    if hh == 0:
                        K.op("dve", lambda e: e.tensor_scalar(out=scv, in0=xs[li][:, :ncols], scalar1=iw[:, tt, 0:1], scalar2=None,
                                                              op0=ALU.mult),
                             reads=[("xs", li), ("iw", tt)], writes=[("arena", "sc", sbk)])
                    else:
                        K.op("dve", lambda e: e.scalar_tensor_tensor(out=scv, in0=xs[li][:, :ncols], scalar=iw[:, tt, hh:hh + 1],
                                                                     in1=scv, op0=ALU.mult, op1=ALU.add),
                             reads=[("xs", li), ("iw", tt)], writes=[("arena", "sc", sbk)])
            SCK = [("arena", "sc", s_) for s_ in range(4)]
            if L < Lblk:
                K.op("pool", lambda e: e.memset(A_sc[:, L:Lblk], NEGBIG), reads=[], writes=SCK)
            LO, W0, MID, CNT, TMP = 16 + tt * 8, 17 + tt * 8, 18 + tt * 8, 19 + tt * 8, 20 + tt * 8

            def col(c_):
                return stat[:, c_:c_ + 1]
            if gi >= 2:
                K.op("dve", lambda e: e.tensor_reduce(out=col(LO), in_=A_sc[:, :L], axis=AX.X, op=ALU.min),
                     reads=SCK, writes=[("stat", LO)])
                K.op("dve", lambda e: e.tensor_reduce(out=col(W0), in_=A_sc[:, :L], axis=AX.X, op=ALU.max),
                     reads=SCK, writes=[("stat", W0)])
                K.op("dve", lambda e: e.tensor_tensor(out=col(W0), in0=col(W0), in1=col(LO), op=ALU.subtract),
                     reads=[("stat", W0), ("stat", LO)], writes=[("stat", W0)])
            K.op("dve", lambda e: e.tensor_tensor(out=A_sc[:, L - 128:L], in0=A_sc[:, L - 128:L], in1=NEG[:, :], op=ALU.add),
                 reads=[("NEG",)] + SCK, writes=SCK)
            if gi >= 2:
                for it in range(NBIS):
                    f = 2.0 ** -(it + 1)
                    K.op("dve", lambda e: e.scalar_tensor_tensor(out=col(MID), in0=col(W0), scalar=f, in1=col(LO),
                                                                 op0=ALU.mult, op1=ALU.add),
                         reads=[("stat", W0), ("stat", LO)], writes=[("stat", MID)])
                    K.op("dve", lambda e: e.tensor_scalar(out=mk[:, :L], in0=A_sc[:, :L], scalar1=col(MID), scalar2=0.0,
                                                          op0=ALU.is_ge, op1=ALU.add, accum_out=col(CNT)),
                         reads=SCK + [("stat", MID)], writes=[("mk",), ("stat", CNT)])
                    K.op("dve", lambda e: e.scalar_tensor_tensor(out=col(TMP), in0=col(CNT), scalar=KTOP - 0.5, in1=col(W0),
                                                                 op0=ALU.is_ge, op1=ALU.mult),
                         reads=[("stat", CNT), ("stat", W0)], writes=[("stat", TMP)])
                    K.op("dve", lambda e: e.scalar_tensor_tensor(out=col(LO), in0=col(TMP), scalar=f, in1=col(LO),
                                                                 op0=ALU.mult, op1=ALU.add),
                         reads=[("stat", TMP), ("stat", LO)], writes=[("stat", LO)])
            else:
                K.op("dve", lambda e: e.memset(col(LO), -1.0e29), reads=[], writes=[("stat", LO)])
            K.op("dve", lambda e: e.tensor_scalar(out=mk[:, :Lblk], in0=A_sc[:, :Lblk], scalar1=col(LO), scalar2=None,
                                                  op0=ALU.is_ge),
                 reads=SCK + [("stat", LO)], writes=[("mk",)])
            nsb = Lblk // 128
            for s0 in range(0, nsb, 8):
                n = min(8, nsb - s0)
                transposes([mk[:, (s0 + k) * 128:(s0 + k + 1) * 128] for k in range(n)], [("mk",)], 128,
                           A_maskT[:, s0:s0 + n, cs], [("arena", "maskT", tt)], eng="act")

        nkt = blk * NT + NT
        MK = [("arena", "maskT", 0), ("arena", "maskT", 1)]
        for hh in range(8):
            g = hh // 4
            bo = K.ps_alloc()
            bs = K.ps_alloc()
            tiles = [(-1, NM, 0)] + [(si, 128, 128 if si == nkt - 1 else 0) for si in range(nkt)]

            def qk(idx):
                si, Ps, q0 = tiles[idx]
                kc0 = 0 if si < 0 else NM + si * 128
                gt = si + 1
                b = K.ps_alloc()
                K.pe([lambda pe: pe.matmul(psb[b][:Ps, q0:TB], lhsT=akT[:, hh, kc0:kc0 + Ps], rhs=A_aqT[:, hh, q0:TB],
                                           start=True, stop=True)],
                     reads=[("akT", gt, g), ("arena", "aqT", 0, g), ("arena", "aqT", 1, g)], writes=[PS(b)])
                pi = nxt("pT", 3)
                K.op("act", lambda e: e.activation(out=pT[pi][:Ps, q0:TB], in_=psb[b][:Ps, q0:TB], func=AF.Exp, scale=ATT_SCALE),
                     reads=[PS(b)], writes=[("pT", pi)])
                K.ps_free(b)
                if si >= 0:
                    K.op("pool", lambda e: e.tensor_tensor(out=pT[pi][:Ps, q0:TB], in0=pT[pi][:Ps, q0:TB],
                                                           in1=A_maskT[:Ps, si, q0:TB], op=ALU.mult),
                         reads=[("pT", pi)] + MK, writes=[("pT", pi)])
                return pi

            def pv(idx, pi):
                si, Ps, q0 = tiles[idx]
                gt = si + 1
                first, last = idx == 0, idx == len(tiles) - 1
                K.pe([lambda pe: pe.matmul(psb[bo][:, q0:TB], lhsT=av[:Ps, gt, hh * 128:(hh + 1) * 128], rhs=pT[pi][:Ps, q0:TB],
                                           start=first, stop=last),
                      lambda pe: pe.matmul(psb[bs][:, q0:TB], lhsT=ones[:Ps, :], rhs=pT[pi][:Ps, q0:TB],
                                           start=first, stop=last)],
                     reads=[("av", gt, g), ("pT", pi), ("ones",)], writes=[PS(bo), PS(bs)])

            LOOK = 2
            pis = {}
            for idx in range(min(LOOK, len(tiles))):
                pis[idx] = qk(idx)
            for idx in range(len(tiles)):
                if idx + LOOK < len(tiles):
                    pis[idx + LOOK] = qk(idx + LOOK)
                pv(idx, pis.pop(idx))
            ri_ = nxt("rs", 2)
            K.op("dve", lambda e: e.reciprocal(out=rs[ri_][:, :], in_=psb[bs][:, :TB]), reads=[PS(bs)], writes=[("rs", ri_)])
            K.op("dve", lambda e: e.tensor_tensor(out=A_attnT[:, hh, :], in0=psb[bo][:, :TB], in1=rs[ri_][:, :], op=ALU.mult),
                 reads=[PS(bo), ("rs", ri_)], writes=[("arena", "attnT", hh)])
            K.ps_free(bo)
            K.ps_free(bs)
        for g in range(2):
            def ev_ga(tt, b, g=g):
                K.op("act", lambda e: e.activation(out=sg[:P, tt, g * 512:(g + 1) * 512], in_=psb[b][:P, :], func=AF.Sigmoid),
                     reads=[PS(b)], writes=[("sg", tt, g)])
            proj(P, ntl, GA + g * 512, 512, ev_ga)

        def ev_ao(tt, dh, b):
            sl = slice(dh * 512, (dh + 1) * 512)
            xi_ = nxt("xs", 2)
            K.op("dve", lambda e: e.tensor_tensor(out=xs[xi_][:P, :], in0=psb[b][:P, :], in1=sg[:P, tt, sl], op=ALU.mult),
                 reads=[PS(b), ("sg", tt, dh)], writes=[("xs", xi_)])
            K.op("pool", lambda e: e.tensor_tensor(out=merged[:P, tt, sl], in0=merged[:P, tt, sl], in1=xs[xi_][:P, :], op=ALU.add),
                 reads=[("xs", xi_)], writes=[("mg", tt, dh)])
        down_proj(lambda kc, tt: A_attnT[:, kc, tt * P:(tt + 1) * P], lambda kc: [("arena", "attnT", kc)], 8, "wao", P, ntl, ev_ao)
        for tt in range(ntl):
            transposes([merged[:P, tt, k * 128:(k + 1) * 128] for k in range(8)], [("mg", tt, 0), ("mg", tt, 1)], P,
                       uT[:, :, tt * P:(tt + 1) * P], [("uT", tt)], eng="act")

        def ev_mo(tt, dh, b):
            sl = slice(dh * 512, (dh + 1) * 512)
            K.op("dve", lambda e: e.tensor_tensor(out=h[:P, tt, sl], in0=psb[b][:P, :], in1=h[:P, tt, sl], op=ALU.add),
                 reads=[PS(b)], writes=[("h", tt)])
        down_proj(lambda kc, tt: uT[:, kc, tt * P:(tt + 1) * P], lambda kc: [("uT", tt) for tt in range(ntl)], 8, "wmo", P, ntl, ev_mo)

    K.dma("sp", h[:NM, 0, :], dram["meta"], writes=[("h", 0)], slot=("x", 0))
    ffn(NM, 1, 0, "w1gu", "w1d")
    mixer(NM, 1, -1)
    for blk in range(nblk):
        for tt in range(NT):
            r0 = blk * TB + tt * 128
            K.dma("sp", h[:, tt, :], dram["x"][r0:r0 + 128, :], writes=[("h", tt)], slot=("x", tt))
        ffn(128, NT, 0, "w1gu", "w1d")
        if stages >= 2:
            mixer(128, NT, blk)
        if stages >= 3:
            ffn(128, NT, 2, "w2gu", "w2d")
        if stages >= 4:
            K.dma("sp", gbc[:, :], dram["gains"][3 * 128:4 * 128, :], writes=[("gbc",)], slot=("gbc",))
        for tt in range(NT):
            r0 = blk * TB + tt * 128
            if stages >= 4:
                c = tt
                hs = h[:, tt, :]
                K.op("act", lambda e: e.activation(out=junk[:, :], in_=hs, func=AF.Square, accum_out=stat[:, c:c + 1]),
                     reads=[("h", tt)], writes=[("stat", c)])
                K.op("act", lambda e: e.activation(out=stat[:, c:c + 1], in_=stat[:, c:c + 1], func=AF.Sqrt, bias=epsb[:, :], scale=1.0 / D),
                     reads=[("stat", c), ("epsb",)], writes=[("stat", c)])
                K.op("dve", lambda e: e.reciprocal(out=stat[:, c:c + 1], in_=stat[:, c:c + 1]),
                     reads=[("stat", c)], writes=[("stat", c)])
                K.op("dve", lambda e: e.scalar_tensor_tensor(out=hs, in0=hs, scalar=stat[:, c:c + 1], in1=gbc[:, :],
                                                             op0=ALU.mult, op1=ALU.mult),
                     reads=[("stat", c), ("gbc",)], writes=[("h", tt)])
            K.dma("pool", dram["y"][r0:r0 + 128, :], h[:, tt, :], reads=[("h", tt)], writes=[], slot=("y", tt))
    sp = K.E["sp"]
    for tt in range(NT):
        s = K.dsem[("y", tt)]
        sp.wait(s[0], s[1])


_CACHE = {}


def _prep_inputs(inputs):
    f = lambda a: np.ascontiguousarray(np.asarray(a, dtype=np.float32))
    shared = {
        "meta": f(inputs["meta_tokens"]),
        "gains": np.ascontiguousarray(np.concatenate([
            np.broadcast_to(f(inputs["ffn1_norm"]).reshape(1, D), (128, D)),
            np.broadcast_to(f(inputs["mix_norm"]).reshape(1, D), (128, D)),
            np.broadcast_to(f(inputs["ffn2_norm"]).reshape(1, D), (128, D)),
            np.broadcast_to(f(inputs["final_norm"]).reshape(1, D), (128, D))], axis=0)),
        "w1g": f(inputs["ffn1_w_gate"][0]), "w1u": f(inputs["ffn1_w_up"][0]), "w1d": f(inputs["ffn1_w_down"][0]),
        "win": f(inputs["w_in"][0]), "wro": f(inputs["w_ret_out"][0]), "wao": f(inputs["w_att_out"][0]),
        "wmo": f(inputs["w_mix_out"][0]),
        "w2g": f(inputs["ffn2_w_gate"][0]), "w2u": f(inputs["ffn2_w_up"][0]), "w2d": f(inputs["ffn2_w_down"][0]),
    }
    shared.update(_consts())
    return shared


def kernel(**inputs):
    x = np.asarray(inputs["x"], dtype=np.float32)
    B = x.shape[0]
    shared = _prep_inputs(inputs)
    if "nc" not in _CACHE:
        _CACHE["nc"] = build_program()
    nc = _CACHE["nc"]
    in_maps = []
    for b in range(B):
        m = dict(shared)
        m["x"] = np.ascontiguousarray(x[b])
        in_maps.append(m)
    res = run_bass_kernel_spmd(nc, in_maps, core_ids=list(range(B)))
    return np.stack([np.asarray(r["y"], dtype=np.float32) for r in res.results], axis=0)
```

```python
import contextlib
import numpy as np
import concourse.bass as bass
import concourse.mybir as mybir
from concourse.bass_utils import run_bass_kernel_spmd

F32 = mybir.dt.float32
BF16 = mybir.dt.bfloat16
AF = mybir.ActivationFunctionType
ALU = mybir.AluOpType
AX = mybir.AxisListType

D = 1024
DFF = 2816
NM = 16
SEQ = 2048
T = NM + SEQ
DIN = 11848
TB = 256
NT = 2
NBLK = SEQ // TB
EPS = 1e-6
KTOP = 256
NBIS = 18
GAM = [1.0 - 2.0 ** (-5.0 - h) for h in range(4)]
RQ, RK, RV, RG, AQ, AK, AV, IQ, IK, IW, GR, GA = 0, 1024, 2048, 4096, 6144, 7168, 8192, 9216, 9728, 9792, 9800, 10824
ATT_SCALE = 128.0 ** -0.5
NEGBIG = -1.0e30

DEBUG = {}


class Eng:
    def __init__(self, K, name, raw):
        self.K, self.name, self.raw = K, name, raw
        self.sem = None
        self.cnt = 0
        self.waited = {}

    def rotate(self):
        if self.sem is None or self.cnt >= 3000:
            self.sem = self.K.new_sem(self.name)
            self.cnt = 0

    def wait(self, sem, val):
        if self.waited.get(id(sem), 0) >= val:
            return
        if sem is self.sem and self.name == "pe":
            return
        self.raw.wait_ge(sem, val)
        self.waited[id(sem)] = val


class Ctx:
    def __init__(self, nc, es):
        self.nc, self.es = nc, es
        self.nsem = 0
        self.keys = {}
        self.base = {}
        self.E = {}
        for n, raw in (("pe", nc.tensor), ("act", nc.scalar), ("dve", nc.vector), ("pool", nc.gpsimd), ("sp", nc.sync)):
            self.E[n] = Eng(self, n, raw)
        self.dsem = {}
        self.ps_free_list = []
        self.ntmp = 0

    def new_sem(self, name):
        self.nsem += 1
        return self.es.enter_context(self.nc.semaphore("s%d_%s" % (self.nsem, name)))

    def sb(self, name, shape, dtype):
        return self.es.enter_context(self.nc.sbuf_tensor(name, shape, dtype))

    def _get(self, k):
        st = self.keys.get(k)
        if st is None:
            b = self.base.get(k[0])
            st = [None, dict(b) if b else {}]
            self.keys[k] = st
        return st

    def retire(self, slot):
        agg = dict(self.base.get(slot, {}))

        def add(t):
            if t is None:
                return
            cur = agg.get(id(t[0]))
            if cur is None or cur[1] < t[1]:
                agg[id(t[0])] = t

        for k in [k for k in self.keys if k[0] == slot]:
            st = self.keys.pop(k)
            add(st[0])
            for t in st[1].values():
                add(t)
        self.base[slot] = agg

    def _deps(self, eng, reads, writes):
        need = {}

        def add(t):
            if t is None:
                return
            cur = need.get(id(t[0]))
            if cur is None or cur[1] < t[1]:
                need[id(t[0])] = t

        for k in reads:
            add(self._get(k)[0])
        for k in writes:
            st = self._get(k)
            add(st[0])
            for t in st[1].values():
                add(t)
        for sem, val in need.values():
            eng.wait(sem, val)

    def _commit(self, t, reads, writes):
        for k in reads:
            if k in writes:
                continue
            st = self._get(k)
            cur = st[1].get(id(t[0]))
            if cur is None or cur[1] < t[1]:
                st[1][id(t[0])] = t
        for k in writes:
            st = self._get(k)
            st[0] = t
            st[1] = {}

    def op(self, en, fn, reads=(), writes=()):
        eng = self.E[en]
        eng.rotate()
        self._deps(eng, reads, writes)
        ins = fn(eng.raw)
        ins.then_inc(eng.sem, 1)
        eng.cnt += 1
        t = (eng.sem, eng.cnt)
        self._commit(t, reads, writes)
        return t

    def pe(self, fns, reads=(), writes=()):
        eng = self.E["pe"]
        eng.rotate()
        self._deps(eng, reads, writes)
        ins = None
        for fn in fns:
            ins = fn(eng.raw)
        ins.then_inc(eng.sem, 1)
        eng.cnt += 1
        t = (eng.sem, eng.cnt)
        self._commit(t, reads, writes)
        return t

    def dma(self, q, out, in_, reads=(), writes=(), slot=None, serial=True):
        eng = self.E[q]
        self._deps(eng, reads, writes)
        s = self.dsem.get(slot)
        if s is None:
            s = [self.new_sem("d"), 0]
            self.dsem[slot] = s
        if serial and s[1] > 0:
            eng.wait(s[0], s[1])
        ins = eng.raw.dma_start(out=out, in_=in_)
        ins.then_inc(s[0], 16)
        s[1] += 16
        t = (s[0], s[1])
        self._commit(t, reads, writes)
        return t

    def ps_alloc(self):
        assert self.ps_free_list, "out of PSUM banks"
        return self.ps_free_list.pop(0)

    def ps_free(self, b):
        self.ps_free_list.append(b)


def _consts():
    pos = np.arange(T, dtype=np.float32)

    def tab(d):
        inv = (np.float32(10000.0) ** (-np.arange(0, d, 2, dtype=np.float32) / np.float32(d))).astype(np.float32)
        ang = (pos[:, None] * inv[None, :]).astype(np.float32)
        return np.cos(ang).astype(np.float32), np.sin(ang).astype(np.float32)

    c256, s256 = tab(256)
    c128, s128 = tab(128)
    c64, s64 = tab(64)
    rope = np.concatenate([c256, s256, c128, s128, c64, s64], axis=1).astype(np.float32)
    i = np.arange(128)
    dt = np.zeros((128, 4, 128), np.float64)
    xi = np.zeros((128, 4, 128), np.float64)
    zeta = np.zeros((128, 8), np.float64)
    for h in range(4):
        lg = np.log1p(-(2.0 ** (-5.0 - h)))
        diff = i[None, :] - i[:, None]
        dt[:, h, :] = np.where(diff >= 0, np.exp(lg * np.maximum(diff, 0)), 0.0) / 16.0
        xi[:, h, :] = np.exp(lg * (i[None, :] + 1.0))
        zeta[:, h] = np.exp(lg * (127.0 - i)) / 16.0
        zeta[:NM, 4 + h] = np.exp(lg * (NM - 1.0 - np.arange(NM))) / 16.0
    neg = np.where(i[None, :] <= i[:, None], 0.0, NEGBIG)
    return {
        "c_rope": rope,
        "c_dt": dt.astype(np.float32).reshape(128, 512),
        "c_xi": xi.astype(np.float32).reshape(128, 512),
        "c_zeta": zeta.astype(np.float32),
        "c_neg": neg.astype(np.float32),
        "c_ident": np.eye(128, dtype=np.float32),
        "c_ones": np.ones((128, 128), np.float32),
    }


WSPECS = [
    ("w1g", D, DFF), ("w1u", D, DFF), ("w1d", DFF, D), ("win", D, DIN), ("wro", 2048, D),
    ("wao", D, D), ("wmo", D, D), ("w2g", D, DFF), ("w2u", D, DFF), ("w2d", DFF, D),
]


WDIMS = {n: (r, c) for n, r, c in WSPECS}
WIN_GROUPS = [(c0, 512) for c0 in range(0, 9728, 512)] + [(IK, 72), (GR, 512), (GR + 512, 512), (GA, 512), (GA + 512, 512)]
WIN_SLAB = {c0: i for i, (c0, w) in enumerate(WIN_GROUPS)}


SCRATCH = ["w1gu", "w1d", "win", "wro", "wao", "wmo", "w2gu", "w2d"]


def _slabs(n):
    if n in ("w1gu", "w2gu"):
        return [("gu", c0, 256) for c0 in range(0, DFF, 256)]
    r, c = WDIMS[n]
    if n == "win":
        return [("c", c0, w) for c0, w in WIN_GROUPS]
    if c == DFF:
        return [("c", c0, min(512, DFF - c0)) for c0 in range(0, DFF, 512)]
    return [("r", r0, min(512, r - r0)) for r0 in range(0, r, 512)]


def _slab_plan(nblk, stages):
    def ffn_(gu, d):
        return [(gu, i) for i in range(len(_slabs(gu)))] + [(d, i) for i in range(len(_slabs(d)))]

    def mix_(meta):
        out = []
        for hp in range(2):
            if not meta:
                out.append(("win", WIN_SLAB[RQ + hp * 512]))
            out.append(("win", WIN_SLAB[RK + hp * 512]))
            out += [("win", WIN_SLAB[RV + hp * 1024 + g * 512]) for g in range(2)]
            if not meta:
                out += [("win", WIN_SLAB[RG + hp * 1024 + g * 512]) for g in range(2)]
        if not meta:
            out += [("win", WIN_SLAB[GR + g * 512]) for g in range(2)]
            out += [("wro", i) for i in range(4)]
            out += [("win", WIN_SLAB[AQ + g * 512]) for g in range(2)]
        out += [("win", WIN_SLAB[AK + g * 512]) for g in range(2)]
        out += [("win", WIN_SLAB[AV + g * 512]) for g in range(2)]
        if not meta:
            out += [("win", WIN_SLAB[IQ]), ("win", WIN_SLAB[IK])]
            out += [("win", WIN_SLAB[GA + g * 512]) for g in range(2)]
            out += [("wao", i) for i in range(2)] + [("wmo", i) for i in range(2)]
        return out

    plan = ffn_("w1gu", "w1d") + mix_(True)
    for b in range(nblk):
        plan += ffn_("w1gu", "w1d")
        if stages >= 2:
            plan += mix_(False)
        if stages >= 3:
            plan += ffn_("w2gu", "w2d")
    return plan


def build_program(nblk=NBLK, stages=99):
    nc = bass.Bass("TRN2", target_bir_lowering=False)
    dram = {}
    dram["x"] = nc.dram_tensor("x", [SEQ, D], F32, kind="ExternalInput").ap()
    dram["meta"] = nc.dram_tensor("meta", [NM, D], F32, kind="ExternalInput").ap()
    dram["gains"] = nc.dram_tensor("gains", [4 * 128, D], F32, kind="ExternalInput").ap()
    for n, r, c in WSPECS:
        dram[n] = nc.dram_tensor(n, [r, c], F32, kind="ExternalInput").ap()
    for n in SCRATCH:
        dram[n + "b"] = nc.dram_tensor(n + "b", [len(_slabs(n)), 128, 4096], BF16, kind="Internal").ap()
    cshapes = {"c_rope": [T, 448], "c_dt": [128, 512], "c_xi": [128, 512], "c_zeta": [128, 8],
               "c_neg": [128, 128], "c_ident": [128, 128], "c_ones": [128, 128]}
    for n, s in cshapes.items():
        dram[n] = nc.dram_tensor(n, s, F32, kind="ExternalInput").ap()
    dram["y"] = nc.dram_tensor("y", [SEQ, D], F32, kind="ExternalOutput").ap()
    for n, s in DEBUG.items():
        dram[n] = nc.dram_tensor(n, s, F32, kind="ExternalOutput").ap()

    with contextlib.ExitStack() as es:
        K = Ctx(nc, es)
        _emit(K, dram, nblk, stages)
    return nc


def _emit(K, dram, nblk, stages):
    nc = K.nc
    sb = K.sb
    akT = sb("akT", [128, 8, T], BF16)
    av = sb("av", [128, 17, D], BF16)
    kiT = sb("kiT", [128, SEQ], BF16)
    S = sb("S", [128, 8, 512], F32)
    Sb = sb("Sb", [128, 8, 512], BF16)
    ident = sb("ident", [128, 128], BF16)
    ones = sb("ones", [128, 128], BF16)
    DT = sb("DT", [128, 4, 128], F32)
    XI = sb("XI", [128, 4, 128], F32)
    zeta = sb("zeta", [128, 8], F32)
    NEG = sb("NEG", [128, 128], F32)
    gbc = sb("gbc", [128, D], F32)
    h = sb("h", [128, NT, D], F32)
    uT = sb("uT", [128, 8, TB], BF16)
    utm = [sb("utm%d" % i, [128, D], BF16) for i in range(2)]
    arena = sb("arena", [128, 13312], BF16)
    merged = sb("merged", [128, NT, D], BF16)
    sg = sb("sg", [128, NT, D], BF16)
    tab = sb("tab", [128, NT, 448], F32)
    iw = sb("iw", [128, NT, 8], F32)
    NWS = 3
    wslot = [sb("wslot%d" % i, [128, 4096], BF16) for i in range(NWS)]
    xs = [sb("xs%d" % i, [128, 512], F32) for i in range(2)]
    rt_all = sb("rt_all", [128, 2048], F32)
    rt = [rt_all[:, i * 256:(i + 1) * 256] for i in range(8)]
    rp = [sb("rp%d" % i, [128, 512], BF16) for i in range(2)]
    sgt = [sb("sgt%d" % i, [128, TB], BF16) for i in range(2)]
    pT = [sb("pT%d" % i, [128, TB], BF16) for i in range(3)]
    rs = [sb("rs%d" % i, [128, TB], F32) for i in range(2)]
    junk = sb("junk", [128, D], BF16)
    mk = sb("mk", [128, SEQ], BF16)
    ATb = [sb("ATb%d" % i, [128, 128], BF16) for i in range(2)]
    rqx = [sb("rqx%d" % i, [128, 2, 128], BF16) for i in range(2)]
    stat = sb("stat", [128, 64], F32)
    epsb = sb("epsb", [128, 1], F32)
    psb = [K.es.enter_context(nc.psum_tensor("psb%d" % i, [128, 512], F32)) for i in range(8)]
    K.ps_free_list = list(range(8))

    def PS(b):
        return ("ps", b)

    rr = {}

    def nxt(name, n):
        v = rr.get(name, 0)
        rr[name] = (v + 1) % n
        return v

    A_actT = arena[:, 0:22 * TB].rearrange("p (a b) -> p a b", a=22)
    A_rqT = arena[:, 0:1024].rearrange("p (a b) -> p a b", a=4)
    A_rkT = arena[:, 1024:2048].rearrange("p (a b) -> p a b", a=4)
    A_kz = arena[:, 2048:3072].rearrange("p (a b) -> p a b", a=NT)
    A_rv = arena[:, 3072:5120].rearrange("p (a b) -> p a b", a=NT)
    A_rg = arena[:, 5120:7168].rearrange("p (a b) -> p a b", a=NT)
    A_ygT = arena[:, 7168:11264].rearrange("p (a b) -> p a b", a=16)
    A_aqT = arena[:, 0:2048].rearrange("p (a b) -> p a b", a=8)
    A_iqT = arena[:, 2048:3072].rearrange("p (a b) -> p a b", a=4)
    A_attnT = arena[:, 3072:5120].rearrange("p (a b) -> p a b", a=8)
    A_sc = arena[:, 5120:9216].bitcast(F32)
    A_maskT = arena[:, 9216:13312].rearrange("p (a b) -> p a b", a=16)

    K.op("dve", lambda e: e.memset(epsb[:, :], EPS), reads=[], writes=[("epsb",)])
    def cload(dst, src, key, cast):
        K.dma("pool" if cast else "sp", dst, src, writes=[key], slot=key)

    cload(ident[:, :], dram["c_ident"], ("ident",), True)
    cload(ones[:, :], dram["c_ones"], ("ones",), True)
    cload(DT[:, :, :].rearrange("p a b -> p (a b)"), dram["c_dt"], ("DT",), False)
    cload(XI[:, :, :].rearrange("p a b -> p (a b)"), dram["c_xi"], ("XI",), False)
    cload(zeta[:, :], dram["c_zeta"], ("zeta",), False)
    cload(NEG[:, :], dram["c_neg"], ("NEG",), False)

    for n in SCRATCH:
        for si, (kind, off, sz) in enumerate(_slabs(n)):
            if kind == "gu":
                dstv = dram[n + "b"][si, :, :].rearrange("p (k c) -> p k c", k=8)
                for j, src_n in enumerate((n[:2] + "g", n[:2] + "u")):
                    src = dram[src_n].rearrange("(k p) c -> p k c", p=128)[:, :, off:off + 256]
                    K.dma("pool", dstv[:, :, j * 256:(j + 1) * 256], src, writes=[], slot=("wb", n), serial=False)
                continue
            if kind == "c":
                dst = dram[n + "b"][si, :, 0:8 * sz].rearrange("p (k c) -> p k c", k=8)
                src = dram[n].rearrange("(k p) c -> p k c", p=128)[:, :, off:off + sz]
            else:
                nrc = sz // 128
                dst = dram[n + "b"][si, :, 0:nrc * 1024].rearrange("p (k c) -> p k c", k=nrc)
                src = dram[n][off:off + sz, :].rearrange("(k p) c -> p k c", p=128)
            K.dma("pool", dst, src, writes=[], slot=("wb", n), serial=False)
        sm = K.dsem[("wb", n)]
        K._commit((sm[0], sm[1]), [], [("wb", n)])

    plan = _slab_plan(nblk, stages)
    wst = {"issued": 0, "used": 0}

    def _issue_one():
        i = wst["issued"]
        if i >= len(plan):
            return
        n, si = plan[i]
        kind, off, sz = _slabs(n)[si]
        ncol = 4096 if kind == "gu" else (8 * sz if kind == "c" else (sz // 128) * 1024)
        sl = i % NWS
        K.dma("sp", wslot[sl][:, :ncol], dram[n + "b"][si, :, 0:ncol], reads=[("wb", n)], writes=[("ws", sl)], slot=("ws", sl))
        wst["issued"] += 1

    def wnext(n, si):
        i = wst["used"]
        assert plan[i] == (n, si), (i, plan[i], n, si)
        while wst["issued"] <= i:
            assert wst["issued"] < wst.get("done", 0) + NWS
            _issue_one()
        wst["used"] += 1
        return i % NWS

    def wdone(k=1):
        wst["done"] = wst.get("done", 0) + k
        while wst["issued"] < min(len(plan), wst["done"] + NWS):
            _issue_one()

    def wload(name, c0, w):
        if name == "win":
            si = WIN_SLAB[c0]
        else:
            si = c0 // 512
        return wnext(name, si)

    def wv(i, w):
        return wslot[i][:, 0:8 * w].rearrange("p (k c) -> p k c", k=8)

    def transposes(srcs, src_reads, P, dst, dst_writes, eng="act"):
        n = len(srcs)
        b = K.ps_alloc()
        pv = psb[b][:, :].bitcast(BF16).rearrange("p (a b) -> p a b", a=8)
        fns = []
        for k, s_ap in enumerate(srcs):
            fns.append(lambda pe, k=k, s_ap=s_ap: pe.transpose(pv[:, k, :P], s_ap, ident[:P, :P]))
        K.pe(fns, reads=list(src_reads) + [("ident",)], writes=[PS(b)])
        if eng == "act":
            K.op("act", lambda e: e.activation(out=dst, in_=pv[:, :n, :P], func=AF.Copy), reads=[PS(b)], writes=dst_writes)
        else:
            K.op("dve", lambda e: e.tensor_copy(dst, pv[:, :n, :P]), reads=[PS(b)], writes=dst_writes)
        K.ps_free(b)

    def norm_to_uT(P, ntl, grow):
        K.dma("sp", gbc[:, :], dram["gains"][grow * 128:(grow + 1) * 128, :], writes=[("gbc",)], slot=("gbc",))
        for tt in range(ntl):
            hs = h[:P, tt, :]
            c = tt
            K.op("act", lambda e: e.activation(out=junk[:P, :], in_=hs, func=AF.Square, accum_out=stat[:P, c:c + 1]),
                 reads=[("h", tt)], writes=[("stat", c)])
            K.op("act", lambda e: e.activation(out=stat[:P, c:c + 1], in_=stat[:P, c:c + 1], func=AF.Sqrt, bias=epsb[:P, :], scale=1.0 / D),
                 reads=[("stat", c), ("epsb",)], writes=[("stat", c)])
            K.op("dve", lambda e: e.reciprocal(out=stat[:P, c:c + 1], in_=stat[:P, c:c + 1]),
                 reads=[("stat", c)], writes=[("stat", c)])
            ui = nxt("utm", 2)
            K.op("dve", lambda e: e.scalar_tensor_tensor(out=utm[ui][:P, :], in0=hs, scalar=stat[:P, c:c + 1],
                                                         in1=gbc[:P, :], op0=ALU.mult, op1=ALU.mult),
                 reads=[("h", tt), ("stat", c), ("gbc",)], writes=[("utm", ui)])
            transposes([utm[ui][:P, k * 128:(k + 1) * 128] for k in range(8)], [("utm", ui)], P,
                       uT[:, :, tt * P:(tt + 1) * P], [("uT", tt)], eng="act")

    def down_proj(lhs_fn, lhs_reads_fn, nk, wname, P, ntl, evac):
        banks = [[K.ps_alloc() for dh in range(2)] for tt in range(ntl)]
        wi = None
        for kc in range(nk):
            if kc % 4 == 0:
                wi = wnext(wname, kc // 4)
            wvv = wslot[wi][:, :].rearrange("p (k c) -> p k c", k=4)
            fns = []
            for tt in range(ntl):
                for dh in range(2):
                    fns.append(lambda pe, tt=tt, dh=dh: pe.matmul(
                        psb[banks[tt][dh]][:P, :], lhsT=lhs_fn(kc, tt), rhs=wvv[:, kc % 4, dh * 512:(dh + 1) * 512],
                        start=(kc == 0), stop=(kc == nk - 1)))
            K.pe(fns, reads=[("ws", wi)] + lhs_reads_fn(kc),
                 writes=[PS(banks[tt][dh]) for tt in range(ntl) for dh in range(2)])
            if kc % 4 == 3 or kc == nk - 1:
                wdone(1)
        for tt in range(ntl):
            for dh in range(2):
                evac(tt, dh, banks[tt][dh])
                K.ps_free(banks[tt][dh])

    def ffn(P, ntl, grow, wgu, wdn):
        N = P * ntl
        norm_to_uT(P, ntl, grow)
        K.retire("arena")
        for j in range(DFF // 256):
            ig = wnext(wgu, j)
            gv = wslot[ig][:, :].rearrange("p (k c) -> p k c", k=8)
            for fl in range(2):
                fc = 2 * j + fl
                pg = K.ps_alloc()
                pu = K.ps_alloc()
                K.pe([lambda pe, k=k: pe.matmul(psb[pg][:, :N], lhsT=gv[:, k, fl * 128:(fl + 1) * 128],
                                                rhs=uT[:, k, :N], start=(k == 0), stop=(k == 7)) for k in range(8)],
                     reads=[("ws", ig)] + [("uT", tt) for tt in range(ntl)], writes=[PS(pg)])
                K.pe([lambda pe, k=k: pe.matmul(psb[pu][:, :N], lhsT=gv[:, k, 256 + fl * 128:256 + (fl + 1) * 128],
                                                rhs=uT[:, k, :N], start=(k == 0), stop=(k == 7)) for k in range(8)],
                     reads=[("ws", ig)] + [("uT", tt) for tt in range(ntl)], writes=[PS(pu)])
                si = nxt("sgt", 2)
                K.op("act", lambda e: e.activation(out=sgt[si][:, :N], in_=psb[pg][:, :N], func=AF.Silu),
                     reads=[PS(pg)], writes=[("sgt", si)])
                K.op("dve", lambda e: e.tensor_tensor(out=A_actT[:, fc, :N], in0=sgt[si][:, :N], in1=psb[pu][:, :N],
                                                      op=ALU.mult),
                     reads=[("sgt", si), PS(pu)], writes=[("arena", "actT", fc)])
                K.ps_free(pg)
                K.ps_free(pu)
            wdone(1)

        def evac(tt, dh, b):
            K.op("dve", lambda e: e.scalar_tensor_tensor(out=h[:P, tt, dh * 512:(dh + 1) * 512], in0=psb[b][:P, :],
                                                         scalar=0.5, in1=h[:P, tt, dh * 512:(dh + 1) * 512],
                                                         op0=ALU.mult, op1=ALU.add),
                 reads=[PS(b)], writes=[("h", tt)])

        down_proj(lambda kc, tt: A_actT[:, kc, tt * P:(tt + 1) * P], lambda kc: [("arena", "actT", kc)], 22, wdn, P, ntl, evac)

    deferred = []

    def flush():
        while deferred:
            deferred.pop(0)()

    def proj(P, ntl, c0, w, evac):
        wi = wload("win", c0, w)
        for tt in range(ntl):
            b = K.ps_alloc()
            K.pe([lambda pe, k=k: pe.matmul(psb[b][:P, :w], lhsT=uT[:, k, tt * P:(tt + 1) * P], rhs=wv(wi, w)[:, k, :w],
                                            start=(k == 0), stop=(k == 7)) for k in range(8)],
                 reads=[("ws", wi), ("uT", tt)], writes=[PS(b)])
            flush()
            post = evac(tt, b)
            if post is not None:
                deferred.append(post)
            K.ps_free(b)
        wdone(1)

    def rope_evac(P, tt, b, w, nh, d, tc0, dst_i):
        hd = d // 2
        xi_ = nxt("xs", 2)
        K.op("act", lambda e: e.activation(out=xs[xi_][:P, :w], in_=psb[b][:P, :w], func=AF.Copy),
             reads=[PS(b)], writes=[("xs", xi_)])
        x = xs[xi_][:P, :w].rearrange("p (h two f) -> p h two f", h=nh, two=2)
        o = rp[dst_i][:P, :w].rearrange("p (h two f) -> p h two f", h=nh, two=2)
        x1, x2 = x[:, :, 0, :], x[:, :, 1, :]
        cosb = tab[:P, tt, tc0:tc0 + hd].unsqueeze(1).to_broadcast([P, nh, hd])
        sinb = tab[:P, tt, tc0 + hd:tc0 + 2 * hd].unsqueeze(1).to_broadcast([P, nh, hd])
        r0 = 4 * nxt("rtset", 2)
        tv = [rt[r0 + i][:P, :nh * hd].rearrange("p (h f) -> p h f", h=nh) for i in range(4)]
        rd = [("xs", xi_), ("tab",)]
        K.op("dve", lambda e: e.tensor_tensor(out=tv[0], in0=x1, in1=cosb, op=ALU.mult), reads=rd, writes=[("rt", r0)])
        K.op("dve", lambda e: e.tensor_tensor(out=tv[1], in0=x2, in1=sinb, op=ALU.mult), reads=rd, writes=[("rt", r0 + 1)])
        K.op("pool", lambda e: e.tensor_tensor(out=tv[2], in0=x2, in1=cosb, op=ALU.mult), reads=rd, writes=[("rt", r0 + 2)])
        K.op("pool", lambda e: e.tensor_tensor(out=tv[3], in0=x1, in1=sinb, op=ALU.mult), reads=rd, writes=[("rt", r0 + 3)])
        K.op("dve", lambda e: e.tensor_tensor(out=o[:, :, 0, :], in0=tv[0], in1=tv[1], op=ALU.subtract),
             reads=[("rt", r0), ("rt", r0 + 1)], writes=[("rp", dst_i, 0)])
        K.op("pool", lambda e: e.tensor_tensor(out=o[:, :, 1, :], in0=tv[2], in1=tv[3], op=ALU.add),
             reads=[("rt", r0 + 2), ("rt", r0 + 3)], writes=[("rp", dst_i, 1)])

    def RP(i):
        return [("rp", i, 0), ("rp", i, 1)]

    def mixer(P, ntl, blk):
        is_meta = blk < 0
        pos0 = 0 if is_meta else NM + blk * TB
        K.dma("sp", tab[:P, 0:ntl, :], dram["c_rope"][pos0:pos0 + P * ntl, :].rearrange("(a p) c -> p a c", p=P),
              writes=[("tab",)], slot=("tab",))
        norm_to_uT(P, ntl, 1)
        K.retire("arena")
        for hp in range(2):
            if not is_meta:
                def ev_rq(tt, b):
                    ri = nxt("rp", 2)
                    rope_evac(P, tt, b, 512, 2, 256, 0, ri)
                    return lambda: transposes([rp[ri][:P, k * 128:(k + 1) * 128] for k in range(4)], RP(ri), P,
                                              A_rqT[:, :, tt * P:(tt + 1) * P], [("arena", "rqT", tt)], eng="dve")
                proj(P, ntl, RQ + hp * 512, 512, ev_rq)

            def ev_rk(tt, b):
                ri = nxt("rp", 2)
                rope_evac(P, tt, b, 512, 2, 256, 0, ri)
                zc = (4 if is_meta else 0) + 2 * hp
                K.op("pool", lambda e: e.tensor_tensor(
                    out=A_kz[:P, tt, :].rearrange("p (h f) -> p h f", h=2),
                    in0=rp[ri][:P, :].rearrange("p (h f) -> p h f", h=2),
                    in1=zeta[:P, zc:zc + 2].unsqueeze(2).to_broadcast([P, 2, 256]), op=ALU.mult),
                    reads=RP(ri) + [("zeta",)], writes=[("arena", "kz", tt)])
                if is_meta:
                    return None
                return lambda: transposes([rp[ri][:P, k * 128:(k + 1) * 128] for k in range(4)], RP(ri), P,
                                          A_rkT[:, :, tt * P:(tt + 1) * P], [("arena", "rkT", tt)], eng="dve")
            proj(P, ntl, RK + hp * 512, 512, ev_rk)
            for g in range(2):
                def ev_rv(tt, b, g=g):
                    K.op("act", lambda e: e.activation(out=A_rv[:P, tt, g * 512:(g + 1) * 512], in_=psb[b][:P, :],
                                                       func=AF.Copy), reads=[PS(b)], writes=[("arena", "rv", tt, g)])
                proj(P, ntl, RV + hp * 1024 + g * 512, 512, ev_rv)
            if not is_meta:
                for g in range(2):
                    def ev_rg(tt, b, g=g):
                        K.op("act", lambda e: e.activation(out=A_rg[:P, tt, g * 512:(g + 1) * 512], in_=psb[b][:P, :],
                                                           func=AF.Silu), reads=[PS(b)], writes=[("arena", "rg", tt, g)])
                    proj(P, ntl, RG + hp * 1024 + g * 512, 512, ev_rg)
            flush()
            for tt in range(ntl):
                cs = slice(tt * P, (tt + 1) * P)
                HL = range(2)
                b1s, b2s, b3s, ais, qis, cst = {}, {}, {}, {}, {}, {}
                if not is_meta:
                    for hl in HL:
                        b1s[hl] = K.ps_alloc()
                        K.pe([lambda pe, dc=dc: pe.matmul(psb[b1s[hl]][:, :128], lhsT=A_rkT[:, 2 * hl + dc, cs],
                                                          rhs=A_rqT[:, 2 * hl + dc, cs], start=(dc == 0), stop=(dc == 1))
                              for dc in range(2)],
                             reads=[("arena", "rkT", tt), ("arena", "rqT", tt)], writes=[PS(b1s[hl])])
                for hl in HL:
                    for dc in range(2):
                        b3 = K.ps_alloc()
                        b3s[(hl, dc)] = b3
                        K.pe([lambda pe: pe.matmul(psb[b3][:, :], lhsT=A_kz[:P, tt, hl * 256 + dc * 128:hl * 256 + (dc + 1) * 128],
                                                   rhs=A_rv[:P, tt, hl * 512:(hl + 1) * 512], start=True, stop=True)],
                             reads=[("arena", "kz", tt), ("arena", "rv", tt, hl)], writes=[PS(b3)])
                if not is_meta:
                    for hl in HL:
                        hh = 2 * hp + hl
                        ai = nxt("ATb", 2)
                        ais[hl] = ai
                        K.op("dve", lambda e: e.tensor_tensor(out=ATb[ai][:, :], in0=psb[b1s[hl]][:, :128], in1=DT[:, hh, :],
                                                              op=ALU.mult),
                             reads=[PS(b1s[hl]), ("DT",)], writes=[("ATb", ai)])
                        K.ps_free(b1s[hl])
                        qi = nxt("rqx", 2)
                        qis[hl] = qi
                        for dc in range(2):
                            K.op("pool", lambda e, dc=dc: e.tensor_tensor(out=rqx[qi][:, dc, :], in0=A_rqT[:, 2 * hl + dc, cs],
                                                                          in1=XI[:, hh, :], op=ALU.mult),
                                 reads=[("arena", "rqT", tt), ("XI",)], writes=[("rqx", qi, dc)])
                    for hl in HL:
                        hh = 2 * hp + hl
                        ai, qi = ais[hl], qis[hl]
                        b2 = K.ps_alloc()
                        b2s[hl] = b2
                        fns = [lambda pe: pe.matmul(psb[b2][:, :], lhsT=ATb[ai][:, :], rhs=A_rv[:, tt, hl * 512:(hl + 1) * 512],
                                                    start=True, stop=False)]
                        for dc in range(2):
                            fns.append(lambda pe, dc=dc: pe.matmul(psb[b2][:, :], lhsT=rqx[qi][:, dc, :], rhs=Sb[:, 2 * hh + dc, :],
                                                                   start=False, stop=(dc == 1)))
                        K.pe(fns, reads=[("ATb", ai), ("arena", "rv", tt, hl), ("rqx", qi, 0), ("rqx", qi, 1), ("Sb", hh, 0), ("Sb", hh, 1)],
                             writes=[PS(b2)])
                for hl in HL:
                    hh = 2 * hp + hl
                    for dc in range(2):
                        b3 = b3s[(hl, dc)]
                        if is_meta:
                            K.op("dve", lambda e: e.tensor_copy(S[:, 2 * hh + dc, :], psb[b3][:, :]),
                                 reads=[PS(b3)], writes=[("S", hh, dc)])
                        else:
                            K.op("dve", lambda e: e.scalar_tensor_tensor(
                                out=S[:, 2 * hh + dc, :], in0=S[:, 2 * hh + dc, :], scalar=float(GAM[hh] ** 128),
                                in1=psb[b3][:, :], op0=ALU.mult, op1=ALU.add),
                                reads=[PS(b3)], writes=[("S", hh, dc)])
                        K.ps_free(b3)
                        K.op("pool", lambda e: e.tensor_copy(Sb[:, 2 * hh + dc, :], S[:, 2 * hh + dc, :]),
                             reads=[("S", hh, dc)], writes=[("Sb", hh, dc)])
                if not is_meta:
                    for hl in HL:
                        b2 = b2s[hl]
                        c = 8 + nxt("gss", 8)
                        cst[hl] = c
                        K.op("act", lambda e: e.activation(out=junk[:, :512], in_=psb[b2][:, :], func=AF.Square,
                                                           accum_out=stat[:, c:c + 1]),
                             reads=[PS(b2)], writes=[("stat", c)])
                    for hl in HL:
                        c = cst[hl]
                        K.op("act", lambda e: e.activation(out=stat[:, c:c + 1], in_=stat[:, c:c + 1], func=AF.Sqrt, bias=epsb[:, :], scale=1.0 / 512),
                             reads=[("stat", c), ("epsb",)], writes=[("stat", c)])
                    for hl in HL:
                        c = cst[hl]
                        K.op("dve", lambda e: e.reciprocal(out=stat[:, c:c + 1], in_=stat[:, c:c + 1]),
                             reads=[("stat", c)], writes=[("stat", c)])
                    for hl in HL:
                        c, b2 = cst[hl], b2s[hl]
                        K.op("dve", lambda e: e.scalar_tensor_tensor(
                            out=A_rg[:, tt, hl * 512:(hl + 1) * 512], in0=psb[b2][:, :], scalar=stat[:, c:c + 1],
                            in1=A_rg[:, tt, hl * 512:(hl + 1) * 512], op0=ALU.mult, op1=ALU.mult),
                            reads=[PS(b2), ("stat", c)], writes=[("arena", "rg", tt, hl)])
                        K.ps_free(b2)
                    transposes([A_rg[:P, tt, k * 128:(k + 1) * 128] for k in range(8)],
                               [("arena", "rg", tt, 0), ("arena", "rg", tt, 1)], P,
                               A_ygT[:, hp * 8:(hp + 1) * 8, tt * P:(tt + 1) * P], [("arena", "ygT", hp, tt)], eng="act")
        if not is_meta:
            for g in range(2):
                def ev_gr(tt, b, g=g):
                    K.op("act", lambda e: e.activation(out=sg[:P, tt, g * 512:(g + 1) * 512], in_=psb[b][:P, :],
                                                       func=AF.Sigmoid), reads=[PS(b)], writes=[("sg", tt, g)])
                proj(P, ntl, GR + g * 512, 512, ev_gr)

            def ev_ro(tt, dh, b):
                K.op("dve", lambda e: e.tensor_tensor(out=merged[:P, tt, dh * 512:(dh + 1) * 512], in0=psb[b][:P, :],
                                                      in1=sg[:P, tt, dh * 512:(dh + 1) * 512], op=ALU.mult),
                     reads=[PS(b), ("sg", tt, dh)], writes=[("mg", tt, dh)])
            down_proj(lambda kc, tt: A_ygT[:, kc, tt * P:(tt + 1) * P],
                      lambda kc: [("arena", "ygT", kc // 8, tt) for tt in range(ntl)], 16, "wro", P, ntl, ev_ro)
        K.retire("arena")
        if not is_meta:
            for g in range(2):
                def ev_aq(tt, b, g=g):
                    ri = nxt("rp", 2)
                    rope_evac(P, tt, b, 512, 4, 128, 256, ri)
                    return lambda: transposes([rp[ri][:P, k * 128:(k + 1) * 128] for k in range(4)], RP(ri), P,
                                              A_aqT[:, g * 4:(g + 1) * 4, tt * P:(tt + 1) * P], [("arena", "aqT", tt, g)], eng="dve")
                proj(P, ntl, AQ + g * 512, 512, ev_aq)
        for g in range(2):
            def ev_ak(tt, b, g=g):
                ri = nxt("rp", 2)
                rope_evac(P, tt, b, 512, 4, 128, 256, ri)
                gt = 0 if is_meta else 1 + blk * NT + tt
                return lambda: transposes([rp[ri][:P, k * 128:(k + 1) * 128] for k in range(4)], RP(ri), P,
                                          akT[:, g * 4:(g + 1) * 4, pos0 + tt * P:pos0 + (tt + 1) * P], [("akT", gt, g)], eng="dve")
            proj(P, ntl, AK + g * 512, 512, ev_ak)
        for g in range(2):
            def ev_av(tt, b, g=g):
                gt = 0 if is_meta else 1 + blk * NT + tt
                K.op("act", lambda e: e.activation(out=av[:P, gt, g * 512:(g + 1) * 512], in_=psb[b][:P, :], func=AF.Copy),
                     reads=[PS(b)], writes=[("av", gt, g)])
            proj(P, ntl, AV + g * 512, 512, ev_av)
        if is_meta:
            flush()
            return

        def ev_iq(tt, b):
            ri = nxt("rp", 2)
            rope_evac(P, tt, b, 512, 8, 64, 384, ri)
            return lambda: transposes([rp[ri][:P, k * 128:(k + 1) * 128] for k in range(4)], RP(ri), P,
                                      A_iqT[:, :, tt * P:(tt + 1) * P], [("arena", "iqT", tt)], eng="dve")
        proj(P, ntl, IQ, 512, ev_iq)

        def ev_ik(tt, b):
            ri = nxt("rp", 2)
            xi_ = nxt("xs", 2)
            K.op("act", lambda e: e.activation(out=xs[xi_][:P, :72], in_=psb[b][:P, :72], func=AF.Copy),
                 reads=[PS(b)], writes=[("xs", xi_)])
            x1, x2 = xs[xi_][:P, 0:32], xs[xi_][:P, 32:64]
            cosb, sinb = tab[:P, tt, 384:416], tab[:P, tt, 416:448]
            rd = [("xs", xi_), ("tab",)]
            K.op("dve", lambda e: e.tensor_tensor(out=rt[0][:P, :32], in0=x1, in1=cosb, op=ALU.mult), reads=rd, writes=[("rt", 0)])
            K.op("dve", lambda e: e.tensor_tensor(out=rt[1][:P, :32], in0=x2, in1=sinb, op=ALU.mult), reads=rd, writes=[("rt", 1)])
            K.op("dve", lambda e: e.tensor_tensor(out=rt[2][:P, :32], in0=x2, in1=cosb, op=ALU.mult), reads=rd, writes=[("rt", 2)])
            K.op("dve", lambda e: e.tensor_tensor(out=rt[3][:P, :32], in0=x1, in1=sinb, op=ALU.mult), reads=rd, writes=[("rt", 3)])
            for rep in range(2):
                K.op("dve", lambda e, rep=rep: e.tensor_tensor(out=rp[ri][:P, rep * 64:rep * 64 + 32], in0=rt[0][:P, :32],
                                                               in1=rt[1][:P, :32], op=ALU.subtract),
                     reads=[("rt", 0), ("rt", 1)], writes=[("rp", ri, 0)])
                K.op("dve", lambda e, rep=rep: e.tensor_tensor(out=rp[ri][:P, rep * 64 + 32:rep * 64 + 64], in0=rt[2][:P, :32],
                                                               in1=rt[3][:P, :32], op=ALU.add),
                     reads=[("rt", 2), ("rt", 3)], writes=[("rp", ri, 1)])
            K.op("dve", lambda e: e.tensor_copy(iw[:P, tt, :], xs[xi_][:P, 64:72]), reads=[("xs", xi_)], writes=[("iw", tt)])
            gi = blk * NT + tt
            return lambda: transposes([rp[ri][:P, 0:128]], RP(ri), P, kiT[:, gi * 128:(gi + 1) * 128].unsqueeze(1),
                                      [("kiT", gi)], eng="dve")
        proj(P, ntl, IK, 72, ev_ik)
        flush()

        Lblk = 128 * (blk * NT + NT)
        SCs = [A_sc, rt_all[:, :]]
        MKs = [mk[:, :], sg[:, :, :].rearrange("p a b -> p (a b)")]
        SCKEYS = [[("arena", "sc", s_) for s_ in range(4)], [("rt", i_) for i_ in range(8)]]
        MKKEYS = [[("mk",)], [("sg", 0, 0), ("sg", 0, 1), ("sg", 1, 0), ("sg", 1, 1)]]

        def col(c_):
            return stat[:, c_:c_ + 1]

        for tt in range(ntl):
            gi = blk * NT + tt
            L = 128 * (gi + 1)
            cs = slice(tt * 128, (tt + 1) * 128)
            SC, SCK = SCs[tt], SCKEYS[tt]
            for sbk in range((L + 511) // 512):
                ncols = min(512, L - sbk * 512)
                kt = [("kiT", t_) for t_ in range(sbk * 4, sbk * 4 + ncols // 128)]
                for hh in range(8):
                    c, r = hh // 2, hh % 2
                    b = K.ps_alloc()
                    K.pe([lambda pe: pe.matmul(psb[b][:, :ncols], lhsT=A_iqT[r * 64:(r + 1) * 64, c, cs],
                                               rhs=kiT[r * 64:(r + 1) * 64, sbk * 512:sbk * 512 + ncols], start=True, stop=True)],
                         reads=[("arena", "iqT", tt)] + kt, writes=[PS(b)])
                    li = nxt("xs", 2)
                    K.op("act", lambda e: e.activation(out=xs[li][:, :ncols], in_=psb[b][:, :ncols], func=AF.Relu),
                         reads=[PS(b)], writes=[("xs", li)])
                    K.ps_free(b)
                    scv = SC[:, sbk * 512:sbk * 512 + ncols]
                    wk = [SCK[sbk]] if tt == 0 else SCK
                    if hh == 0:
                        K.op("dve", lambda e: e.tensor_scalar(out=scv, in0=xs[li][:, :ncols], scalar1=iw[:, tt, 0:1], scalar2=None,
                                                              op0=ALU.mult),
                             reads=[("xs", li), ("iw", tt)], writes=wk)
                    else:
                        K.op("dve", lambda e: e.scalar_tensor_tensor(out=scv, in0=xs[li][:, :ncols], scalar=iw[:, tt, hh:hh + 1],
                                                                     in1=scv, op0=ALU.mult, op1=ALU.add),
                             reads=[("xs", li), ("iw", tt)], writes=wk)

        def bis_gen(tt):
            gi = blk * NT + tt
            L = 128 * (gi + 1)
            SC, SCK, MKT, MKK = SCs[tt], SCKEYS[tt], MKs[tt], MKKEYS[tt]
            LO, W0, MID, CNT, TMP = 16 + tt * 8, 17 + tt * 8, 18 + tt * 8, 19 + tt * 8, 20 + tt * 8
            if L < Lblk:
                K.op("pool", lambda e: e.memset(SC[:, L:Lblk], NEGBIG), reads=[], writes=SCK)
                yield
            if gi >= 2:
                K.op("dve", lambda e: e.tensor_reduce(out=col(LO), in_=SC[:, :L], axis=AX.X, op=ALU.min),
                     reads=SCK, writes=[("stat", LO)])
                yield
                K.op("dve", lambda e: e.tensor_reduce(out=col(W0), in_=SC[:, :L], axis=AX.X, op=ALU.max),
                     reads=SCK, writes=[("stat", W0)])
                yield
                K.op("dve", lambda e: e.tensor_tensor(out=col(W0), in0=col(W0), in1=col(LO), op=ALU.subtract),
                     reads=[("stat", W0), ("stat", LO)], writes=[("stat", W0)])
                yield
            K.op("dve", lambda e: e.tensor_tensor(out=SC[:, L - 128:L], in0=SC[:, L - 128:L], in1=NEG[:, :], op=ALU.add),
                 reads=[("NEG",)] + SCK, writes=SCK)
            yield
            if gi >= 2:
                for it in range(NBIS):
                    f = 2.0 ** -(it + 1)
                    K.op("dve", lambda e: e.scalar_tensor_tensor(out=col(MID), in0=col(W0), scalar=f, in1=col(LO),
                                                                 op0=ALU.mult, op1=ALU.add),
                         reads=[("stat", W0), ("stat", LO)], writes=[("stat", MID)])
                    yield
                    K.op("dve", lambda e: e.tensor_scalar(out=MKT[:, :L], in0=SC[:, :L], scalar1=col(MID), scalar2=0.0,
                                                          op0=ALU.is_ge, op1=ALU.add, accum_out=col(CNT)),
                         reads=SCK + [("stat", MID)], writes=MKK + [("stat", CNT)])
                    yield
                    K.op("dve", lambda e: e.scalar_tensor_tensor(out=col(TMP), in0=col(CNT), scalar=KTOP - 0.5, in1=col(W0),
                                                                 op0=ALU.is_ge, op1=ALU.mult),
                         reads=[("stat", CNT), ("stat", W0)], writes=[("stat", TMP)])
                    yield
                    K.op("dve", lambda e: e.scalar_tensor_tensor(out=col(LO), in0=col(TMP), scalar=f, in1=col(LO),
                                                                 op0=ALU.mult, op1=ALU.add),
                         reads=[("stat", TMP), ("stat", LO)], writes=[("stat", LO)])
                    yield
            else:
                K.op("dve", lambda e: e.memset(col(LO), -1.0e29), reads=[], writes=[("stat", LO)])
                yield
            K.op("dve", lambda e: e.tensor_scalar(out=MKT[:, :Lblk], in0=SC[:, :Lblk], scalar1=col(LO), scalar2=None,
                                                  op0=ALU.is_ge),
                 reads=SCK + [("stat", LO)], writes=MKK)
            yield

        gens = [bis_gen(tt) for tt in range(ntl)]
        alive = list(gens)
        while alive:
            for gnr in list(alive):
                try:
                    next(gnr)
                except StopIteration:
                    alive.remove(gnr)
        for tt in range(ntl):
            cs = slice(tt * 128, (tt + 1) * 128)
            nsb = Lblk // 128
            for s0 in range(0, nsb, 8):
                n = min(8, nsb - s0)
                transposes([MKs[tt][:, (s0 + k) * 128:(s0 + k + 1) * 128] for k in range(n)], MKKEYS[tt], 128,
                           A_maskT[:, s0:s0 + n, cs], [("arena", "maskT", tt)], eng="act")

        nkt = blk * NT + NT
        MK = [("arena", "maskT", 0), ("arena", "maskT", 1)]
        for hh in range(8):
            g = hh // 4
            bo = K.ps_alloc()
            bs = K.ps_alloc()
            tiles = [(-1, NM, 0)] + [(si, 128, 128 if si == nkt - 1 else 0) for si in range(nkt)]

            def qk(idx):
                si, Ps, q0 = tiles[idx]
                kc0 = 0 if si < 0 else NM + si * 128
                gt = si + 1
                b = K.ps_alloc()
                K.pe([lambda pe: pe.matmul(psb[b][:Ps, q0:TB], lhsT=akT[:, hh, kc0:kc0 + Ps], rhs=A_aqT[:, hh, q0:TB],
                                           start=True, stop=True)],
                     reads=[("akT", gt, g), ("arena", "aqT", 0, g), ("arena", "aqT", 1, g)], writes=[PS(b)])
                pi = nxt("pT", 3)
                K.op("act", lambda e: e.activation(out=pT[pi][:Ps, q0:TB], in_=psb[b][:Ps, q0:TB], func=AF.Exp, scale=ATT_SCALE),
                     reads=[PS(b)], writes=[("pT", pi)])
                K.ps_free(b)
                if si >= 0:
                    K.op("pool", lambda e: e.tensor_tensor(out=pT[pi][:Ps, q0:TB], in0=pT[pi][:Ps, q0:TB],
                                                           in1=A_maskT[:Ps, si, q0:TB], op=ALU.mult),
                         reads=[("pT", pi)] + MK, writes=[("pT", pi)])
                return pi

            def pv(idx, pi):
                si, Ps, q0 = tiles[idx]
                gt = si + 1
                first, last = idx == 0, idx == len(tiles) - 1
                K.pe([lambda pe: pe.matmul(psb[bo][:, q0:TB], lhsT=av[:Ps, gt, hh * 128:(hh + 1) * 128], rhs=pT[pi][:Ps, q0:TB],
                                           start=first, stop=last),
                      lambda pe: pe.matmul(psb[bs][:, q0:TB], lhsT=ones[:Ps, :], rhs=pT[pi][:Ps, q0:TB],
                                           start=first, stop=last)],
                     reads=[("av", gt, g), ("pT", pi), ("ones",)], writes=[PS(bo), PS(bs)])

            LOOK = 2
            pis = {}
            for idx in range(min(LOOK, len(tiles))):
                pis[idx] = qk(idx)
            for idx in range(len(tiles)):
                if idx + LOOK < len(tiles):
                    pis[idx + LOOK] = qk(idx + LOOK)
                pv(idx, pis.pop(idx))
            ri_ = nxt("rs", 2)
            K.op("dve", lambda e: e.reciprocal(out=rs[ri_][:, :], in_=psb[bs][:, :TB]), reads=[PS(bs)], writes=[("rs", ri_)])
            K.op("dve", lambda e: e.tensor_tensor(out=A_attnT[:, hh, :], in0=psb[bo][:, :TB], in1=rs[ri_][:, :], op=ALU.mult),
                 reads=[PS(bo), ("rs", ri_)], writes=[("arena", "attnT", hh)])
            K.ps_free(bo)
            K.ps_free(bs)
        for g in range(2):
            def ev_ga(tt, b, g=g):
                K.op("act", lambda e: e.activation(out=sg[:P, tt, g * 512:(g + 1) * 512], in_=psb[b][:P, :], func=AF.Sigmoid),
                     reads=[PS(b)], writes=[("sg", tt, g)])
            proj(P, ntl, GA + g * 512, 512, ev_ga)

        def ev_ao(tt, dh, b):
            sl = slice(dh * 512, (dh + 1) * 512)
            xi_ = nxt("xs", 2)
            K.op("dve", lambda e: e.tensor_tensor(out=xs[xi_][:P, :], in0=psb[b][:P, :], in1=sg[:P, tt, sl], op=ALU.mult),
                 reads=[PS(b), ("sg", tt, dh)], writes=[("xs", xi_)])
            K.op("pool", lambda e: e.tensor_tensor(out=merged[:P, tt, sl], in0=merged[:P, tt, sl], in1=xs[xi_][:P, :], op=ALU.add),
                 reads=[("xs", xi_)], writes=[("mg", tt, dh)])
        down_proj(lambda kc, tt: A_attnT[:, kc, tt * P:(tt + 1) * P], lambda kc: [("arena", "attnT", kc)], 8, "wao", P, ntl, ev_ao)
        for tt in range(ntl):
            transposes([merged[:P, tt, k * 128:(k + 1) * 128] for k in range(8)], [("mg", tt, 0), ("mg", tt, 1)], P,
                       uT[:, :, tt * P:(tt + 1) * P], [("uT", tt)], eng="act")

        def ev_mo(tt, dh, b):
            sl = slice(dh * 512, (dh + 1) * 512)
            K.op("dve", lambda e: e.tensor_tensor(out=h[:P, tt, sl], in0=psb[b][:P, :], in1=h[:P, tt, sl], op=ALU.add),
                 reads=[PS(b)], writes=[("h", tt)])
        down_proj(lambda kc, tt: uT[:, kc, tt * P:(tt + 1) * P], lambda kc: [("uT", tt) for tt in range(ntl)], 8, "wmo", P, ntl, ev_mo)

    K.dma("sp", h[:NM, 0, :], dram["meta"], writes=[("h", 0)], slot=("x", 0))
    ffn(NM, 1, 0, "w1gu", "w1d")
    mixer(NM, 1, -1)
    for blk in range(nblk):
        for tt in range(NT):
            r0 = blk * TB + tt * 128
            K.dma("sp", h[:, tt, :], dram["x"][r0:r0 + 128, :], writes=[("h", tt)], slot=("x", tt))
        ffn(128, NT, 0, "w1gu", "w1d")
        if stages >= 2:
            mixer(128, NT, blk)
        if stages >= 3:
            ffn(128, NT, 2, "w2gu", "w2d")
        if stages >= 4:
            K.dma("sp", gbc[:, :], dram["gains"][3 * 128:4 * 128, :], writes=[("gbc",)], slot=("gbc",))
        for tt in range(NT):
            r0 = blk * TB + tt * 128
            if stages >= 4:
                c = tt
                hs = h[:, tt, :]
                K.op("act", lambda e: e.activation(out=junk[:, :], in_=hs, func=AF.Square, accum_out=stat[:, c:c + 1]),
                     reads=[("h", tt)], writes=[("stat", c)])
                K.op("act", lambda e: e.activation(out=stat[:, c:c + 1], in_=stat[:, c:c + 1], func=AF.Sqrt, bias=epsb[:, :], scale=1.0 / D),
                     reads=[("stat", c), ("epsb",)], writes=[("stat", c)])
                K.op("dve", lambda e: e.reciprocal(out=stat[:, c:c + 1], in_=stat[:, c:c + 1]),
                     reads=[("stat", c)], writes=[("stat", c)])
                K.op("dve", lambda e: e.scalar_tensor_tensor(out=hs, in0=hs, scalar=stat[:, c:c + 1], in1=gbc[:, :],
                                                             op0=ALU.mult, op1=ALU.mult),
                     reads=[("stat", c), ("gbc",)], writes=[("h", tt)])
            K.dma("pool", dram["y"][r0:r0 + 128, :], h[:, tt, :], reads=[("h", tt)], writes=[], slot=("y", tt))
    sp = K.E["sp"]
    for tt in range(NT):
        s = K.dsem[("y", tt)]
        sp.wait(s[0], s[1])


_CACHE = {}


def _prep_inputs(inputs):
    f = lambda a: np.ascontiguousarray(np.asarray(a, dtype=np.float32))
    shared = {
        "meta": f(inputs["meta_tokens"]),
        "gains": np.ascontiguousarray(np.concatenate([
            np.broadcast_to(f(inputs["ffn1_norm"]).reshape(1, D), (128, D)),
            np.broadcast_to(f(inputs["mix_norm"]).reshape(1, D), (128, D)),
            np.broadcast_to(f(inputs["ffn2_norm"]).reshape(1, D), (128, D)),
            np.broadcast_to(f(inputs["final_norm"]).reshape(1, D), (128, D))], axis=0)),
        "w1g": f(inputs["ffn1_w_gate"][0]), "w1u": f(inputs["ffn1_w_up"][0]), "w1d": f(inputs["ffn1_w_down"][0]),
        "win": f(inputs["w_in"][0]), "wro": f(inputs["w_ret_out"][0]), "wao": f(inputs["w_att_out"][0]),
        "wmo": f(inputs["w_mix_out"][0]),
        "w2g": f(inputs["ffn2_w_gate"][0]), "w2u": f(inputs["ffn2_w_up"][0]), "w2d": f(inputs["ffn2_w_down"][0]),
    }
    shared.update(_consts())
    return shared


def kernel(**inputs):
    x = np.asarray(inputs["x"], dtype=np.float32)
    B = x.shape[0]
    shared = _prep_inputs(inputs)
    if "nc" not in _CACHE:
        _CACHE["nc"] = build_program()
    nc = _CACHE["nc"]
    in_maps = []
    for b in range(B):
        m = dict(shared)
        m["x"] = np.ascontiguousarray(x[b])
        in_maps.append(m)
    res = run_bass_kernel_spmd(nc, in_maps, core_ids=list(range(B)))
    return np.stack([np.asarray(r["y"], dtype=np.float32) for r in res.results], axis=0)
```

```python
import contextlib
import numpy as np
import concourse.bass as bass
import concourse.mybir as mybir
from concourse.bass_utils import run_bass_kernel_spmd

F32 = mybir.dt.float32
BF16 = mybir.dt.bfloat16
AF = mybir.ActivationFunctionType
ALU = mybir.AluOpType
AX = mybir.AxisListType

D = 1024
DFF = 2816
NM = 16
SEQ = 2048
T = NM + SEQ
DIN = 11848
TB = 256
NT = 2
NBLK = SEQ // TB
EPS = 1e-6
KTOP = 256
NBIS = 18
GAM = [1.0 - 2.0 ** (-5.0 - h) for h in range(4)]
RQ, RK, RV, RG, AQ, AK, AV, IQ, IK, IW, GR, GA = 0, 1024, 2048, 4096, 6144, 7168, 8192, 9216, 9728, 9792, 9800, 10824
ATT_SCALE = 128.0 ** -0.5
NEGBIG = -1.0e30

DEBUG = {}


class Eng:
    def __init__(self, K, name, raw):
        self.K, self.name, self.raw = K, name, raw
        self.sem = None
        self.cnt = 0
        self.waited = {}

    def rotate(self):
        if self.sem is None or self.cnt >= 3000:
            self.sem = self.K.new_sem(self.name)
            self.cnt = 0

    def wait(self, sem, val):
        if self.waited.get(id(sem), 0) >= val:
            return
        if sem is self.sem and self.name == "pe":
            return
        self.raw.wait_ge(sem, val)
        self.waited[id(sem)] = val


class Ctx:
    def __init__(self, nc, es):
        self.nc, self.es = nc, es
        self.nsem = 0
        self.keys = {}
        self.base = {}
        self.E = {}
        for n, raw in (("pe", nc.tensor), ("act", nc.scalar), ("dve", nc.vector), ("pool", nc.gpsimd), ("sp", nc.sync)):
            self.E[n] = Eng(self, n, raw)
        self.dsem = {}
        self.ps_free_list = []
        self.ntmp = 0

    def new_sem(self, name):
        self.nsem += 1
        return self.es.enter_context(self.nc.semaphore("s%d_%s" % (self.nsem, name)))

    def sb(self, name, shape, dtype):
        return self.es.enter_context(self.nc.sbuf_tensor(name, shape, dtype))

    def _get(self, k):
        st = self.keys.get(k)
        if st is None:
            b = self.base.get(k[0])
            st = [None, dict(b) if b else {}]
            self.keys[k] = st
        return st

    def retire(self, slot):
        agg = dict(self.base.get(slot, {}))

        def add(t):
            if t is None:
                return
            cur = agg.get(id(t[0]))
            if cur is None or cur[1] < t[1]:
                agg[id(t[0])] = t

        for k in [k for k in self.keys if k[0] == slot]:
            st = self.keys.pop(k)
            add(st[0])
            for t in st[1].values():
                add(t)
        self.base[slot] = agg

    def _deps(self, eng, reads, writes):
        need = {}

        def add(t):
            if t is None:
                return
            cur = need.get(id(t[0]))
            if cur is None or cur[1] < t[1]:
                need[id(t[0])] = t

        for k in reads:
            add(self._get(k)[0])
        for k in writes:
            st = self._get(k)
            add(st[0])
            for t in st[1].values():
                add(t)
        for sem, val in need.values():
            eng.wait(sem, val)

    def _commit(self, t, reads, writes):
        for k in reads:
            if k in writes:
                continue
            st = self._get(k)
            cur = st[1].get(id(t[0]))
            if cur is None or cur[1] < t[1]:
                st[1][id(t[0])] = t
        for k in writes:
            st = self._get(k)
            st[0] = t
            st[1] = {}

    def op(self, en, fn, reads=(), writes=()):
        eng = self.E[en]
        eng.rotate()
        self._deps(eng, reads, writes)
        ins = fn(eng.raw)
        ins.then_inc(eng.sem, 1)
        eng.cnt += 1
        t = (eng.sem, eng.cnt)
        self._commit(t, reads, writes)
        return t

    def pe(self, fns, reads=(), writes=()):
        eng = self.E["pe"]
        eng.rotate()
        self._deps(eng, reads, writes)
        ins = None
        for fn in fns:
            ins = fn(eng.raw)
        ins.then_inc(eng.sem, 1)
        eng.cnt += 1
        t = (eng.sem, eng.cnt)
        self._commit(t, reads, writes)
        return t

    def dma(self, q, out, in_, reads=(), writes=(), slot=None, serial=True):
        eng = self.E[q]
        self._deps(eng, reads, writes)
        s = self.dsem.get(slot)
        if s is None:
            s = [self.new_sem("d"), 0]
            self.dsem[slot] = s
        if serial and s[1] > 0:
            eng.wait(s[0], s[1])
        ins = eng.raw.dma_start(out=out, in_=in_)
        ins.then_inc(s[0], 16)
        s[1] += 16
        t = (s[0], s[1])
        self._commit(t, reads, writes)
        return t

    def ps_alloc(self):
        assert self.ps_free_list, "out of PSUM banks"
        return self.ps_free_list.pop(0)

    def ps_free(self, b):
        self.ps_free_list.append(b)


def _consts():
    pos = np.arange(T, dtype=np.float32)

    def tab(d):
        inv = (np.float32(10000.0) ** (-np.arange(0, d, 2, dtype=np.float32) / np.float32(d))).astype(np.float32)
        ang = (pos[:, None] * inv[None, :]).astype(np.float32)
        return np.cos(ang).astype(np.float32), np.sin(ang).astype(np.float32)

    c256, s256 = tab(256)
    c128, s128 = tab(128)
    c64, s64 = tab(64)
    rope = np.concatenate([c256, s256, c128, s128, c64, s64], axis=1).astype(np.float32)
    i = np.arange(128)
    dt = np.zeros((128, 4, 128), np.float64)
    xi = np.zeros((128, 4, 128), np.float64)
    zeta = np.zeros((128, 8), np.float64)
    for h in range(4):
        lg = np.log1p(-(2.0 ** (-5.0 - h)))
        diff = i[None, :] - i[:, None]
        dt[:, h, :] = np.where(diff >= 0, np.exp(lg * np.maximum(diff, 0)), 0.0) / 16.0
        xi[:, h, :] = np.exp(lg * (i[None, :] + 1.0))
        zeta[:, h] = np.exp(lg * (127.0 - i)) / 16.0
        zeta[:NM, 4 + h] = np.exp(lg * (NM - 1.0 - np.arange(NM))) / 16.0
    neg = np.where(i[None, :] <= i[:, None], 0.0, NEGBIG)
    return {
        "c_rope": rope,
        "c_dt": dt.astype(np.float32).reshape(128, 512),
        "c_xi": xi.astype(np.float32).reshape(128, 512),
        "c_zeta": zeta.astype(np.float32),
        "c_neg": neg.astype(np.float32),
        "c_ident": np.eye(128, dtype=np.float32),
        "c_ones": np.ones((128, 128), np.float32),
    }


WSPECS = [
    ("w1g", D, DFF), ("w1u", D, DFF), ("w1d", DFF, D), ("win", D, DIN), ("wro", 2048, D),
    ("wao", D, D), ("wmo", D, D), ("w2g", D, DFF), ("w2u", D, DFF), ("w2d", DFF, D),
]


WDIMS = {n: (r, c) for n, r, c in WSPECS}
WIN_GROUPS = [(c0, 512) for c0 in range(0, 9728, 512)] + [(IK, 72), (GR, 512), (GR + 512, 512), (GA, 512), (GA + 512, 512)]
WIN_SLAB = {c0: i for i, (c0, w) in enumerate(WIN_GROUPS)}


SCRATCH = ["w1gu", "w1d", "win", "wro", "wao", "wmo", "w2gu", "w2d"]


def _slabs(n):
    if n in ("w1gu", "w2gu"):
        return [("gu", c0, 256) for c0 in range(0, DFF, 256)]
    r, c = WDIMS[n]
    if n == "win":
        return [("c", c0, w) for c0, w in WIN_GROUPS]
    if c == DFF:
        return [("c", c0, min(512, DFF - c0)) for c0 in range(0, DFF, 512)]
    return [("r", r0, min(512, r - r0)) for r0 in range(0, r, 512)]


def _slab_plan(nblk, stages):
    def ffn_(gu, d):
        return [(gu, i) for i in range(len(_slabs(gu)))] + [(d, i) for i in range(len(_slabs(d)))]

    def mix_(meta):
        out = []
        for hp in range(2):
            if not meta:
                out.append(("win", WIN_SLAB[RQ + hp * 512]))
            out.append(("win", WIN_SLAB[RK + hp * 512]))
            out += [("win", WIN_SLAB[RV + hp * 1024 + g * 512]) for g in range(2)]
            if not meta:
                out += [("win", WIN_SLAB[RG + hp * 1024 + g * 512]) for g in range(2)]
        if not meta:
            out += [("win", WIN_SLAB[GR + g * 512]) for g in range(2)]
            out += [("wro", i) for i in range(4)]
            out += [("win", WIN_SLAB[AQ + g * 512]) for g in range(2)]
        out += [("win", WIN_SLAB[AK + g * 512]) for g in range(2)]
        out += [("win", WIN_SLAB[AV + g * 512]) for g in range(2)]
        if not meta:
            out += [("win", WIN_SLAB[IQ]), ("win", WIN_SLAB[IK])]
            out += [("win", WIN_SLAB[GA + g * 512]) for g in range(2)]
            out += [("wao", i) for i in range(2)] + [("wmo", i) for i in range(2)]
        return out

    plan = ffn_("w1gu", "w1d") + mix_(True)
    for b in range(nblk):
        plan += ffn_("w1gu", "w1d")
        if stages >= 2:
            plan += mix_(False)
        if stages >= 3:
            plan += ffn_("w2gu", "w2d")
    return plan


def build_program(nblk=NBLK, stages=99):
    nc = bass.Bass("TRN2", target_bir_lowering=False)
    dram = {}
    dram["x"] = nc.dram_tensor("x", [SEQ, D], F32, kind="ExternalInput").ap()
    dram["meta"] = nc.dram_tensor("meta", [NM, D], F32, kind="ExternalInput").ap()
    dram["gains"] = nc.dram_tensor("gains", [4 * 128, D], F32, kind="ExternalInput").ap()
    for n, r, c in WSPECS:
        dram[n] = nc.dram_tensor(n, [r, c], F32, kind="ExternalInput").ap()
    for n in SCRATCH:
        dram[n + "b"] = nc.dram_tensor(n + "b", [len(_slabs(n)), 128, 4096], BF16, kind="Internal").ap()
    cshapes = {"c_rope": [T, 448], "c_dt": [128, 512], "c_xi": [128, 512], "c_zeta": [128, 8],
               "c_neg": [128, 128], "c_ident": [128, 128], "c_ones": [128, 128]}
    for n, s in cshapes.items():
        dram[n] = nc.dram_tensor(n, s, F32, kind="ExternalInput").ap()
    dram["y"] = nc.dram_tensor("y", [SEQ, D], F32, kind="ExternalOutput").ap()
    for n, s in DEBUG.items():
        dram[n] = nc.dram_tensor(n, s, F32, kind="ExternalOutput").ap()

    with contextlib.ExitStack() as es:
        K = Ctx(nc, es)
        _emit(K, dram, nblk, stages)
    return nc


def _emit(K, dram, nblk, stages):
    nc = K.nc
    sb = K.sb
    akT = sb("akT", [128, 8, T], BF16)
    av = sb("av", [128, 17, D], BF16)
    kiT = sb("kiT", [128, SEQ], BF16)
    S = sb("S", [128, 8, 512], F32)
    Sb = sb("Sb", [128, 8, 512], BF16)
    ident = sb("ident", [128, 128], BF16)
    ones = sb("ones", [128, 128], BF16)
    DT = sb("DT", [128, 4, 128], F32)
    XI = sb("XI", [128, 4, 128], F32)
    zeta = sb("zeta", [128, 8], F32)
    NEG = sb("NEG", [128, 128], F32)
    gbc = sb("gbc", [128, D], F32)
    h = sb("h", [128, NT, D], F32)
    uT = sb("uT", [128, 8, TB], BF16)
    utm = [sb("utm%d" % i, [128, D], BF16) for i in range(2)]
    arena = sb("arena", [128, 13312], BF16)
    merged = sb("merged", [128, NT, D], BF16)
    sg = sb("sg", [128, NT, D], BF16)
    tab = sb("tab", [128, NT, 448], F32)
    iw = sb("iw", [128, NT, 8], F32)
    NWS = 3
    wslot = [sb("wslot%d" % i, [128, 4096], BF16) for i in range(NWS)]
    xs = [sb("xs%d" % i, [128, 512], F32) for i in range(2)]
    rt_all = sb("rt_all", [128, 2048], F32)
    rt = [rt_all[:, i * 256:(i + 1) * 256] for i in range(8)]
    rp = [sb("rp%d" % i, [128, 512], BF16) for i in range(2)]
    sgt = [sb("sgt%d" % i, [128, TB], BF16) for i in range(2)]
    pT = [sb("pT%d" % i, [128, TB], BF16) for i in range(3)]
    rs = [sb("rs%d" % i, [128, TB], F32) for i in range(2)]
    junk = sb("junk", [128, D], BF16)
    mk = sb("mk", [128, SEQ], BF16)
    ATb = [sb("ATb%d" % i, [128, 128], BF16) for i in range(2)]
    rqx = [sb("rqx%d" % i, [128, 2, 128], BF16) for i in range(2)]
    stat = sb("stat", [128, 64], F32)
    epsb = sb("epsb", [128, 1], F32)
    psb = [K.es.enter_context(nc.psum_tensor("psb%d" % i, [128, 512], F32)) for i in range(8)]
    K.ps_free_list = list(range(8))

    def PS(b):
        return ("ps", b)

    rr = {}

    def nxt(name, n):
        v = rr.get(name, 0)
        rr[name] = (v + 1) % n
        return v

    A_actT = arena[:, 0:22 * TB].rearrange("p (a b) -> p a b", a=22)
    A_rqT = arena[:, 0:1024].rearrange("p (a b) -> p a b", a=4)
    A_rkT = arena[:, 1024:2048].rearrange("p (a b) -> p a b", a=4)
    A_kz = arena[:, 2048:3072].rearrange("p (a b) -> p a b", a=NT)
    A_rv = arena[:, 3072:5120].rearrange("p (a b) -> p a b", a=NT)
    A_rg = arena[:, 5120:7168].rearrange("p (a b) -> p a b", a=NT)
    A_ygT = arena[:, 7168:11264].rearrange("p (a b) -> p a b", a=16)
    A_aqT = arena[:, 0:2048].rearrange("p (a b) -> p a b", a=8)
    A_iqT = arena[:, 2048:3072].rearrange("p (a b) -> p a b", a=4)
    A_attnT = arena[:, 3072:5120].rearrange("p (a b) -> p a b", a=8)
    A_sc = arena[:, 5120:9216].bitcast(F32)
    A_maskT = arena[:, 9216:13312].rearrange("p (a b) -> p a b", a=16)

    K.op("dve", lambda e: e.memset(epsb[:, :], EPS), reads=[], writes=[("epsb",)])
    def cload(dst, src, key, cast):
        K.dma("pool" if cast else "sp", dst, src, writes=[key], slot=key)

    cload(ident[:, :], dram["c_ident"], ("ident",), True)
    cload(ones[:, :], dram["c_ones"], ("ones",), True)
    cload(DT[:, :, :].rearrange("p a b -> p (a b)"), dram["c_dt"], ("DT",), False)
    cload(XI[:, :, :].rearrange("p a b -> p (a b)"), dram["c_xi"], ("XI",), False)
    cload(zeta[:, :], dram["c_zeta"], ("zeta",), False)
    cload(NEG[:, :], dram["c_neg"], ("NEG",), False)

    for n in SCRATCH:
        for si, (kind, off, sz) in enumerate(_slabs(n)):
            if kind == "gu":
                dstv = dram[n + "b"][si, :, :].rearrange("p (k c) -> p k c", k=8)
                for j, src_n in enumerate((n[:2] + "g", n[:2] + "u")):
                    src = dram[src_n].rearrange("(k p) c -> p k c", p=128)[:, :, off:off + 256]
                    K.dma("pool", dstv[:, :, j * 256:(j + 1) * 256], src, writes=[], slot=("wb", n), serial=False)
                continue
            if kind == "c":
                dst = dram[n + "b"][si, :, 0:8 * sz].rearrange("p (k c) -> p k c", k=8)
                src = dram[n].rearrange("(k p) c -> p k c", p=128)[:, :, off:off + sz]
            else:
                nrc = sz // 128
                dst = dram[n + "b"][si, :, 0:nrc * 1024].rearrange("p (k c) -> p k c", k=nrc)
                src = dram[n][off:off + sz, :].rearrange("(k p) c -> p k c", p=128)
            K.dma("pool", dst, src, writes=[], slot=("wb", n), serial=False)
        sm = K.dsem[("wb", n)]
        K._commit((sm[0], sm[1]), [], [("wb", n)])
        if n in ("w1d", "win", "wmo"):
            for m in SCRATCH[:SCRATCH.index(n) + 1]:
                sm2 = K.dsem[("wb", m)]
                K.E["pool"].wait(sm2[0], sm2[1])

    plan = _slab_plan(nblk, stages)
    wst = {"issued": 0, "used": 0}

    def _issue_one():
        i = wst["issued"]
        if i >= len(plan):
            return
        n, si = plan[i]
        kind, off, sz = _slabs(n)[si]
        ncol = 4096 if kind == "gu" else (8 * sz if kind == "c" else (sz // 128) * 1024)
        sl = i % NWS
        K.dma("sp", wslot[sl][:, :ncol], dram[n + "b"][si, :, 0:ncol], reads=[("wb", n)], writes=[("ws", sl)], slot=("ws", sl))
        wst["issued"] += 1

    def wnext(n, si):
        i = wst["used"]
        assert plan[i] == (n, si), (i, plan[i], n, si)
        while wst["issued"] <= i:
            assert wst["issued"] < wst.get("done", 0) + NWS
            _issue_one()
        wst["used"] += 1
        return i % NWS

    def wdone(k=1):
        wst["done"] = wst.get("done", 0) + k
        while wst["issued"] < min(len(plan), wst["done"] + NWS):
            _issue_one()

    def wload(name, c0, w):
        if name == "win":
            si = WIN_SLAB[c0]
        else:
            si = c0 // 512
        return wnext(name, si)

    def wv(i, w):
        return wslot[i][:, 0:8 * w].rearrange("p (k c) -> p k c", k=8)

    def transposes(srcs, src_reads, P, dst, dst_writes, eng="act"):
        n = len(srcs)
        b = K.ps_alloc()
        pv = psb[b][:, :].bitcast(BF16).rearrange("p (a b) -> p a b", a=8)
        fns = []
        for k, s_ap in enumerate(srcs):
            fns.append(lambda pe, k=k, s_ap=s_ap: pe.transpose(pv[:, k, :P], s_ap, ident[:P, :P]))
        K.pe(fns, reads=list(src_reads) + [("ident",)], writes=[PS(b)])
        if eng == "act":
            K.op("act", lambda e: e.activation(out=dst, in_=pv[:, :n, :P], func=AF.Copy), reads=[PS(b)], writes=dst_writes)
        else:
            K.op("dve", lambda e: e.tensor_copy(dst, pv[:, :n, :P]), reads=[PS(b)], writes=dst_writes)
        K.ps_free(b)

    def norm_to_uT(P, ntl, grow):
        K.dma("sp", gbc[:, :], dram["gains"][grow * 128:(grow + 1) * 128, :], writes=[("gbc",)], slot=("gbc",))
        for tt in range(ntl):
            hs = h[:P, tt, :]
            c = tt
            K.op("act", lambda e: e.activation(out=junk[:P, :], in_=hs, func=AF.Square, accum_out=stat[:P, c:c + 1]),
                 reads=[("h", tt)], writes=[("stat", c)])
            K.op("act", lambda e: e.activation(out=stat[:P, c:c + 1], in_=stat[:P, c:c + 1], func=AF.Sqrt, bias=epsb[:P, :], scale=1.0 / D),
                 reads=[("stat", c), ("epsb",)], writes=[("stat", c)])
            K.op("dve", lambda e: e.reciprocal(out=stat[:P, c:c + 1], in_=stat[:P, c:c + 1]),
                 reads=[("stat", c)], writes=[("stat", c)])
            ui = nxt("utm", 2)
            K.op("dve", lambda e: e.scalar_tensor_tensor(out=utm[ui][:P, :], in0=hs, scalar=stat[:P, c:c + 1],
                                                         in1=gbc[:P, :], op0=ALU.mult, op1=ALU.mult),
                 reads=[("h", tt), ("stat", c), ("gbc",)], writes=[("utm", ui)])
            transposes([utm[ui][:P, k * 128:(k + 1) * 128] for k in range(8)], [("utm", ui)], P,
                       uT[:, :, tt * P:(tt + 1) * P], [("uT", tt)], eng="act")

    def down_proj(lhs_fn, lhs_reads_fn, nk, wname, P, ntl, evac):
        banks = [[K.ps_alloc() for dh in range(2)] for tt in range(ntl)]
        wi = None
        for kc in range(nk):
            if kc % 4 == 0:
                wi = wnext(wname, kc // 4)
            wvv = wslot[wi][:, :].rearrange("p (k c) -> p k c", k=4)
            fns = []
            for tt in range(ntl):
                for dh in range(2):
                    fns.append(lambda pe, tt=tt, dh=dh: pe.matmul(
                        psb[banks[tt][dh]][:P, :], lhsT=lhs_fn(kc, tt), rhs=wvv[:, kc % 4, dh * 512:(dh + 1) * 512],
                        start=(kc == 0), stop=(kc == nk - 1)))
            K.pe(fns, reads=[("ws", wi)] + lhs_reads_fn(kc),
                 writes=[PS(banks[tt][dh]) for tt in range(ntl) for dh in range(2)])
            if kc % 4 == 3 or kc == nk - 1:
                wdone(1)
        for tt in range(ntl):
            for dh in range(2):
                evac(tt, dh, banks[tt][dh])
                K.ps_free(banks[tt][dh])

    def ffn(P, ntl, grow, wgu, wdn):
        N = P * ntl
        norm_to_uT(P, ntl, grow)
        K.retire("arena")
        for j in range(DFF // 256):
            ig = wnext(wgu, j)
            gv = wslot[ig][:, :].rearrange("p (k c) -> p k c", k=8)
            for fl in range(2):
                fc = 2 * j + fl
                pg = K.ps_alloc()
                pu = K.ps_alloc()
                K.pe([lambda pe, k=k: pe.matmul(psb[pg][:, :N], lhsT=gv[:, k, fl * 128:(fl + 1) * 128],
                                                rhs=uT[:, k, :N], start=(k == 0), stop=(k == 7)) for k in range(8)],
                     reads=[("ws", ig)] + [("uT", tt) for tt in range(ntl)], writes=[PS(pg)])
                K.pe([lambda pe, k=k: pe.matmul(psb[pu][:, :N], lhsT=gv[:, k, 256 + fl * 128:256 + (fl + 1) * 128],
                                                rhs=uT[:, k, :N], start=(k == 0), stop=(k == 7)) for k in range(8)],
                     reads=[("ws", ig)] + [("uT", tt) for tt in range(ntl)], writes=[PS(pu)])
                si = nxt("sgt", 2)
                K.op("act", lambda e: e.activation(out=sgt[si][:, :N], in_=psb[pg][:, :N], func=AF.Silu),
                     reads=[PS(pg)], writes=[("sgt", si)])
                K.op("dve", lambda e: e.tensor_tensor(out=A_actT[:, fc, :N], in0=sgt[si][:, :N], in1=psb[pu][:, :N],
                                                      op=ALU.mult),
                     reads=[("sgt", si), PS(pu)], writes=[("arena", "actT", fc)])
                K.ps_free(pg)
                K.ps_free(pu)
            wdone(1)

        def evac(tt, dh, b):
            K.op("dve", lambda e: e.scalar_tensor_tensor(out=h[:P, tt, dh * 512:(dh + 1) * 512], in0=psb[b][:P, :],
                                                         scalar=0.5, in1=h[:P, tt, dh * 512:(dh + 1) * 512],
                                                         op0=ALU.mult, op1=ALU.add),
                 reads=[PS(b)], writes=[("h", tt)])

        down_proj(lambda kc, tt: A_actT[:, kc, tt * P:(tt + 1) * P], lambda kc: [("arena", "actT", kc)], 22, wdn, P, ntl, evac)

    deferred = []

    def flush():
        while deferred:
            deferred.pop(0)()

    def proj(P, ntl, c0, w, evac):
        wi = wload("win", c0, w)
        for tt in range(ntl):
            b = K.ps_alloc()
            K.pe([lambda pe, k=k: pe.matmul(psb[b][:P, :w], lhsT=uT[:, k, tt * P:(tt + 1) * P], rhs=wv(wi, w)[:, k, :w],
                                            start=(k == 0), stop=(k == 7)) for k in range(8)],
                 reads=[("ws", wi), ("uT", tt)], writes=[PS(b)])
            flush()
            post = evac(tt, b)
            if post is not None:
                deferred.append(post)
            K.ps_free(b)
        wdone(1)

    def rope_evac(P, tt, b, w, nh, d, tc0, dst_i):
        hd = d // 2
        xi_ = nxt("xs", 2)
        K.op("act", lambda e: e.activation(out=xs[xi_][:P, :w], in_=psb[b][:P, :w], func=AF.Copy),
             reads=[PS(b)], writes=[("xs", xi_)])
        x = xs[xi_][:P, :w].rearrange("p (h two f) -> p h two f", h=nh, two=2)
        o = rp[dst_i][:P, :w].rearrange("p (h two f) -> p h two f", h=nh, two=2)
        x1, x2 = x[:, :, 0, :], x[:, :, 1, :]
        cosb = tab[:P, tt, tc0:tc0 + hd].unsqueeze(1).to_broadcast([P, nh, hd])
        sinb = tab[:P, tt, tc0 + hd:tc0 + 2 * hd].unsqueeze(1).to_broadcast([P, nh, hd])
        r0 = 4 * nxt("rtset", 2)
        tv = [rt[r0 + i][:P, :nh * hd].rearrange("p (h f) -> p h f", h=nh) for i in range(4)]
        rd = [("xs", xi_), ("tab",)]
        K.op("dve", lambda e: e.tensor_tensor(out=tv[0], in0=x1, in1=cosb, op=ALU.mult), reads=rd, writes=[("rt", r0)])
        K.op("dve", lambda e: e.tensor_tensor(out=tv[1], in0=x2, in1=sinb, op=ALU.mult), reads=rd, writes=[("rt", r0 + 1)])
        K.op("pool", lambda e: e.tensor_tensor(out=tv[2], in0=x2, in1=cosb, op=ALU.mult), reads=rd, writes=[("rt", r0 + 2)])
        K.op("pool", lambda e: e.tensor_tensor(out=tv[3], in0=x1, in1=sinb, op=ALU.mult), reads=rd, writes=[("rt", r0 + 3)])
        K.op("dve", lambda e: e.tensor_tensor(out=o[:, :, 0, :], in0=tv[0], in1=tv[1], op=ALU.subtract),
             reads=[("rt", r0), ("rt", r0 + 1)], writes=[("rp", dst_i, 0)])
        K.op("pool", lambda e: e.tensor_tensor(out=o[:, :, 1, :], in0=tv[2], in1=tv[3], op=ALU.add),
             reads=[("rt", r0 + 2), ("rt", r0 + 3)], writes=[("rp", dst_i, 1)])

    def RP(i):
        return [("rp", i, 0), ("rp", i, 1)]

    def mixer(P, ntl, blk):
        is_meta = blk < 0
        pos0 = 0 if is_meta else NM + blk * TB
        K.dma("sp", tab[:P, 0:ntl, :], dram["c_rope"][pos0:pos0 + P * ntl, :].rearrange("(a p) c -> p a c", p=P),
              writes=[("tab",)], slot=("tab",))
        norm_to_uT(P, ntl, 1)
        K.retire("arena")
        for hp in range(2):
            if not is_meta:
                def ev_rq(tt, b):
                    ri = nxt("rp", 2)
                    rope_evac(P, tt, b, 512, 2, 256, 0, ri)
                    return lambda: transposes([rp[ri][:P, k * 128:(k + 1) * 128] for k in range(4)], RP(ri), P,
                                              A_rqT[:, :, tt * P:(tt + 1) * P], [("arena", "rqT", tt)], eng="dve")
                proj(P, ntl, RQ + hp * 512, 512, ev_rq)

            def ev_rk(tt, b):
                ri = nxt("rp", 2)
                rope_evac(P, tt, b, 512, 2, 256, 0, ri)
                zc = (4 if is_meta else 0) + 2 * hp
                K.op("pool", lambda e: e.tensor_tensor(
                    out=A_kz[:P, tt, :].rearrange("p (h f) -> p h f", h=2),
                    in0=rp[ri][:P, :].rearrange("p (h f) -> p h f", h=2),
                    in1=zeta[:P, zc:zc + 2].unsqueeze(2).to_broadcast([P, 2, 256]), op=ALU.mult),
                    reads=RP(ri) + [("zeta",)], writes=[("arena", "kz", tt)])
                if is_meta:
                    return None
                return lambda: transposes([rp[ri][:P, k * 128:(k + 1) * 128] for k in range(4)], RP(ri), P,
                                          A_rkT[:, :, tt * P:(tt + 1) * P], [("arena", "rkT", tt)], eng="dve")
            proj(P, ntl, RK + hp * 512, 512, ev_rk)
            for g in range(2):
                def ev_rv(tt, b, g=g):
                    K.op("act", lambda e: e.activation(out=A_rv[:P, tt, g * 512:(g + 1) * 512], in_=psb[b][:P, :],
                                                       func=AF.Copy), reads=[PS(b)], writes=[("arena", "rv", tt, g)])
                proj(P, ntl, RV + hp * 1024 + g * 512, 512, ev_rv)
            if not is_meta:
                for g in range(2):
                    def ev_rg(tt, b, g=g):
                        K.op("act", lambda e: e.activation(out=A_rg[:P, tt, g * 512:(g + 1) * 512], in_=psb[b][:P, :],
                                                           func=AF.Silu), reads=[PS(b)], writes=[("arena", "rg", tt, g)])
                    proj(P, ntl, RG + hp * 1024 + g * 512, 512, ev_rg)
            flush()
            for tt in range(ntl):
                cs = slice(tt * P, (tt + 1) * P)
                HL = range(2)
                b1s, b2s, b3s, ais, qis, cst = {}, {}, {}, {}, {}, {}
                if not is_meta:
                    for hl in HL:
                        b1s[hl] = K.ps_alloc()
                        K.pe([lambda pe, dc=dc: pe.matmul(psb[b1s[hl]][:, :128], lhsT=A_rkT[:, 2 * hl + dc, cs],
                                                          rhs=A_rqT[:, 2 * hl + dc, cs], start=(dc == 0), stop=(dc == 1))
                              for dc in range(2)],
                             reads=[("arena", "rkT", tt), ("arena", "rqT", tt)], writes=[PS(b1s[hl])])
                for hl in HL:
                    for dc in range(2):
                        b3 = K.ps_alloc()
                        b3s[(hl, dc)] = b3
                        K.pe([lambda pe: pe.matmul(psb[b3][:, :], lhsT=A_kz[:P, tt, hl * 256 + dc * 128:hl * 256 + (dc + 1) * 128],
                                                   rhs=A_rv[:P, tt, hl * 512:(hl + 1) * 512], start=True, stop=True)],
                             reads=[("arena", "kz", tt), ("arena", "rv", tt, hl)], writes=[PS(b3)])
                if not is_meta:
                    for hl in HL:
                        hh = 2 * hp + hl
                        ai = nxt("ATb", 2)
                        ais[hl] = ai
                        K.op("dve", lambda e: e.tensor_tensor(out=ATb[ai][:, :], in0=psb[b1s[hl]][:, :128], in1=DT[:, hh, :],
                                                              op=ALU.mult),
                             reads=[PS(b1s[hl]), ("DT",)], writes=[("ATb", ai)])
                        K.ps_free(b1s[hl])
                        qi = nxt("rqx", 2)
                        qis[hl] = qi
                        for dc in range(2):
                            K.op("pool", lambda e, dc=dc: e.tensor_tensor(out=rqx[qi][:, dc, :], in0=A_rqT[:, 2 * hl + dc, cs],
                                                                          in1=XI[:, hh, :], op=ALU.mult),
                                 reads=[("arena", "rqT", tt), ("XI",)], writes=[("rqx", qi, dc)])
                    for hl in HL:
                        hh = 2 * hp + hl
                        ai, qi = ais[hl], qis[hl]
                        b2 = K.ps_alloc()
                        b2s[hl] = b2
                        fns = [lambda pe: pe.matmul(psb[b2][:, :], lhsT=ATb[ai][:, :], rhs=A_rv[:, tt, hl * 512:(hl + 1) * 512],
                                                    start=True, stop=False)]
                        for dc in range(2):
                            fns.append(lambda pe, dc=dc: pe.matmul(psb[b2][:, :], lhsT=rqx[qi][:, dc, :], rhs=Sb[:, 2 * hh + dc, :],
                                                                   start=False, stop=(dc == 1)))
                        K.pe(fns, reads=[("ATb", ai), ("arena", "rv", tt, hl), ("rqx", qi, 0), ("rqx", qi, 1), ("Sb", hh, 0), ("Sb", hh, 1)],
                             writes=[PS(b2)])
                for hl in HL:
                    hh = 2 * hp + hl
                    for dc in range(2):
                        b3 = b3s[(hl, dc)]
                        if is_meta:
                            K.op("dve", lambda e: e.tensor_copy(S[:, 2 * hh + dc, :], psb[b3][:, :]),
                                 reads=[PS(b3)], writes=[("S", hh, dc)])
                        else:
                            K.op("dve", lambda e: e.scalar_tensor_tensor(
                                out=S[:, 2 * hh + dc, :], in0=S[:, 2 * hh + dc, :], scalar=float(GAM[hh] ** 128),
                                in1=psb[b3][:, :], op0=ALU.mult, op1=ALU.add),
                                reads=[PS(b3)], writes=[("S", hh, dc)])
                        K.ps_free(b3)
                        K.op("pool", lambda e: e.tensor_copy(Sb[:, 2 * hh + dc, :], S[:, 2 * hh + dc, :]),
                             reads=[("S", hh, dc)], writes=[("Sb", hh, dc)])
                if not is_meta:
                    for hl in HL:
                        b2 = b2s[hl]
                        c = 8 + nxt("gss", 8)
                        cst[hl] = c
                        K.op("act", lambda e: e.activation(out=junk[:, :512], in_=psb[b2][:, :], func=AF.Square,
                                                           accum_out=stat[:, c:c + 1]),
                             reads=[PS(b2)], writes=[("stat", c)])
                    for hl in HL:
                        c = cst[hl]
                        K.op("act", lambda e: e.activation(out=stat[:, c:c + 1], in_=stat[:, c:c + 1], func=AF.Sqrt, bias=epsb[:, :], scale=1.0 / 512),
                             reads=[("stat", c), ("epsb",)], writes=[("stat", c)])
                    for hl in HL:
                        c = cst[hl]
                        K.op("dve", lambda e: e.reciprocal(out=stat[:, c:c + 1], in_=stat[:, c:c + 1]),
                             reads=[("stat", c)], writes=[("stat", c)])
                    for hl in HL:
                        c, b2 = cst[hl], b2s[hl]
                        K.op("dve", lambda e: e.scalar_tensor_tensor(
                            out=A_rg[:, tt, hl * 512:(hl + 1) * 512], in0=psb[b2][:, :], scalar=stat[:, c:c + 1],
                            in1=A_rg[:, tt, hl * 512:(hl + 1) * 512], op0=ALU.mult, op1=ALU.mult),
                            reads=[PS(b2), ("stat", c)], writes=[("arena", "rg", tt, hl)])
                        K.ps_free(b2)
                    transposes([A_rg[:P, tt, k * 128:(k + 1) * 128] for k in range(8)],
                               [("arena", "rg", tt, 0), ("arena", "rg", tt, 1)], P,
                               A_ygT[:, hp * 8:(hp + 1) * 8, tt * P:(tt + 1) * P], [("arena", "ygT", hp, tt)], eng="act")
        if not is_meta:
            for g in range(2):
                def ev_gr(tt, b, g=g):
                    K.op("act", lambda e: e.activation(out=sg[:P, tt, g * 512:(g + 1) * 512], in_=psb[b][:P, :],
                                                       func=AF.Sigmoid), reads=[PS(b)], writes=[("sg", tt, g)])
                proj(P, ntl, GR + g * 512, 512, ev_gr)

            def ev_ro(tt, dh, b):
                K.op("dve", lambda e: e.tensor_tensor(out=merged[:P, tt, dh * 512:(dh + 1) * 512], in0=psb[b][:P, :],
                                                      in1=sg[:P, tt, dh * 512:(dh + 1) * 512], op=ALU.mult),
                     reads=[PS(b), ("sg", tt, dh)], writes=[("mg", tt, dh)])
            down_proj(lambda kc, tt: A_ygT[:, kc, tt * P:(tt + 1) * P],
                      lambda kc: [("arena", "ygT", kc // 8, tt) for tt in range(ntl)], 16, "wro", P, ntl, ev_ro)
        K.retire("arena")
        if not is_meta:
            for g in range(2):
                def ev_aq(tt, b, g=g):
                    ri = nxt("rp", 2)
                    rope_evac(P, tt, b, 512, 4, 128, 256, ri)
                    return lambda: transposes([rp[ri][:P, k * 128:(k + 1) * 128] for k in range(4)], RP(ri), P,
                                              A_aqT[:, g * 4:(g + 1) * 4, tt * P:(tt + 1) * P], [("arena", "aqT", tt, g)], eng="dve")
                proj(P, ntl, AQ + g * 512, 512, ev_aq)
        for g in range(2):
            def ev_ak(tt, b, g=g):
                ri = nxt("rp", 2)
                rope_evac(P, tt, b, 512, 4, 128, 256, ri)
                gt = 0 if is_meta else 1 + blk * NT + tt
                return lambda: transposes([rp[ri][:P, k * 128:(k + 1) * 128] for k in range(4)], RP(ri), P,
                                          akT[:, g * 4:(g + 1) * 4, pos0 + tt * P:pos0 + (tt + 1) * P], [("akT", gt, g)], eng="dve")
            proj(P, ntl, AK + g * 512, 512, ev_ak)
        for g in range(2):
            def ev_av(tt, b, g=g):
                gt = 0 if is_meta else 1 + blk * NT + tt
                K.op("act", lambda e: e.activation(out=av[:P, gt, g * 512:(g + 1) * 512], in_=psb[b][:P, :], func=AF.Copy),
                     reads=[PS(b)], writes=[("av", gt, g)])
            proj(P, ntl, AV + g * 512, 512, ev_av)
        if is_meta:
            flush()
            return

        def ev_iq(tt, b):
            ri = nxt("rp", 2)
            rope_evac(P, tt, b, 512, 8, 64, 384, ri)
            return lambda: transposes([rp[ri][:P, k * 128:(k + 1) * 128] for k in range(4)], RP(ri), P,
                                      A_iqT[:, :, tt * P:(tt + 1) * P], [("arena", "iqT", tt)], eng="dve")
        proj(P, ntl, IQ, 512, ev_iq)

        def ev_ik(tt, b):
            ri = nxt("rp", 2)
            xi_ = nxt("xs", 2)
            K.op("act", lambda e: e.activation(out=xs[xi_][:P, :72], in_=psb[b][:P, :72], func=AF.Copy),
                 reads=[PS(b)], writes=[("xs", xi_)])
            x1, x2 = xs[xi_][:P, 0:32], xs[xi_][:P, 32:64]
            cosb, sinb = tab[:P, tt, 384:416], tab[:P, tt, 416:448]
            rd = [("xs", xi_), ("tab",)]
            K.op("dve", lambda e: e.tensor_tensor(out=rt[0][:P, :32], in0=x1, in1=cosb, op=ALU.mult), reads=rd, writes=[("rt", 0)])
            K.op("dve", lambda e: e.tensor_tensor(out=rt[1][:P, :32], in0=x2, in1=sinb, op=ALU.mult), reads=rd, writes=[("rt", 1)])
            K.op("dve", lambda e: e.tensor_tensor(out=rt[2][:P, :32], in0=x2, in1=cosb, op=ALU.mult), reads=rd, writes=[("rt", 2)])
            K.op("dve", lambda e: e.tensor_tensor(out=rt[3][:P, :32], in0=x1, in1=sinb, op=ALU.mult), reads=rd, writes=[("rt", 3)])
            for rep in range(2):
                K.op("dve", lambda e, rep=rep: e.tensor_tensor(out=rp[ri][:P, rep * 64:rep * 64 + 32], in0=rt[0][:P, :32],
                                                               in1=rt[1][:P, :32], op=ALU.subtract),
                     reads=[("rt", 0), ("rt", 1)], writes=[("rp", ri, 0)])
                K.op("dve", lambda e, rep=rep: e.tensor_tensor(out=rp[ri][:P, rep * 64 + 32:rep * 64 + 64], in0=rt[2][:P, :32],
                                                               in1=rt[3][:P, :32], op=ALU.add),
                     reads=[("rt", 2), ("rt", 3)], writes=[("rp", ri, 1)])
            K.op("dve", lambda e: e.tensor_copy(iw[:P, tt, :], xs[xi_][:P, 64:72]), reads=[("xs", xi_)], writes=[("iw", tt)])
            gi = blk * NT + tt
            return lambda: transposes([rp[ri][:P, 0:128]], RP(ri), P, kiT[:, gi * 128:(gi + 1) * 128].unsqueeze(1),
                                      [("kiT", gi)], eng="dve")
        proj(P, ntl, IK, 72, ev_ik)
        flush()

        Lblk = 128 * (blk * NT + NT)
        SCs = [A_sc, rt_all[:, :]]
        MKs = [mk[:, :], sg[:, :, :].rearrange("p a b -> p (a b)")]
        SCKEYS = [[("arena", "sc", s_) for s_ in range(4)], [("rt", i_) for i_ in range(8)]]
        MKKEYS = [[("mk",)], [("sg", 0, 0), ("sg", 0, 1), ("sg", 1, 0), ("sg", 1, 1)]]

        def col(c_):
            return stat[:, c_:c_ + 1]

        for tt in range(ntl):
            gi = blk * NT + tt
            L = 128 * (gi + 1)
            cs = slice(tt * 128, (tt + 1) * 128)
            SC, SCK = SCs[tt], SCKEYS[tt]
            for sbk in range((L + 511) // 512):
                ncols = min(512, L - sbk * 512)
                kt = [("kiT", t_) for t_ in range(sbk * 4, sbk * 4 + ncols // 128)]
                for hh in range(8):
                    c, r = hh // 2, hh % 2
                    b = K.ps_alloc()
                    K.pe([lambda pe: pe.matmul(psb[b][:, :ncols], lhsT=A_iqT[r * 64:(r + 1) * 64, c, cs],
                                               rhs=kiT[r * 64:(r + 1) * 64, sbk * 512:sbk * 512 + ncols], start=True, stop=True)],
                         reads=[("arena", "iqT", tt)] + kt, writes=[PS(b)])
                    li = nxt("xs", 2)
                    K.op("act", lambda e: e.activation(out=xs[li][:, :ncols], in_=psb[b][:, :ncols], func=AF.Relu),
                         reads=[PS(b)], writes=[("xs", li)])
                    K.ps_free(b)
                    scv = SC[:, sbk * 512:sbk * 512 + ncols]
                    wk = [SCK[sbk]] if tt == 0 else SCK
                    if hh == 0:
                        K.op("dve", lambda e: e.tensor_scalar(out=scv, in0=xs[li][:, :ncols], scalar1=iw[:, tt, 0:1], scalar2=None,
                                                              op0=ALU.mult),
                             reads=[("xs", li), ("iw", tt)], writes=wk)
                    else:
                        K.op("dve", lambda e: e.scalar_tensor_tensor(out=scv, in0=xs[li][:, :ncols], scalar=iw[:, tt, hh:hh + 1],
                                                                     in1=scv, op0=ALU.mult, op1=ALU.add),
                             reads=[("xs", li), ("iw", tt)], writes=wk)

        def bis_gen(tt):
            gi = blk * NT + tt
            L = 128 * (gi + 1)
            SC, SCK, MKT, MKK = SCs[tt], SCKEYS[tt], MKs[tt], MKKEYS[tt]
            LO, W0, MID, CNT, TMP = 16 + tt * 8, 17 + tt * 8, 18 + tt * 8, 19 + tt * 8, 20 + tt * 8
            if L < Lblk:
                K.op("pool", lambda e: e.memset(SC[:, L:Lblk], NEGBIG), reads=[], writes=SCK)
                yield
            if gi >= 2:
                K.op("dve", lambda e: e.tensor_reduce(out=col(LO), in_=SC[:, :L], axis=AX.X, op=ALU.min),
                     reads=SCK, writes=[("stat", LO)])
                yield
                K.op("dve", lambda e: e.tensor_reduce(out=col(W0), in_=SC[:, :L], axis=AX.X, op=ALU.max),
                     reads=SCK, writes=[("stat", W0)])
                yield
                K.op("dve", lambda e: e.tensor_tensor(out=col(W0), in0=col(W0), in1=col(LO), op=ALU.subtract),
                     reads=[("stat", W0), ("stat", LO)], writes=[("stat", W0)])
                yield
            K.op("dve", lambda e: e.tensor_tensor(out=SC[:, L - 128:L], in0=SC[:, L - 128:L], in1=NEG[:, :], op=ALU.add),
                 reads=[("NEG",)] + SCK, writes=SCK)
            yield
            if gi >= 2:
                for it in range(NBIS):
                    f = 2.0 ** -(it + 1)
                    K.op("dve", lambda e: e.scalar_tensor_tensor(out=col(MID), in0=col(W0), scalar=f, in1=col(LO),
                                                                 op0=ALU.mult, op1=ALU.add),
                         reads=[("stat", W0), ("stat", LO)], writes=[("stat", MID)])
                    yield
                    K.op("dve", lambda e: e.tensor_scalar(out=MKT[:, :L], in0=SC[:, :L], scalar1=col(MID), scalar2=0.0,
                                                          op0=ALU.is_ge, op1=ALU.add, accum_out=col(CNT)),
                         reads=SCK + [("stat", MID)], writes=MKK + [("stat", CNT)])
                    yield
                    K.op("dve", lambda e: e.scalar_tensor_tensor(out=col(TMP), in0=col(CNT), scalar=KTOP - 0.5, in1=col(W0),
                                                                 op0=ALU.is_ge, op1=ALU.mult),
                         reads=[("stat", CNT), ("stat", W0)], writes=[("stat", TMP)])
                    yield
                    K.op("dve", lambda e: e.scalar_tensor_tensor(out=col(LO), in0=col(TMP), scalar=f, in1=col(LO),
                                                                 op0=ALU.mult, op1=ALU.add),
                         reads=[("stat", TMP), ("stat", LO)], writes=[("stat", LO)])
                    yield
            else:
                K.op("dve", lambda e: e.memset(col(LO), -1.0e29), reads=[], writes=[("stat", LO)])
                yield
            K.op("dve", lambda e: e.tensor_scalar(out=MKT[:, :Lblk], in0=SC[:, :Lblk], scalar1=col(LO), scalar2=None,
                                                  op0=ALU.is_ge),
                 reads=SCK + [("stat", LO)], writes=MKK)
            yield

        gens = [bis_gen(tt) for tt in range(ntl)]
        alive = list(gens)
        while alive:
            for gnr in list(alive):
                try:
                    next(gnr)
                except StopIteration:
                    alive.remove(gnr)
        for tt in range(ntl):
            cs = slice(tt * 128, (tt + 1) * 128)
            nsb = Lblk // 128
            for s0 in range(0, nsb, 8):
                n = min(8, nsb - s0)
                transposes([MKs[tt][:, (s0 + k) * 128:(s0 + k + 1) * 128] for k in range(n)], MKKEYS[tt], 128,
                           A_maskT[:, s0:s0 + n, cs], [("arena", "maskT", tt)], eng="act")

        nkt = blk * NT + NT
        MK = [("arena", "maskT", 0), ("arena", "maskT", 1)]
        for hh in range(8):
            g = hh // 4
            bo = K.ps_alloc()
            bs = K.ps_alloc()
            tiles = [(-1, NM, 0)] + [(si, 128, 128 if si == nkt - 1 else 0) for si in range(nkt)]

            def qk(idx):
                si, Ps, q0 = tiles[idx]
                kc0 = 0 if si < 0 else NM + si * 128
                gt = si + 1
                b = K.ps_alloc()
                K.pe([lambda pe: pe.matmul(psb[b][:Ps, q0:TB], lhsT=akT[:, hh, kc0:kc0 + Ps], rhs=A_aqT[:, hh, q0:TB],
                                           start=True, stop=True)],
                     reads=[("akT", gt, g), ("arena", "aqT", 0, g), ("arena", "aqT", 1, g)], writes=[PS(b)])
                pi = nxt("pT", 3)
                K.op("act", lambda e: e.activation(out=pT[pi][:Ps, q0:TB], in_=psb[b][:Ps, q0:TB], func=AF.Exp, scale=ATT_SCALE),
                     reads=[PS(b)], writes=[("pT", pi)])
                K.ps_free(b)
                if si >= 0:
                    K.op("pool", lambda e: e.tensor_tensor(out=pT[pi][:Ps, q0:TB], in0=pT[pi][:Ps, q0:TB],
                                                           in1=A_maskT[:Ps, si, q0:TB], op=ALU.mult),
                         reads=[("pT", pi)] + MK, writes=[("pT", pi)])
                return pi

            def pv(idx, pi):
                si, Ps, q0 = tiles[idx]
                gt = si + 1
                first, last = idx == 0, idx == len(tiles) - 1
                K.pe([lambda pe: pe.matmul(psb[bo][:, q0:TB], lhsT=av[:Ps, gt, hh * 128:(hh + 1) * 128], rhs=pT[pi][:Ps, q0:TB],
                                           start=first, stop=last),
                      lambda pe: pe.matmul(psb[bs][:, q0:TB], lhsT=ones[:Ps, :], rhs=pT[pi][:Ps, q0:TB],
                                           start=first, stop=last)],
                     reads=[("av", gt, g), ("pT", pi), ("ones",)], writes=[PS(bo), PS(bs)])

            LOOK = 2
            pis = {}
            for idx in range(min(LOOK, len(tiles))):
                pis[idx] = qk(idx)
            for idx in range(len(tiles)):
                if idx + LOOK < len(tiles):
                    pis[idx + LOOK] = qk(idx + LOOK)
                pv(idx, pis.pop(idx))
            ri_ = nxt("rs", 2)
            K.op("dve", lambda e: e.reciprocal(out=rs[ri_][:, :], in_=psb[bs][:, :TB]), reads=[PS(bs)], writes=[("rs", ri_)])
            K.op("dve", lambda e: e.tensor_tensor(out=A_attnT[:, hh, :], in0=psb[bo][:, :TB], in1=rs[ri_][:, :], op=ALU.mult),
                 reads=[PS(bo), ("rs", ri_)], writes=[("arena", "attnT", hh)])
            K.ps_free(bo)
            K.ps_free(bs)
        for g in range(2):
            def ev_ga(tt, b, g=g):
                K.op("act", lambda e: e.activation(out=sg[:P, tt, g * 512:(g + 1) * 512], in_=psb[b][:P, :], func=AF.Sigmoid),
                     reads=[PS(b)], writes=[("sg", tt, g)])
            proj(P, ntl, GA + g * 512, 512, ev_ga)

        def ev_ao(tt, dh, b):
            sl = slice(dh * 512, (dh + 1) * 512)
            xi_ = nxt("xs", 2)
            K.op("dve", lambda e: e.tensor_tensor(out=xs[xi_][:P, :], in0=psb[b][:P, :], in1=sg[:P, tt, sl], op=ALU.mult),
                 reads=[PS(b), ("sg", tt, dh)], writes=[("xs", xi_)])
            K.op("pool", lambda e: e.tensor_tensor(out=merged[:P, tt, sl], in0=merged[:P, tt, sl], in1=xs[xi_][:P, :], op=ALU.add),
                 reads=[("xs", xi_)], writes=[("mg", tt, dh)])
        down_proj(lambda kc, tt: A_attnT[:, kc, tt * P:(tt + 1) * P], lambda kc: [("arena", "attnT", kc)], 8, "wao", P, ntl, ev_ao)
        for tt in range(ntl):
            transposes([merged[:P, tt, k * 128:(k + 1) * 128] for k in range(8)], [("mg", tt, 0), ("mg", tt, 1)], P,
                       uT[:, :, tt * P:(tt + 1) * P], [("uT", tt)], eng="act")

        def ev_mo(tt, dh, b):
            sl = slice(dh * 512, (dh + 1) * 512)
            K.op("dve", lambda e: e.tensor_tensor(out=h[:P, tt, sl], in0=psb[b][:P, :], in1=h[:P, tt, sl], op=ALU.add),
                 reads=[PS(b)], writes=[("h", tt)])
        down_proj(lambda kc, tt: uT[:, kc, tt * P:(tt + 1) * P], lambda kc: [("uT", tt) for tt in range(ntl)], 8, "wmo", P, ntl, ev_mo)

    K.dma("sp", h[:NM, 0, :], dram["meta"], writes=[("h", 0)], slot=("x", 0))
    ffn(NM, 1, 0, "w1gu", "w1d")
    mixer(NM, 1, -1)
    for blk in range(nblk):
        for tt in range(NT):
            r0 = blk * TB + tt * 128
            K.dma("sp", h[:, tt, :], dram["x"][r0:r0 + 128, :], writes=[("h", tt)], slot=("x", tt))
        ffn(128, NT, 0, "w1gu", "w1d")
        if stages >= 2:
            mixer(128, NT, blk)
        if stages >= 3:
            ffn(128, NT, 2, "w2gu", "w2d")
        if stages >= 4:
            K.dma("sp", gbc[:, :], dram["gains"][3 * 128:4 * 128, :], writes=[("gbc",)], slot=("gbc",))
        for tt in range(NT):
            r0 = blk * TB + tt * 128
            if stages >= 4:
                c = tt
                hs = h[:, tt, :]
                K.op("act", lambda e: e.activation(out=junk[:, :], in_=hs, func=AF.Square, accum_out=stat[:, c:c + 1]),
                     reads=[("h", tt)], writes=[("stat", c)])
                K.op("act", lambda e: e.activation(out=stat[:, c:c + 1], in_=stat[:, c:c + 1], func=AF.Sqrt, bias=epsb[:, :], scale=1.0 / D),
                     reads=[("stat", c), ("epsb",)], writes=[("stat", c)])
                K.op("dve", lambda e: e.reciprocal(out=stat[:, c:c + 1], in_=stat[:, c:c + 1]),
                     reads=[("stat", c)], writes=[("stat", c)])
                K.op("dve", lambda e: e.scalar_tensor_tensor(out=hs, in0=hs, scalar=stat[:, c:c + 1], in1=gbc[:, :],
                                                             op0=ALU.mult, op1=ALU.mult),
                     reads=[("stat", c), ("gbc",)], writes=[("h", tt)])
            K.dma("pool", dram["y"][r0:r0 + 128, :], h[:, tt, :], reads=[("h", tt)], writes=[], slot=("y", tt))
    sp = K.E["sp"]
    for tt in range(NT):
        s = K.dsem[("y", tt)]
        sp.wait(s[0], s[1])


_CACHE = {}


def _prep_inputs(inputs):
    f = lambda a: np.ascontiguousarray(np.asarray(a, dtype=np.float32))
    shared = {
        "meta": f(inputs["meta_tokens"]),
        "gains": np.ascontiguousarray(np.concatenate([
            np.broadcast_to(f(inputs["ffn1_norm"]).reshape(1, D), (128, D)),
            np.broadcast_to(f(inputs["mix_norm"]).reshape(1, D), (128, D)),
            np.broadcast_to(f(inputs["ffn2_norm"]).reshape(1, D), (128, D)),
            np.broadcast_to(f(inputs["final_norm"]).reshape(1, D), (128, D))], axis=0)),
        "w1g": f(inputs["ffn1_w_gate"][0]), "w1u": f(inputs["ffn1_w_up"][0]), "w1d": f(inputs["ffn1_w_down"][0]),
        "win": f(inputs["w_in"][0]), "wro": f(inputs["w_ret_out"][0]), "wao": f(inputs["w_att_out"][0]),
        "wmo": f(inputs["w_mix_out"][0]),
        "w2g": f(inputs["ffn2_w_gate"][0]), "w2u": f(inputs["ffn2_w_up"][0]), "w2d": f(inputs["ffn2_w_down"][0]),
    }
    shared.update(_consts())
    return shared


def kernel(**inputs):
    x = np.asarray(inputs["x"], dtype=np.float32)
    B = x.shape[0]
    shared = _prep_inputs(inputs)
    if "nc" not in _CACHE:
        _CACHE["nc"] = build_program()
    nc = _CACHE["nc"]
    in_maps = []
    for b in range(B):
        m = dict(shared)
        m["x"] = np.ascontiguousarray(x[b])
        in_maps.append(m)
    res = run_bass_kernel_spmd(nc, in_maps, core_ids=list(range(B)))
    return np.stack([np.asarray(r["y"], dtype=np.float32) for r in res.results], axis=0)
```
